# Optimizing a Trainium2 kernel written in Bass

```python
import jax
import jax.numpy as jnp
from jax import lax
import numpy as np

D_MODEL = 2048
BATCH = 2
SEQ = 8192
DEPTH = 2

LRU_WIDTH = D_MODEL // 2
LRU_HEADS = 4
LRU_BLOCK = LRU_WIDTH // LRU_HEADS
CONV_WIDTH = 4
LRU_C = 8.0
HGRN_WIDTH = D_MODEL // 2
HGRN_EXPAND = 128
HGRN_HEADS = HGRN_WIDTH // HGRN_EXPAND
HGRN_HEAD_V = HGRN_WIDTH // HGRN_HEADS
HGRN_CHUNK = 64
ATTN_HEADS = 16
HEAD_DIM = D_MODEL // ATTN_HEADS
ROPE_DIM = HEAD_DIM // 4
ROPE_THETA = 500000.0
DILATED_GROUPS = ((128, 1), (512, 4), (2048, 16))
DSWA_BLOCK = 128
D_FF = 4 * D_MODEL
NORM_EPS = 1e-6
N_EVEN = (DEPTH + 1) // 2
N_ODD = DEPTH // 2
IN_SPLITS = (LRU_WIDTH, 2 * LRU_WIDTH, 2 * LRU_WIDTH + HGRN_WIDTH,
             2 * LRU_WIDTH + 2 * HGRN_WIDTH, 2 * LRU_WIDTH + 3 * HGRN_WIDTH)
IN_COLS = 2 * LRU_WIDTH + 4 * HGRN_WIDTH

kernel_name = 'hybrid_rglru_hgrn2_dilated_swa'

F32 = jnp.float32


def rms_norm(x, gain):
    xf = x.astype(F32)
    y = xf * lax.rsqrt(jnp.mean(xf * xf, axis=-1, keepdims=True) + NORM_EPS)
    return (y * gain.astype(F32)).astype(x.dtype)


def causal_depthwise_conv(x, w, b):
    out = lax.conv_general_dilated(
        x, w[:, None, :].astype(x.dtype), window_strides=(1,),
        padding=[(CONV_WIDTH - 1, 0)], dimension_numbers=('NWC', 'WIO', 'NWC'),
        feature_group_count=x.shape[-1])
    return out + b.astype(x.dtype)


def rg_lru(xb, w_a, b_a, w_i, b_i, lam):
    bsz, seq, _ = xb.shape
    xf = xb.astype(F32)
    xh = xf.reshape(bsz, seq, LRU_HEADS, LRU_BLOCK)
    r = jax.nn.sigmoid(jnp.einsum('bshi,hij->bshj', xh, w_a.astype(F32)).reshape(bsz, seq, LRU_WIDTH)
                       + b_a.astype(F32))
    i = jax.nn.sigmoid(jnp.einsum('bshi,hij->bshj', xh, w_i.astype(F32)).reshape(bsz, seq, LRU_WIDTH)
                       + b_i.astype(F32))
    log_a = -LRU_C * r * jax.nn.softplus(-lam.astype(F32))
    a = jnp.exp(log_a)
    u = jnp.sqrt(-jnp.expm1(2.0 * log_a)) * (i * xf)

    def combine(left, right):
        a_l, h_l = left
        a_r, h_r = right
        return a_l * a_r, a_r * h_l + h_r

    _, h = lax.associative_scan(combine, (a, u), axis=1)
    return h


def hgrn2(q_raw, f_raw, v_raw, g_raw, lower_bound, g_norm):
    bsz, seq, _ = q_raw.shape
    n_chunks = seq // HGRN_CHUNK
    q = jax.nn.silu(q_raw.astype(F32))
    fz = f_raw.astype(F32)
    log_f = jnp.log(lower_bound + (1.0 - lower_bound) * jax.nn.sigmoid(fz))
    k = (1.0 - lower_bound) * jax.nn.sigmoid(-fz)

    def to_chunks(t, d):
        return t.reshape(bsz, n_chunks, HGRN_CHUNK, HGRN_HEADS, d).transpose(1, 0, 3, 2, 4)

    xs = (to_chunks(q, HGRN_EXPAND), to_chunks(k, HGRN_EXPAND),
          to_chunks(log_f, HGRN_EXPAND), to_chunks(v_raw.astype(F32), HGRN_HEAD_V))
    causal = jnp.tril(jnp.ones((HGRN_CHUNK, HGRN_CHUNK), dtype=bool))[None, None, :, :, None]

    def chunk_step(state, inp):
        qc, kc, gc, vc = inp
        b = jnp.cumsum(gc, axis=2)
        b_last = b[:, :, -1:, :]
        o_inter = jnp.einsum('bhtk,bhkv->bhtv', qc * jnp.exp(b), state)
        decay = jnp.exp(jnp.where(causal, b[:, :, :, None, :] - b[:, :, None, :, :], -jnp.inf))
        scores = jnp.einsum('bhtk,bhsk,bhtsk->bhts', qc, kc, decay)
        o_intra = jnp.einsum('bhts,bhsv->bhtv', scores, vc)
        new_state = (jnp.exp(b_last[:, :, 0, :])[..., None] * state
                     + jnp.einsum('bhsk,bhsv->bhkv', kc * jnp.exp(b_last - b), vc))
        return new_state, o_inter + o_intra

    state0 = jnp.zeros((bsz, HGRN_HEADS, HGRN_EXPAND, HGRN_HEAD_V), F32)
    _, o = lax.scan(chunk_step, state0, xs)
    o = o.transpose(1, 0, 3, 2, 4).reshape(bsz, seq, HGRN_HEADS, HGRN_HEAD_V)
    o = o * lax.rsqrt(jnp.mean(o * o, axis=-1, keepdims=True) + NORM_EPS)
    o = o.reshape(bsz, seq, HGRN_WIDTH) * g_norm.astype(F32)
    return o * jax.nn.silu(g_raw.astype(F32))


def recurrent_mixers(h, w_in, conv_w, conv_b, w_a, b_a, w_i, b_i, lam, lower_bound, g_norm, w_out):
    proj = h @ w_in
    x_lru, y_lru, q_h, f_h, v_h, g_h = jnp.split(proj, list(IN_SPLITS), axis=-1)
    lru = rg_lru(causal_depthwise_conv(x_lru, conv_w, conv_b), w_a, b_a, w_i, b_i, lam)
    lru = lru * jax.nn.gelu(y_lru.astype(F32), approximate=True)
    hg = hgrn2(q_h, f_h, v_h, g_h, lower_bound, g_norm)
    mixed = jnp.concatenate([lru, hg], axis=-1).astype(h.dtype)
    return mixed @ w_out


def partial_rope(t, positions):
    half = ROPE_DIM // 2
    inv_freq = 1.0 / (ROPE_THETA ** (jnp.arange(half, dtype=F32) * (2.0 / ROPE_DIM)))
    ang = positions.astype(F32)[:, :, None, None] * inv_freq
    cos, sin = jnp.cos(ang), jnp.sin(ang)
    t1 = t[..., :half]
    t2 = t[..., half:ROPE_DIM]
    return jnp.concatenate([t1 * cos - t2 * sin, t2 * cos + t1 * sin, t[..., ROPE_DIM:]], axis=-1)


def dilated_branch(q, k, v, window, dilation):
    bsz, seq, nh, hd = q.shape
    n_dist = window // dilation
    qb = DSWA_BLOCK
    length = seq // dilation
    nb = -(-length // qb)
    lp = nb * qb
    groups = bsz * dilation

    def to_sub(t):
        t = t.reshape(bsz, length, dilation, nh, hd).transpose(0, 2, 3, 1, 4).reshape(groups, nh, length, hd)
        t = jnp.pad(t, ((0, 0), (0, 0), (0, lp - length), (0, 0)))
        return t.reshape(groups, nh, nb, qb, hd)

    def band(t):
        prev = jnp.pad(t, ((0, 0), (0, 0), (1, 0), (0, 0), (0, 0)))[:, :, :-1]
        return jnp.concatenate([prev, t], axis=3)

    def from_sub(t):
        t = t.reshape(groups, nh, lp, -1)[:, :, :length]
        return t.reshape(bsz, dilation, nh, length, -1).transpose(0, 3, 1, 2, 4).reshape(bsz, seq, nh, -1)

    qs = to_sub(q)
    kb = band(to_sub(k))
    vb = band(to_sub(v))
    s = jnp.einsum('ghnqd,ghnkd->ghnqk', qs, kb)
    qi = jnp.arange(qb)[:, None]
    kj = jnp.arange(2 * qb)[None, :]
    dist = qi - kj + qb
    key_idx = jnp.arange(nb)[:, None, None] * qb - qb + kj[None]
    mask = (dist >= 0) & (dist <= n_dist) & (key_idx >= 0)
    s = jnp.where(mask, s, -jnp.inf)
    m = jnp.max(s, axis=-1)
    p = jnp.exp(s - m[..., None])
    l = jnp.sum(p, axis=-1)
    o = jnp.einsum('ghnqk,ghnkd->ghnqd', p, vb)
    return from_sub(m), from_sub(l), from_sub(o)


def dilated_attention(h, positions, w_qkv, w_o):
    bsz, seq, _ = h.shape
    qkv = (h @ w_qkv).astype(F32).reshape(bsz, seq, 3, ATTN_HEADS, HEAD_DIM)
    q = partial_rope(qkv[:, :, 0], positions) * (HEAD_DIM ** -0.5)
    k = partial_rope(qkv[:, :, 1], positions)
    v = qkv[:, :, 2]
    branches = [dilated_branch(q, k, v, w, d) for (w, d) in DILATED_GROUPS]
    m_max = branches[0][0]
    for br in branches[1:]:
        m_max = jnp.maximum(m_max, br[0])
    w0 = jnp.exp(branches[0][0] - m_max)
    num = w0 * branches[0][2]
    den = w0 * branches[0][1]
    for m_b, l_b, o_b in branches[1:]:
        w_b = jnp.exp(m_b - m_max)
        num = num + w_b * o_b
        den = den + w_b * l_b
    out = (num / den).reshape(bsz, seq, D_MODEL).astype(h.dtype)
    return out @ w_o


def squared_relu_mlp(h, w1, w2):
    return jnp.square(jax.nn.relu(h @ w1)) @ w2


def setup_inputs(seed: int = 0) -> dict:
    key = jax.random.key(seed)
    ks = jax.random.split(key, 20)

    def nrm(k, shape, scale):
        return jax.random.normal(k, shape, F32) * scale

    x = nrm(ks[0], (BATCH, SEQ, D_MODEL), 1.0)
    positions = jnp.broadcast_to(jnp.arange(SEQ, dtype=jnp.int32), (BATCH, SEQ))
    norm_mix = 1.0 + nrm(ks[1], (DEPTH, D_MODEL), 0.05)
    norm_mlp = 1.0 + nrm(ks[2], (DEPTH, D_MODEL), 0.05)
    final_norm = 1.0 + nrm(ks[3], (D_MODEL,), 0.05)
    rec_w_in = nrm(ks[4], (N_EVEN, D_MODEL, IN_COLS), D_MODEL ** -0.5)
    rec_conv_w = nrm(ks[5], (N_EVEN, CONV_WIDTH, LRU_WIDTH), CONV_WIDTH ** -0.5)
    rec_conv_b = nrm(ks[6], (N_EVEN, LRU_WIDTH), 0.01)
    lru_w_a = nrm(ks[7], (N_EVEN, LRU_HEADS, LRU_BLOCK, LRU_BLOCK), LRU_BLOCK ** -0.5)
    lru_b_a = nrm(ks[8], (N_EVEN, LRU_WIDTH), 0.01)
    lru_w_i = nrm(ks[9], (N_EVEN, LRU_HEADS, LRU_BLOCK, LRU_BLOCK), LRU_BLOCK ** -0.5)
    lru_b_i = nrm(ks[10], (N_EVEN, LRU_WIDTH), 0.01)
    a_pow_c = jax.random.uniform(ks[11], (N_EVEN, LRU_WIDTH), F32, minval=0.9, maxval=0.999)
    a_base = a_pow_c ** (1.0 / LRU_C)
    lru_lambda = jnp.log(a_base) - jnp.log1p(-a_base)
    hgrn_lb_logits = nrm(ks[12], (DEPTH + 1, HGRN_WIDTH), 0.5)
    hgrn_g_norm = 1.0 + nrm(ks[13], (N_EVEN, HGRN_WIDTH), 0.05)
    rec_w_out = nrm(ks[14], (N_EVEN, D_MODEL, D_MODEL), D_MODEL ** -0.5)
    attn_w_qkv = nrm(ks[15], (N_ODD, D_MODEL, 3 * D_MODEL), D_MODEL ** -0.5)
    attn_w_o = nrm(ks[16], (N_ODD, D_MODEL, D_MODEL), D_MODEL ** -0.5)
    mlp_w1 = nrm(ks[17], (DEPTH, D_MODEL, D_FF), D_MODEL ** -0.5)
    mlp_w2 = nrm(ks[18], (DEPTH, D_FF, D_MODEL), D_FF ** -0.5)
    return {'x': x, 'positions': positions, 'norm_mix': norm_mix, 'norm_mlp': norm_mlp,
            'final_norm': final_norm, 'rec_w_in': rec_w_in, 'rec_conv_w': rec_conv_w,
            'rec_conv_b': rec_conv_b, 'lru_w_a': lru_w_a, 'lru_b_a': lru_b_a,
            'lru_w_i': lru_w_i, 'lru_b_i': lru_b_i, 'lru_lambda': lru_lambda,
            'hgrn_lb_logits': hgrn_lb_logits, 'hgrn_g_norm': hgrn_g_norm,
            'rec_w_out': rec_w_out, 'attn_w_qkv': attn_w_qkv, 'attn_w_o': attn_w_o,
            'mlp_w1': mlp_w1, 'mlp_w2': mlp_w2}


def reference(x, positions, norm_mix, norm_mlp, final_norm, rec_w_in, rec_conv_w, rec_conv_b,
              lru_w_a, lru_b_a, lru_w_i, lru_b_i, lru_lambda, hgrn_lb_logits, hgrn_g_norm,
              rec_w_out, attn_w_qkv, attn_w_o, mlp_w1, mlp_w2):
    lower_bounds = jnp.cumsum(jax.nn.softmax(hgrn_lb_logits.astype(F32), axis=0), axis=0)
    h = x
    for layer in range(DEPTH):
        hn = rms_norm(h, norm_mix[layer])
        j = layer // 2
        if layer % 2 == 0:
            mix = recurrent_mixers(hn, rec_w_in[j], rec_conv_w[j], rec_conv_b[j], lru_w_a[j],
                                   lru_b_a[j], lru_w_i[j], lru_b_i[j], lru_lambda[j],
                                   lower_bounds[layer], hgrn_g_norm[j], rec_w_out[j])
        else:
            mix = dilated_attention(hn, positions, attn_w_qkv[j], attn_w_o[j])
        h = h + mix
        h = h + squared_relu_mlp(rms_norm(h, norm_mlp[layer]), mlp_w1[layer], mlp_w2[layer])
    return rms_norm(h, final_norm)
```

```python
import numpy as np
import concourse.bass as bass
import concourse.mybir as mybir
from contextlib import ExitStack

F32 = mybir.dt.float32
BF16 = mybir.dt.bfloat16
AF = mybir.ActivationFunctionType
ALU = mybir.AluOpType
AX = mybir.AxisListType

COMPUTE = ("pe", "act", "dve", "pool")
QUEUES = ("sp", "actq", "poolq")
STREAM_OF = {"pe": "pe", "act": "act", "dve": "dve", "pool": "pool",
             "sp": "sp", "actq": "act", "poolq": "pool"}


class Buf:
    __slots__ = ("name", "last_w", "readers")

    def __init__(self, name):
        self.name = name
        self.last_w = None
        self.readers = []


class Op:
    __slots__ = ("eng", "stream", "fn", "deps", "sig", "tick", "semkey", "idx", "is_dma", "inc")

    def __init__(self, eng, fn, sig, semkey):
        self.eng = eng
        self.stream = STREAM_OF[eng]
        self.fn = fn
        self.deps = []
        self.sig = sig
        self.tick = None
        self.semkey = semkey
        self.is_dma = eng in QUEUES
        self.inc = 16


class Prog:
    def __init__(self, nc):
        self.nc = nc
        self.ops = []
        self.streams = {"pe": [], "act": [], "dve": [], "pool": [], "sp": []}
        self.stack = ExitStack()
        self.nbuf = 0

    def sbuf(self, name, shape, dtype):
        return self.stack.enter_context(self.nc.sbuf_tensor(name, list(shape), dtype))

    def psum(self, name, shape, dtype=F32):
        return self.stack.enter_context(self.nc.psum_tensor(name, list(shape), dtype))

    def buf(self, name=None):
        self.nbuf += 1
        return Buf(name or f"b{self.nbuf}")

    def bufs(self, n, name="b"):
        return [self.buf(f"{name}{i}") for i in range(n)]

    def op(self, eng, fn, reads=(), writes=(), sig=True, semkey=None, inc=16):
        o = Op(eng, fn, sig, semkey)
        o.inc = inc
        if o.is_dma:
            assert semkey is not None
        deps = set()
        for b in reads:
            if b.last_w is not None:
                deps.add(b.last_w)
        for b in writes:
            if b.last_w is not None:
                deps.add(b.last_w)
            lastr = {}
            for r in b.readers:
                if r.is_dma:
                    deps.add(r)
                else:
                    lastr[r.eng] = r
            for r in lastr.values():
                deps.add(r)
        deps.discard(o)
        for b in reads:
            b.readers.append(o)
        for b in writes:
            b.last_w = o
            b.readers = []
        o.deps = list(deps)
        self.ops.append(o)
        self.streams[o.stream].append(o)
        return o

    def barrier(self):
        last = {}
        for o in self.ops:
            if o.fn is None:
                continue
            if o.is_dma:
                last[("dma", o.semkey)] = o
            else:
                last[("eng", o.eng)] = o
        for st in self.streams:
            o = Op(st, None, False, None)
            o.is_dma = False
            o.deps = list(last.values())
            self.ops.append(o)
            self.streams[st].append(o)

    def emit(self, final_wait_ops=()):
        nc = self.nc
        for o in self.ops:
            for d in o.deps:
                if d.stream == o.stream == "pe" and not d.is_dma:
                    continue
                if not d.is_dma:
                    d.sig = True
        counters = {}
        semnames = {}
        for o in self.ops:
            if o.fn is None:
                continue
            if o.is_dma:
                key = ("dma", o.semkey)
                counters[key] = counters.get(key, 0) + o.inc
                o.tick = (key, counters[key])
            elif o.sig:
                key = ("eng", o.eng)
                counters[key] = counters.get(key, 0) + 1
                o.tick = (key, counters[key])
        for st, lst in self.streams.items():
            nxt = {}
            for o in reversed(lst):
                if o.fn is None or o.is_dma:
                    continue
                if o.sig:
                    nxt[o.eng] = o.tick
                else:
                    o.tick = nxt.get(o.eng)
        self._nosig = True
        sems = {}
        for key in counters:
            nm = "s_" + "_".join(str(k) for k in key)
            sems[key] = self.stack.enter_context(nc.semaphore(nm))
        self.sems = sems
        maxcnt = max(counters.values()) if counters else 0
        engobj = {"pe": "tensor", "act": "scalar", "dve": "vector", "pool": "gpsimd", "sp": "sync"}
        block = self.stack.enter_context(nc.Block())

        def make_stream(st):
            lst = self.streams[st]

            def body(eng):
                waited = {}
                for o in lst:
                    need = {}
                    for d in o.deps:
                        if d.tick is None:
                            raise RuntimeError("dependency on non-signalling op")
                        k, v = d.tick
                        if d.stream == o.stream and not d.is_dma:
                            if st == "pe":
                                continue
                        if v > need.get(k, 0):
                            need[k] = v
                    for k, v in need.items():
                        if waited.get(k, 0) >= v:
                            continue
                        eng.wait_ge(sems[k], v)
                        waited[k] = v
                    if o.fn is None:
                        continue
                    ins = o.fn(eng)
                    if o.tick is not None and (o.is_dma or o.sig):
                        k, v = o.tick
                        ins.then_inc(sems[k], o.inc if o.is_dma else 1)
                if st == "sp":
                    for o in final_wait_ops:
                        k, v = o.tick
                        eng.wait_ge(sems[k], v)
            return body

        for st in ("sp", "pe", "act", "dve", "pool"):
            if not self.streams[st] and st != "sp":
                continue
            getattr(block, engobj[st])(make_stream(st))
        return maxcnt

    def close(self):
        self.stack.close()


D = 2048
EPS = 1e-6


class Arena:
    def __init__(self, P, nfloats=52500):
        self.P = P
        self.t = P.sbuf("arena", [128, nfloats], F32)
        self.n = nfloats
        self.off = 0

    def f32(self, n):
        assert self.off + n <= self.n, f"arena overflow {self.off}+{n}"
        ap = self.t[:, self.off:self.off + n]
        self.off += n
        return ap

    def bf16(self, n):
        m = (n + 1) // 2
        assert self.off + m <= self.n, f"arena overflow {self.off}+{m}"
        ap = self.t[:, self.off:self.off + m].bitcast(BF16)
        self.off += m
        return ap[:, 0:n]

    def mark(self):
        return self.off

    def reset(self, m=0):
        self.P.barrier()
        self.off = m


class Ctx:
    def __init__(self, nc):
        self.nc = nc
        self.P = Prog(nc)
        P = self.P
        self.arena = Arena(P)
        self.ps = [P.psum(f"ps{i}", [128, 512], F32) for i in range(8)]
        self.psb = [P.buf(f"psb{i}") for i in range(8)]
        self.identf = P.sbuf("identf", [128, 128], F32)
        self.ident = P.sbuf("ident", [128, 128], BF16)
        self.ones_bf = P.sbuf("ones_bf", [128, 128], BF16)
        self.ones_f = P.sbuf("ones_f", [128, 128], F32)
        self.bconst = P.buf("const")
        b = self.bconst
        P.op("pool", lambda e: e.memset(self.identf[:], 1.0), writes=[b])
        P.op("pool", lambda e: e.affine_select(out=self.identf[:], in_=self.identf[:], pattern=[[-1, 128]],
                                               compare_op=ALU.is_equal, fill=0.0, base=0, channel_multiplier=1),
             reads=[b], writes=[b])
        P.op("pool", lambda e: e.memset(self.ones_f[:], 1.0), writes=[b])
        P.op("dve", lambda e: e.tensor_copy(out=self.ident[:], in_=self.identf[:]), reads=[b], writes=[b])
        P.op("dve", lambda e: e.tensor_copy(out=self.ones_bf[:], in_=self.ones_f[:]), reads=[b], writes=[b])
        self.dma_rr = 0
        self.out_ops = []

    def psbf(self, i):
        return self.ps[i][:].bitcast(BF16)

    def hwq(self):
        self.dma_rr += 1
        return "sp" if self.dma_rr % 2 else "actq"


class WStream:
    def __init__(self, cx, name, nslots, nelem):
        self.cx = cx
        self.name = name
        self.n = nslots
        self.slots = [cx.arena.bf16(nelem) for _ in range(nslots)]
        self.bufs = [cx.P.buf(f"{name}{i}") for i in range(nslots)]
        self.i = 0

    def load(self, src_ap, view):
        cx = self.cx
        s = self.i % self.n
        self.i += 1
        dst = view(self.slots[s])
        b = self.bufs[s]
        cx.P.op("poolq", lambda e: e.dma_start(out=dst, in_=src_ap), writes=[b], semkey=f"{self.name}{s}")
        return dst, b


def rmsnorm_T(cx, src_tiles, gain_bc, bgain, hnT, bhnT, ntiles, hn_tmp, bhn_tmp, stat, bstat, col0=0):
    P = cx.P
    psi = 0
    for i in range(ntiles):
        x_ap, bx = src_tiles[i]
        tmp = hn_tmp[i % 2]
        btmp = bhn_tmp[i % 2]
        st = stat[:, 4 * i:4 * i + 4]
        P.op("act", lambda e, x_ap=x_ap, tmp=tmp, st=st: e.activation(out=tmp, in_=x_ap, func=AF.Square, accum_out=st[:, 0:1]),
             reads=[bx], writes=[btmp, bstat])
        P.op("dve", lambda e, st=st: e.tensor_scalar(out=st[:, 1:2], in0=st[:, 0:1], scalar1=1.0 / D, scalar2=EPS,
                                                    op0=ALU.mult, op1=ALU.add), reads=[bstat], writes=[bstat])
        P.op("act", lambda e, st=st: e.activation(out=st[:, 2:3], in_=st[:, 1:2], func=AF.Sqrt), reads=[bstat], writes=[bstat])
        P.op("dve", lambda e, st=st: e.reciprocal(out=st[:, 3:4], in_=st[:, 2:3]), reads=[bstat], writes=[bstat])
        P.op("dve", lambda e, x_ap=x_ap, tmp=tmp, st=st: e.scalar_tensor_tensor(out=tmp, in0=x_ap, scalar=st[:, 3:4], in1=gain_bc,
                                                                              op0=ALU.mult, op1=ALU.mult),
             reads=[bx, bstat, bgain], writes=[btmp])
        for half in range(2):
            bank = 6 + (psi % 2)
            psi += 1
            pv = cx.psbf(bank).rearrange("p (c t) -> p c t", t=128)
            for j in range(8):
                kc = half * 8 + j
                P.op("pe", lambda e, pv=pv, j=j, tmp=tmp, kc=kc: e.transpose(pv[:, j, :], tmp[:, kc * 128:(kc + 1) * 128], cx.ident[:]),
                     reads=[btmp, cx.bconst], writes=[cx.psb[bank]], sig=(j == 7))
            dst = hnT[:, half * 8:(half + 1) * 8, col0 + i * 128: col0 + (i + 1) * 128]
            if (i + half) % 2 == 0:
                P.op("act", lambda e, dst=dst, pv=pv: e.activation(out=dst, in_=pv, func=AF.Copy), reads=[cx.psb[bank]], writes=[bhnT])
            else:
                P.op("dve", lambda e, dst=dst, pv=pv: e.tensor_copy(out=dst, in_=pv), reads=[cx.psb[bank]], writes=[bhnT])


def mlp_block(cx, h_in, h_out, gain_dram, w1, w2, T, final_gain=None):
    P = cx.P
    A = cx.arena
    m0 = A.mark()
    FF = 4 * D
    TT = 512
    gain_bc = A.f32(D); bgain = P.buf("gain")
    P.op("sp", lambda e: e.dma_start(out=gain_bc, in_=gain_dram.partition_broadcast(128)), writes=[bgain], semkey="gain")
    if final_gain is not None:
        fg_bc = A.f32(D); bfg = P.buf("fgain")
        P.op("sp", lambda e: e.dma_start(out=fg_bc, in_=final_gain.partition_broadcast(128)), writes=[bfg], semkey="fgain")
    hres = [A.f32(D) for _ in range(4)]
    bres = [P.buf(f"hres{i}") for i in range(4)]
    hn_tmp = [A.bf16(D) for _ in range(2)]
    bhn_tmp = [P.buf("hntmp0"), P.buf("hntmp1")]
    stat = A.f32(16); bstat = P.buf("stat")
    hnT = A.bf16(16 * TT).rearrange("p (c t) -> p c t", t=TT); bhnT = P.buf("hnT")
    aT = A.bf16(64 * TT).rearrange("p (c t) -> p c t", t=TT)
    baT = [P.buf(f"aT{i}") for i in range(64)]
    sq = [A.f32(TT) for _ in range(2)]; bsq = [P.buf("sq0"), P.buf("sq1")]
    W1C = 256
    w1s = WStream(cx, "w1s", 2, 16 * W1C)
    W2K = 8
    w2s = WStream(cx, "w2s", 2, W2K * 512)
    w1v = w1.rearrange("(kc p) n -> p kc n", p=128)
    w2v = w2.rearrange("(fc p) n -> p fc n", p=128)
    nblk = T // TT
    for blk in range(nblk):
        t0 = blk * TT
        for i in range(4):
            P.op(cx.hwq(), lambda e, i=i, t0=t0: e.dma_start(out=hres[i], in_=h_in[t0 + i * 128: t0 + (i + 1) * 128, :]),
                 writes=[bres[i]], semkey=f"hres{i}")
        rmsnorm_T(cx, [(hres[i], bres[i]) for i in range(4)], gain_bc, bgain, hnT, bhnT, 4, hn_tmp, bhn_tmp, stat, bstat)
        ei = 0
        for g in range(FF // W1C):
            wt, bw = w1s.load(w1v[:, :, g * W1C:(g + 1) * W1C], lambda s: s.rearrange("p (c n) -> p c n", n=W1C))
            for mm in range(W1C // 128):
                m = g * (W1C // 128) + mm
                bank = ei % 4
                for kc in range(16):
                    P.op("pe", lambda e, bank=bank, wt=wt, kc=kc, mm=mm: e.matmul(cx.ps[bank][:], lhsT=wt[:, kc, mm * 128:(mm + 1) * 128],
                                                                                   rhs=hnT[:, kc, :], start=(kc == 0), stop=(kc == 15)),
                         reads=[bw, bhnT], writes=[cx.psb[bank]], sig=(kc == 15))
                s = sq[ei % 2]; bs = bsq[ei % 2]
                P.op("act", lambda e, s=s, bank=bank: e.activation(out=s, in_=cx.ps[bank][:], func=AF.Square), reads=[cx.psb[bank]], writes=[bs])
                P.op("dve", lambda e, s=s, bank=bank, m=m: e.scalar_tensor_tensor(out=aT[:, m, :], in0=cx.ps[bank][:], scalar=0.0, in1=s,
                                                                                  op0=ALU.is_gt, op1=ALU.mult),
                     reads=[cx.psb[bank], bs], writes=[baT[m]])
                ei += 1
        for dq in range(4):
            for fg in range(64 // W2K):
                wt, bw = w2s.load(w2v[:, fg * W2K:(fg + 1) * W2K, dq * 512:(dq + 1) * 512], lambda s: s.rearrange("p (c n) -> p c n", n=512))
                for tt in range(4):
                    for fl in range(W2K):
                        fc = fg * W2K + fl
                        P.op("pe", lambda e, tt=tt, fc=fc, fl=fl, wt=wt: e.matmul(cx.ps[tt][:], lhsT=aT[:, fc, tt * 128:(tt + 1) * 128],
                                                                                 rhs=wt[:, fl, :], start=(fc == 0), stop=(fc == 63)),
                             reads=[bw, baT[fc]], writes=[cx.psb[tt]], sig=(fc == 63))
            for tt in range(4):
                dst = hres[tt][:, dq * 512:(dq + 1) * 512]
                P.op("dve", lambda e, dst=dst, tt=tt: e.tensor_tensor(out=dst, in0=cx.ps[tt][:], in1=dst, op=ALU.add),
                     reads=[cx.psb[tt], bres[tt]], writes=[bres[tt]])
        for i in range(4):
            src = hres[i]
            if final_gain is not None:
                st = stat[:, 0:4]
                tmp = sq[0].bitcast(BF16)
                junk = hn_tmp[0]
                P.op("act", lambda e, src=src, junk=junk, st=st: e.activation(out=junk, in_=src, func=AF.Square, accum_out=st[:, 0:1]),
                     reads=[bres[i]], writes=[bhn_tmp[0], bstat])
                P.op("dve", lambda e, st=st: e.tensor_scalar(out=st[:, 1:2], in0=st[:, 0:1], scalar1=1.0 / D, scalar2=EPS,
                                                            op0=ALU.mult, op1=ALU.add), reads=[bstat], writes=[bstat])
                P.op("act", lambda e, st=st: e.activation(out=st[:, 2:3], in_=st[:, 1:2], func=AF.Sqrt), reads=[bstat], writes=[bstat])
                P.op("dve", lambda e, st=st: e.reciprocal(out=st[:, 3:4], in_=st[:, 2:3]), reads=[bstat], writes=[bstat])
                P.op("dve", lambda e, src=src, st=st: e.scalar_tensor_tensor(out=src, in0=src, scalar=st[:, 3:4], in1=fg_bc,
                                                                            op0=ALU.mult, op1=ALU.mult),
                     reads=[bres[i], bstat, bfg], writes=[bres[i]])
            o = P.op(cx.hwq(), lambda e, i=i, t0=t0, src=src: e.dma_start(out=h_out[t0 + i * 128: t0 + (i + 1) * 128, :], in_=src),
                     reads=[bres[i]], semkey=f"hout{i}")
            cx.out_ops.append(o)
    A.reset(m0)


NVEC = 96
V_CW, V_CB, V_BA, V_BI, V_LAM, V_LB, V_GN = 0, 32, 40, 48, 56, 64, 88
GELU_C = 0.7978845608028654


def proj_fm(cx, hnT, bhnT, col0, T, wv, wcol0, ncol, ws, consume, wtile=None, after_load=None):
    P = cx.P
    wtile = wtile or ncol
    cnt = 0
    for g in range(ncol // wtile):
        wt, bw = ws.load(wv[:, :, wcol0 + g * wtile: wcol0 + (g + 1) * wtile], lambda s: s[:, 0:16 * wtile].rearrange("p (c n) -> p c n", n=wtile))
        if after_load is not None:
            after_load()
        for mm in range(wtile // 128):
            m = g * (wtile // 128) + mm
            for n in range((T + 511) // 512):
                tn = min(512, T - n * 512)
                bank = cnt % 2
                cnt += 1
                for kc in range(16):
                    P.op("pe", lambda e, bank=bank, wt=wt, kc=kc, mm=mm, n=n, tn=tn: e.matmul(
                        cx.ps[bank][:, 0:tn], lhsT=wt[:, kc, mm * 128:(mm + 1) * 128], rhs=hnT[:, kc, col0 + n * 512: col0 + n * 512 + tn],
                        start=(kc == 0), stop=(kc == 15)), reads=[bw, bhnT], writes=[cx.psb[bank]], sig=(kc == 15))
                consume(m, n, tn, bank)


def stage_r1(cx, T, x, xh, gain, w_in, vecs, w_a, w_i, outs):
    P = cx.P
    A = cx.arena
    m0 = A.mark()
    NT = T // 128
    TC = 128 + T
    vec = A.f32(NVEC); bvec = P.buf("vec")
    P.op("sp", lambda e: e.dma_start(out=vec, in_=vecs), writes=[bvec], semkey="vec")
    gain_bc = A.f32(D); bgain = P.buf("gain")
    P.op("sp", lambda e: e.dma_start(out=gain_bc, in_=gain.partition_broadcast(128)), writes=[bgain], semkey="gain")
    der = A.f32(64); bder = P.buf("der")
    lb, oml, noml, sca = der[:, 0:8], der[:, 8:16], der[:, 16:24], der[:, 24:32]
    t0_, t1_, t2_ = der[:, 32:40], der[:, 40:48], der[:, 48:56]
    l0, l1, l2 = vec[:, V_LB:V_LB + 8], vec[:, V_LB + 8:V_LB + 16], vec[:, V_LB + 16:V_LB + 24]
    dv = lambda fn, r=(bvec, bder), w=(bder,): P.op("dve", fn, reads=list(r), writes=list(w))
    av = lambda fn, r=(bvec, bder), w=(bder,): P.op("act", fn, reads=list(r), writes=list(w))
    dv(lambda e: e.tensor_max(out=t0_, in0=l0, in1=l1))
    dv(lambda e: e.tensor_max(out=t0_, in0=t0_, in1=l2))
    dv(lambda e: e.tensor_sub(out=t1_, in0=l0, in1=t0_))
    av(lambda e: e.activation(out=lb, in_=t1_, func=AF.Exp))
    dv(lambda e: e.tensor_sub(out=t1_, in0=l1, in1=t0_))
    av(lambda e: e.activation(out=t2_, in_=t1_, func=AF.Exp))
    dv(lambda e: e.tensor_add(out=oml, in0=lb, in1=t2_))
    dv(lambda e: e.tensor_sub(out=t1_, in0=l2, in1=t0_))
    av(lambda e: e.activation(out=t2_, in_=t1_, func=AF.Exp))
    dv(lambda e: e.tensor_add(out=oml, in0=oml, in1=t2_))
    dv(lambda e: e.reciprocal(out=t2_, in_=oml))
    dv(lambda e: e.tensor_mul(out=lb, in0=lb, in1=t2_))
    dv(lambda e: e.tensor_scalar(out=oml, in0=lb, scalar1=-1.0, scalar2=1.0, op0=ALU.mult, op1=ALU.add))
    dv(lambda e: e.tensor_scalar(out=noml, in0=oml, scalar1=-1.0, scalar2=None, op0=ALU.mult))
    lam = vec[:, V_LAM:V_LAM + 8]
    dv(lambda e: e.tensor_scalar(out=t1_, in0=lam, scalar1=-1.0, scalar2=None, op0=ALU.mult))
    dv(lambda e: e.tensor_max(out=t0_, in0=lam, in1=t1_))
    av(lambda e: e.activation(out=t0_, in_=t0_, func=AF.Exp, scale=-1.0))
    dv(lambda e: e.tensor_scalar(out=t1_, in0=t0_, scalar1=2.0, scalar2=None, op0=ALU.add))
    dv(lambda e: e.reciprocal(out=t1_, in_=t1_))
    dv(lambda e: e.tensor_mul(out=t0_, in0=t0_, in1=t1_))
    dv(lambda e: e.tensor_mul(out=t1_, in0=t0_, in1=t0_))
    dv(lambda e: e.tensor_scalar(out=t2_, in0=t1_, scalar1=1.0 / 15, scalar2=1.0 / 13, op0=ALU.mult, op1=ALU.add))
    for cst in (1.0 / 11, 1.0 / 9, 1.0 / 7, 1.0 / 5, 1.0 / 3, 1.0):
        dv(lambda e: e.tensor_mul(out=t2_, in0=t2_, in1=t1_))
        dv(lambda e, cst=cst: e.tensor_scalar(out=t2_, in0=t2_, scalar1=cst, scalar2=None, op0=ALU.add))
    dv(lambda e: e.tensor_mul(out=t2_, in0=t2_, in1=t0_))
    dv(lambda e: e.tensor_scalar(out=t0_, in0=lam, scalar1=-1.0, scalar2=0.0, op0=ALU.mult, op1=ALU.max))
    dv(lambda e: e.scalar_tensor_tensor(out=sca, in0=t2_, scalar=2.0, in1=t0_, op0=ALU.mult, op1=ALU.add))
    dv(lambda e: e.tensor_scalar(out=sca, in0=sca, scalar1=-8.0, scalar2=None, op0=ALU.mult))
    if "dbg" in outs:
        P.op("sp", lambda e: e.dma_start(out=outs["dbg"][:, 0:64], in_=der), reads=[bder], semkey="dbg")
        P.op("sp", lambda e: e.dma_start(out=outs["dbg"][:, 64:64 + NVEC], in_=vec), reads=[bvec], semkey="dbg2")
    hnT = A.bf16(16 * TC).rearrange("p (c t) -> p c t", t=TC); bhnT = P.buf("hnT")
    mk1 = A.mark()
    xt = [A.f32(D) for _ in range(2)]; bxt = [P.buf("xt0"), P.buf("xt1")]
    hn_tmp = [A.bf16(D) for _ in range(2)]; bhn_tmp = [P.buf("hntmp0"), P.buf("hntmp1")]
    stat = A.f32(4 * 2); bstat = P.buf("stat")
    for i in range(NT + 1):
        s = i % 2
        src = xh if i == 0 else x[(i - 1) * 128: i * 128, :]
        P.op(cx.hwq(), lambda e, s=s, src=src: e.dma_start(out=xt[s], in_=src), writes=[bxt[s]], semkey=f"xt{s}")
        rmsnorm_T(cx, [(xt[s], bxt[s])], gain_bc, bgain, hnT, bhnT, 1, [hn_tmp[s]], [bhn_tmp[s]], stat[:, 4 * s:4 * s + 4], bstat, col0=i * 128)
    A.reset(mk1)
    w_in_v = w_in.rearrange("(kc p) n -> p kc n", p=128)
    ws = WStream(cx, "wst", 2, 16 * 256)
    nT5 = (T + 511) // 512
    mk2 = A.mark()
    rows = [A.f32(T) for _ in range(2)]; brows = [P.buf("row0"), P.buf("row1")]
    tmpa = [A.f32(512) for _ in range(2)]; btmpa = [P.buf("tmpa0"), P.buf("tmpa1")]
    tmpb = [A.f32(512) for _ in range(2)]; btmpb = [P.buf("tmpb0"), P.buf("tmpb1")]
    cnt = [0]

    def consume_gy(m, n, tn, bank):
        k = cnt[0] % 2; cnt[0] += 1
        row, brow = rows[m % 2], brows[m % 2]
        ps = cx.ps[bank][:, 0:tn]; bps = cx.psb[bank]
        ta, tb = tmpa[k][:, 0:tn], tmpb[k][:, 0:tn]
        P.op("act", lambda e: e.activation(out=ta, in_=ps, func=AF.Square), reads=[bps], writes=[btmpa[k]])
        P.op("dve", lambda e: e.tensor_scalar(out=ta, in0=ta, scalar1=0.044715, scalar2=1.0, op0=ALU.mult, op1=ALU.add), reads=[btmpa[k]], writes=[btmpa[k]])
        P.op("dve", lambda e: e.tensor_tensor(out=ta, in0=ta, in1=ps, op=ALU.mult), reads=[btmpa[k], bps], writes=[btmpa[k]])
        P.op("act", lambda e: e.activation(out=tb, in_=ta, func=AF.Sigmoid, scale=2.0 * GELU_C), reads=[btmpa[k]], writes=[btmpb[k]])
        P.op("dve", lambda e: e.tensor_tensor(out=row[:, n * 512:n * 512 + tn], in0=tb, in1=ps, op=ALU.mult), reads=[btmpb[k], bps], writes=[brow])
        if n == nT5 - 1:
            P.op("sp", lambda e: e.dma_start(out=outs["gy"][m], in_=row), reads=[brow], semkey=f"orow{m % 2}")

    proj_fm(cx, hnT, bhnT, 128, T, w_in_v, 1024, 1024, ws, consume_gy, wtile=256)

    def consume_sg(m, n, tn, bank):
        row, brow = rows[m % 2], brows[m % 2]
        ps = cx.ps[bank][:, 0:tn]; bps = cx.psb[bank]
        P.op("act", lambda e: e.activation(out=row[:, n * 512:n * 512 + tn], in_=ps, func=AF.Silu), reads=[bps], writes=[brow])
        if n == nT5 - 1:
            P.op("sp", lambda e: e.dma_start(out=outs["sg"][m], in_=row), reads=[brow], semkey=f"orow{m % 2}")

    proj_fm(cx, hnT, bhnT, 128, T, w_in_v, 5120, 1024, ws, consume_sg, wtile=256)
    A.reset(mk2)
    car = A.f32(24 + 1024); bcar = P.buf("car")
    mk3 = A.mark()
    TL = T + 4
    xl = [A.f32(TL) for _ in range(2)]; bxl = [P.buf("xl0"), P.buf("xl1")]
    xc = [A.f32(T) for _ in range(2)]; bxc = [P.buf("xc0"), P.buf("xc1")]
    xcb = [A.bf16(T) for _ in range(2)]; bxcb = [P.buf("xcb0"), P.buf("xcb1")]
    arow = [A.f32(T) for _ in range(2)]; barow = [P.buf("arow0"), P.buf("arow1")]
    urow = [A.f32(T) for _ in range(2)]; burow = [P.buf("urow0"), P.buf("urow1")]
    hsc = A.f32(T); bhsc = P.buf("hsc")
    rt = [A.f32(512) for _ in range(2)]; brt = [P.buf("rt0"), P.buf("rt1")]
    it = [A.f32(512) for _ in range(2)]; bit = [P.buf("it0"), P.buf("it1")]
    tt_ = [A.f32(512) for _ in range(2)]; btt = [P.buf("tt0"), P.buf("tt1")]
    sumr = A.f32(8); bsumr = P.buf("sumr")
    wg = WStream(cx, "wg", 2, 2 * 2 * 256)
    for h in range(4):
        def consume_xl(m, n, tn, bank, h=h):
            cc = m
            P.op("act", lambda e: e.activation(out=xl[cc][:, 4 + n * 512: 4 + n * 512 + tn], in_=cx.ps[bank][:, 0:tn], func=AF.Copy),
                 reads=[cx.psb[bank]], writes=[bxl[cc]])
        wt, bw = ws.load(w_in_v[:, :, h * 256:(h + 1) * 256], lambda s: s[:, 0:16 * 256].rearrange("p (c n) -> p c n", n=256))
        for cc in range(2):
            for n in range(nT5):
                tn = min(512, T - n * 512)
                bank = (cc * nT5 + n) % 2
                for kc in range(16):
                    P.op("pe", lambda e, bank=bank, kc=kc, cc=cc, n=n, tn=tn, wt=wt: e.matmul(
                        cx.ps[bank][:, 0:tn], lhsT=wt[:, kc, cc * 128:(cc + 1) * 128], rhs=hnT[:, kc, 128 + n * 512: 128 + n * 512 + tn],
                        start=(kc == 0), stop=(kc == 15)), reads=[bw, bhnT], writes=[cx.psb[bank]], sig=(kc == 15))
                consume_xl(cc, n, tn, bank)
            bank = 2 + cc
            for kc in range(16):
                P.op("pe", lambda e, bank=bank, kc=kc, cc=cc, wt=wt: e.matmul(
                    cx.ps[bank][:, 0:4], lhsT=wt[:, kc, cc * 128:(cc + 1) * 128], rhs=hnT[:, kc, 124:128],
                    start=(kc == 0), stop=(kc == 15)), reads=[bw, bhnT], writes=[cx.psb[bank]], sig=(kc == 15))
            P.op("act", lambda e, cc=cc, bank=bank: e.activation(out=xl[cc][:, 0:4], in_=cx.ps[bank][:, 0:4], func=AF.Copy),
                 reads=[cx.psb[bank]], writes=[bxl[cc]])
            c = 2 * h + cc
            cwj = lambda j, c=c: vec[:, V_CW + 8 * j + c: V_CW + 8 * j + c + 1]
            P.op("act", lambda e, cc=cc, c=c, cwj=cwj: e.activation(out=xc[cc], in_=xl[cc][:, 1:1 + T], func=AF.Identity, scale=cwj(0),
                                                                   bias=vec[:, V_CB + c:V_CB + c + 1]), reads=[bxl[cc], bvec], writes=[bxc[cc]])
            for j in range(1, 4):
                P.op("dve", lambda e, cc=cc, j=j, cwj=cwj: e.scalar_tensor_tensor(out=xc[cc], in0=xl[cc][:, 1 + j:1 + j + T], scalar=cwj(j), in1=xc[cc],
                                                                                op0=ALU.mult, op1=ALU.add), reads=[bxl[cc], bvec, bxc[cc]], writes=[bxc[cc]])
            P.op("act", lambda e, cc=cc: e.activation(out=xcb[cc], in_=xc[cc], func=AF.Copy), reads=[bxc[cc]], writes=[bxcb[cc]])
        s = wg.i % wg.n
        wgt = wg.slots[s].rearrange("p (g i n) -> p g i n", g=2, i=2); bwg = wg.bufs[s]
        wg.i += 1
        P.op("poolq", lambda e, wgt=wgt, h=h: e.dma_start(out=wgt[:, 0], in_=w_a[h].rearrange("(i p) n -> p i n", p=128)), writes=[bwg], semkey=f"wg{s}")
        P.op("poolq", lambda e, wgt=wgt, h=h: e.dma_start(out=wgt[:, 1], in_=w_i[h].rearrange("(i p) n -> p i n", p=128)), writes=[bwg], semkey=f"wg{s}")
        for jj in range(2):
            j = 2 * h + jj
            ar, bar = arow[j % 2], barow[j % 2]
            ur, bur = urow[j % 2], burow[j % 2]
            for n in range(nT5):
                tn = min(512, T - n * 512)
                k = n % 2
                sl = slice(n * 512, n * 512 + tn)
                for g in range(2):
                    bank = 4 + g
                    for ii in range(2):
                        P.op("pe", lambda e, bank=bank, g=g, ii=ii, jj=jj, sl=sl, tn=tn, wgt=wgt: e.matmul(
                            cx.ps[bank][:, 0:tn], lhsT=wgt[:, g, ii, jj * 128:(jj + 1) * 128], rhs=xcb[ii][:, sl], start=(ii == 0), stop=(ii == 1)),
                            reads=[bwg, bxcb[ii]], writes=[cx.psb[bank]], sig=(ii == 1))
                r_, i_, t_ = rt[k][:, 0:tn], it[k][:, 0:tn], tt_[k][:, 0:tn]
                P.op("act", lambda e, r_=r_, j=j, n=n: e.activation(out=r_, in_=cx.ps[4][:, 0:r_.shape[1]], func=AF.Sigmoid, bias=vec[:, V_BA + j:V_BA + j + 1],
                                                                  accum_out=sumr[:, n:n + 1]), reads=[cx.psb[4], bvec], writes=[brt[k], bsumr])
                P.op("act", lambda e, i_=i_, j=j: e.activation(out=i_, in_=cx.ps[5][:, 0:i_.shape[1]], func=AF.Sigmoid, bias=vec[:, V_BI + j:V_BI + j + 1]),
                     reads=[cx.psb[5], bvec], writes=[bit[k]])
                P.op("act", lambda e, r_=r_, j=j, sl=sl, ar=ar: e.activation(out=ar[:, sl], in_=r_, func=AF.Exp, scale=sca[:, j:j + 1]),
                     reads=[brt[k], bder], writes=[bar])
                P.op("dve", lambda e, t_=t_, sl=sl, ar=ar: e.tensor_tensor(out=t_, in0=ar[:, sl], in1=ar[:, sl], op=ALU.mult), reads=[bar], writes=[btt[k]])
                P.op("dve", lambda e, t_=t_: e.tensor_scalar(out=t_, in0=t_, scalar1=-1.0, scalar2=1.0, op0=ALU.mult, op1=ALU.add), reads=[btt[k]], writes=[btt[k]])
                P.op("act", lambda e, t_=t_: e.activation(out=t_, in_=t_, func=AF.Sqrt), reads=[btt[k]], writes=[btt[k]])
                P.op("dve", lambda e, t_=t_, i_=i_: e.tensor_tensor(out=t_, in0=t_, in1=i_, op=ALU.mult), reads=[btt[k], bit[k]], writes=[btt[k]])
                P.op("dve", lambda e, t_=t_, sl=sl, ur=ur, jj=jj: e.tensor_tensor(out=ur[:, sl], in0=t_, in1=xc[jj][:, sl], op=ALU.mult),
                     reads=[btt[k], bxc[jj]], writes=[bur])
            P.op("sp", lambda e, j=j, ar=ar: e.dma_start(out=outs["a"][j], in_=ar), reads=[bar], semkey=f"oa{j % 2}")
            P.op("sp", lambda e, j=j, ur=ur: e.dma_start(out=outs["u"][j], in_=ur), reads=[bur], semkey=f"ou{j % 2}")
            P.op("dve", lambda e, ar=ar, ur=ur: e.tensor_tensor_scan(out=hsc, data0=ar, data1=ur, initial=0.0, op0=ALU.mult, op1=ALU.add),
                 reads=[bar, bur], writes=[bhsc])
            P.op("dve", lambda e, j=j: e.tensor_copy(out=car[:, j:j + 1], in_=hsc[:, T - 1:T]), reads=[bhsc], writes=[bcar])
            P.op("dve", lambda e, j=j: e.tensor_reduce(out=car[:, 8 + j:9 + j], in_=sumr[:, 0:nT5], axis=AX.X, op=ALU.add), reads=[bsumr], writes=[bcar])
            P.op("act", lambda e, j=j: e.activation(out=car[:, 8 + j:9 + j], in_=car[:, 8 + j:9 + j], func=AF.Exp, scale=sca[:, j:j + 1]),
                 reads=[bcar, bder], writes=[bcar])
    A.reset(mk3)
    NC_ = T // 64
    ones_row = A.bf16(T); mreset = A.bf16(T); bcm = P.buf("cmask")
    P.op("pool", lambda e: e.memset(ones_row, 1.0), writes=[bcm])
    P.op("pool", lambda e: e.memset(mreset, 1.0), writes=[bcm])
    P.op("pool", lambda e: e.memset(mreset.rearrange("p (c t) -> p c t", t=64)[:, :, 0:1], 0.0), reads=[bcm], writes=[bcm])
    mut = A.bf16(64); mutf = A.f32(64); bmut = P.buf("mut")
    P.op("pool", lambda e: e.memset(mutf[0:64, :], 1.0), writes=[bmut])
    P.op("pool", lambda e: e.affine_select(out=mutf[0:64, :], in_=mutf[0:64, :], pattern=[[1, 64]], compare_op=ALU.is_ge, fill=0.0, base=0, channel_multiplier=-1),
         reads=[bmut], writes=[bmut])
    P.op("dve", lambda e: e.tensor_copy(out=mut[0:64, :], in_=mutf[0:64, :]), reads=[bmut], writes=[bmut])
    qf = A.f32(T); bqf = P.buf("qf")
    sig = A.f32(T); bsig = P.buf("sig")
    lgf = A.f32(T); blgf = P.buf("lgf")
    bb = A.f32(T); bbb = P.buf("bb")
    Bs = A.f32(T); bBs = P.buf("Bs")
    eb = A.f32(T); beb = P.buf("eb")
    qt = A.bf16(T); bqt = P.buf("qt")
    kt = A.bf16(T); bkt = P.buf("kt")
    qh = A.bf16(T); bqh = P.buf("qh")
    qt2 = A.bf16(T); bqt2 = P.buf("qt2")
    ktok = A.bf16(NC_ * 128).rearrange("p (c k) -> p c k", k=128); bktok = P.buf("ktok")
    vtok = A.bf16(NC_ * 128).rearrange("p (c k) -> p c k", k=128); bvtok = P.buf("vtok")
    oloc = A.f32(T); boloc = P.buf("oloc")
    S = A.f32(128); bS = P.buf("S")
    Sb = A.bf16(128); bSb = P.buf("Sb")
    stmp = A.f32(128); bstmp = P.buf("stmp")
    scm = [A.bf16(64) for _ in range(2)]; bscm = [P.buf("scm0"), P.buf("scm1")]
    wq = ws
    for h in range(8):
        lbh, omlh, nomlh = lb[:, h:h + 1], oml[:, h:h + 1], noml[:, h:h + 1]

        def consume_q(m, n, tn, bank):
            P.op("act", lambda e: e.activation(out=qf[:, n * 512:n * 512 + tn], in_=cx.ps[bank][:, 0:tn], func=AF.Silu), reads=[cx.psb[bank]], writes=[bqf])

        def consume_f(m, n, tn, bank):
            P.op("act", lambda e: e.activation(out=sig[:, n * 512:n * 512 + tn], in_=cx.ps[bank][:, 0:tn], func=AF.Sigmoid), reads=[cx.psb[bank]], writes=[bsig])

        proj_fm(cx, hnT, bhnT, 128, T, w_in_v, 2048 + h * 128, 128, wq, consume_q)
        proj_fm(cx, hnT, bhnT, 128, T, w_in_v, 3072 + h * 128, 128, wq, consume_f)
        wt, bw = wq.load(w_in_v[:, :, 4096 + h * 128: 4096 + (h + 1) * 128], lambda s: s[:, 0:16 * 128].rearrange("p (c n) -> p c n", n=128))
        for c4 in range(0, NC_, 4):
            bank = (c4 // 4) % 2
            pv = cx.ps[bank][0:64, :].rearrange("p (c k) -> p c k", k=128)
            for cj in range(4):
                c = c4 + cj
                for kc in range(16):
                    P.op("pe", lambda e, pv=pv, cj=cj, c=c, kc=kc, wt=wt: e.matmul(pv[:, cj, :], lhsT=hnT[:, kc, 128 + c * 64: 128 + (c + 1) * 64], rhs=wt[:, kc, :],
                                                                                  start=(kc == 0), stop=(kc == 15)),
                         reads=[bw, bhnT], writes=[cx.psb[bank]], sig=(kc == 15 and cj == 3))
            P.op("act" if (c4 // 4) % 2 else "dve",
                 (lambda e, pv=pv, c4=c4: e.activation(out=vtok[0:64, c4:c4 + 4, :], in_=pv, func=AF.Copy)) if (c4 // 4) % 2 else
                 (lambda e, pv=pv, c4=c4: e.tensor_copy(out=vtok[0:64, c4:c4 + 4, :], in_=pv)),
                 reads=[cx.psb[bank]], writes=[bvtok])
        P.op("act", lambda e, omlh=omlh, lbh=lbh: e.activation(out=lgf, in_=sig, func=AF.Ln, scale=omlh, bias=lbh), reads=[bsig, bder], writes=[blgf])
        P.op("dve", lambda e, nomlh=nomlh, omlh=omlh: e.tensor_scalar(out=sig, in0=sig, scalar1=nomlh, scalar2=omlh, op0=ALU.mult, op1=ALU.add),
             reads=[bsig, bder], writes=[bsig])
        P.op("dve", lambda e: e.tensor_tensor_scan(out=bb, data0=mreset, data1=lgf, initial=0.0, op0=ALU.mult, op1=ALU.add), reads=[bcm, blgf], writes=[bbb])
        P.op("dve", lambda e: e.tensor_tensor_scan(out=Bs, data0=ones_row, data1=lgf, initial=0.0, op0=ALU.mult, op1=ALU.add), reads=[bcm, blgf], writes=[bBs])
        P.op("act", lambda e: e.activation(out=eb, in_=bb, func=AF.Exp), reads=[bbb], writes=[beb])
        P.op("dve", lambda e: e.tensor_scalar(out=lgf, in0=bb, scalar1=-1.0, scalar2=80.0, op0=ALU.mult, op1=ALU.min), reads=[bbb, blgf], writes=[blgf])
        P.op("act", lambda e: e.activation(out=lgf, in_=lgf, func=AF.Exp), reads=[blgf], writes=[blgf])
        P.op("dve", lambda e: e.tensor_tensor(out=kt, in0=sig, in1=lgf, op=ALU.mult), reads=[bsig, blgf], writes=[bkt])
        P.op("dve", lambda e: e.tensor_tensor(out=lgf, in0=qf, in1=eb, op=ALU.mult), reads=[bqf, beb, bkt], writes=[blgf])
        P.op("act", lambda e: e.activation(out=qt, in_=lgf, func=AF.Copy), reads=[blgf], writes=[bqt])
        P.op("pool", lambda e: e.memset(qt2[:, 0:64], 0.0), writes=[bqt2])
        P.op("dve", lambda e: e.tensor_tensor(out=qt2.rearrange("p (c t) -> p c t", t=64)[:, 1:NC_, :], in0=lgf.rearrange("p (c t) -> p c t", t=64)[:, 1:NC_, :],
                                              in1=eb.rearrange("p (c t) -> p c t", t=64)[:, 0:NC_ - 1, 63:64].to_broadcast([128, NC_ - 1, 64]), op=ALU.mult),
             reads=[blgf, beb], writes=[bqt2])
        P.op("act", lambda e: e.activation(out=Bs, in_=Bs, func=AF.Exp), reads=[bBs], writes=[bBs])
        P.op("dve", lambda e: e.tensor_tensor(out=qh, in0=qf, in1=Bs, op=ALU.mult), reads=[bqf, bBs], writes=[bqh])
        P.op("sp", lambda e, h=h: e.dma_start(out=outs["qh"][h], in_=qh), reads=[bqh], semkey="oqh")
        P.op("dve", lambda e, h=h: e.tensor_copy(out=car[:, 16 + h:17 + h], in_=Bs[:, T - 1:T]), reads=[bBs], writes=[bcar])
        for c8 in range(0, NC_, 8):
            bank = 2 + (c8 // 8) % 2
            pv = cx.psbf(bank)[0:64, :].rearrange("p (c k) -> p c k", k=128)
            for cj in range(8):
                c = c8 + cj
                P.op("pe", lambda e, pv=pv, cj=cj, c=c: e.transpose(pv[:, cj, :], kt[:, c * 64:(c + 1) * 64], cx.ident[:]),
                     reads=[bkt, cx.bconst], writes=[cx.psb[bank]], sig=(cj == 7))
            P.op("dve", lambda e, pv=pv, c8=c8: e.tensor_copy(out=ktok[0:64, c8:c8 + 8, :], in_=pv), reads=[cx.psb[bank]], writes=[bktok])
        P.op("pool", lambda e: e.memset(S, 0.0), writes=[bS])
        P.op("pool", lambda e: e.memset(Sb, 0.0), writes=[bSb])
        ebv = eb.rearrange("p (c t) -> p c t", t=64)
        for c in range(NC_):
            cs = slice(c * 64, (c + 1) * 64)
            k2 = c % 2
            sbank = 4 + k2
            P.op("pe", lambda e, sbank=sbank, cs=cs: e.matmul(cx.ps[sbank][0:64, 0:64], lhsT=kt[:, cs], rhs=qt[:, cs], start=True, stop=True),
                 reads=[bkt, bqt], writes=[cx.psb[sbank]])
            P.op("dve", lambda e, sbank=sbank, k2=k2: e.tensor_tensor(out=scm[k2][0:64, :], in0=cx.ps[sbank][0:64, 0:64], in1=mut[0:64, :], op=ALU.mult),
                 reads=[cx.psb[sbank], bmut], writes=[bscm[k2]])
            pbank = 2 + k2
            P.op("pe", lambda e, pbank=pbank, c=c: e.matmul(cx.ps[pbank][:, 0:128], lhsT=ktok[0:64, c, :], rhs=vtok[0:64, c, :], start=True, stop=True),
                 reads=[bktok, bvtok], writes=[cx.psb[pbank]])
            obank = 6 + (c // 8) % 2
            oc = slice((c % 8) * 64, (c % 8 + 1) * 64)
            P.op("pe", lambda e, obank=obank, oc=oc, cs=cs: e.matmul(cx.ps[obank][:, oc], lhsT=Sb, rhs=qt2[:, cs], start=True, stop=False),
                 reads=[bSb, bqt2], writes=[cx.psb[obank]], sig=False)
            P.op("pe", lambda e, obank=obank, oc=oc, c=c, k2=k2: e.matmul(cx.ps[obank][:, oc], lhsT=vtok[0:64, c, :], rhs=scm[k2][0:64, :], start=False, stop=True),
                 reads=[bvtok, bscm[k2]], writes=[cx.psb[obank]])
            if c % 8 == 7 or c == NC_ - 1:
                c0 = (c // 8) * 8
                w = (c - c0 + 1) * 64
                P.op("act", lambda e, obank=obank, c0=c0, w=w: e.activation(out=oloc[:, c0 * 64: c0 * 64 + w], in_=cx.ps[obank][:, 0:w], func=AF.Copy),
                     reads=[cx.psb[obank]], writes=[boloc])
            ep = ebv[:, max(c - 1, 0), 63:64]
            P.op("dve", lambda e, pbank=pbank, ep=ep: e.scalar_tensor_tensor(out=S, in0=S, scalar=ep, in1=cx.ps[pbank][:, 0:128], op0=ALU.mult, op1=ALU.add),
                 reads=[bS, beb, cx.psb[pbank]], writes=[bS])
            P.op("act", lambda e: e.activation(out=Sb, in_=S, func=AF.Copy), reads=[bS], writes=[bSb])
        P.op("dve", lambda e: e.tensor_scalar(out=S, in0=S, scalar1=ebv[:, NC_ - 1, 63:64], scalar2=None, op0=ALU.mult), reads=[bS, beb], writes=[bS])
        P.op("sp", lambda e, h=h: e.dma_start(out=outs["ol"][h], in_=oloc), reads=[boloc], semkey="ool")
        P.op("dve", lambda e, h=h: e.tensor_copy(out=car[:, 24 + h * 128: 24 + (h + 1) * 128], in_=S), reads=[bS], writes=[bcar])
    o = P.op("sp", lambda e: e.dma_start(out=outs["car"], in_=car), reads=[bcar], semkey="ocar")
    cx.out_ops.append(o)
    A.reset(m0)


def proj_tm_residual(cx, actT, bactT, w, res_in, out, T):
    P = cx.P
    A = cx.arena
    wv = w.rearrange("(kc p) n -> p kc n", p=128)
    wt = A.bf16(16 * D).rearrange("p (c n) -> p c n", n=D); bwt = [P.buf(f"wres{dq}") for dq in range(4)]
    for dq in range(4):
        P.op("poolq", lambda e, dq=dq: e.dma_start(out=wt[:, :, dq * 512:(dq + 1) * 512], in_=wv[:, :, dq * 512:(dq + 1) * 512]), writes=[bwt[dq]], semkey=f"wres{dq}")
    xt = [A.f32(D) for _ in range(2)]; bxt = [P.buf("rx0"), P.buf("rx1")]
    for i in range(T // 128):
        s = i % 2
        P.op(cx.hwq(), lambda e, s=s, i=i: e.dma_start(out=xt[s], in_=res_in[i * 128:(i + 1) * 128, :]), writes=[bxt[s]], semkey=f"rx{s}")
        for dq in range(4):
            bank = (i % 2) * 4 + dq
            for c in range(16):
                P.op("pe", lambda e, bank=bank, c=c, i=i, dq=dq: e.matmul(cx.ps[bank][:], lhsT=actT[:, c, i * 128:(i + 1) * 128], rhs=wt[:, c, dq * 512:(dq + 1) * 512],
                                                                          start=(c == 0), stop=(c == 15)), reads=[bactT, bwt[dq]], writes=[cx.psb[bank]], sig=(c == 15))
            dst = xt[s][:, dq * 512:(dq + 1) * 512]
            P.op("dve", lambda e, dst=dst, bank=bank: e.tensor_tensor(out=dst, in0=cx.ps[bank][:], in1=dst, op=ALU.add), reads=[cx.psb[bank], bxt[s]], writes=[bxt[s]])
        P.op(cx.hwq(), lambda e, s=s, i=i: e.dma_start(out=out[i * 128:(i + 1) * 128, :], in_=xt[s]), reads=[bxt[s]], semkey=f"ro{s}")


def stage_r2(cx, T, car_all, cmask, ins, x, vecs, w_out, h1_out, NR=8):
    P = cx.P
    A = cx.arena
    m0 = A.mark()
    nT5 = (T + 511) // 512
    vec = A.f32(NVEC); bvec = P.buf("vec")
    P.op("sp", lambda e: e.dma_start(out=vec, in_=vecs), writes=[bvec], semkey="vec")
    epsb = A.f32(1)
    P.op("pool", lambda e: e.memset(epsb, EPS), writes=[bvec])
    cm = A.f32(NR); bcm = P.buf("cm")
    P.op("sp", lambda e: e.dma_start(out=cm, in_=cmask), writes=[bcm], semkey="cm")
    mixT = A.bf16(16 * T).rearrange("p (c t) -> p c t", t=T); bmix = P.buf("mixT")
    hc = A.f32(8); bhc = P.buf("hc")
    Sc = A.f32(1024); bSc = P.buf("Sc")
    Scb = A.bf16(1024); bScb = P.buf("Scb")
    mk = A.mark()
    CW = 24 + 1024
    cars = A.f32(NR * CW).rearrange("p (r c) -> p r c", c=CW); bcars = P.buf("cars")
    P.op("sp", lambda e: e.dma_start(out=cars, in_=car_all.rearrange("r p c -> p r c")), writes=[bcars], semkey="cars")
    th = A.f32(8); bth = P.buf("th")
    tS = A.f32(1024); btS = P.buf("tS")
    P.op("pool", lambda e: e.memset(hc, 0.0), writes=[bhc])
    P.op("pool", lambda e: e.memset(Sc, 0.0), writes=[bSc])
    Sc3 = Sc.rearrange("p (h v) -> p h v", v=128)
    tS3 = tS.rearrange("p (h v) -> p h v", v=128)
    for r in range(NR):
        cr = cars[:, r, :]
        mr = cm[:, r:r + 1]
        P.op("dve", lambda e, cr=cr: e.tensor_tensor(out=th, in0=hc, in1=cr[:, 8:16], op=ALU.mult), reads=[bhc, bcars], writes=[bth])
        P.op("dve", lambda e, cr=cr: e.tensor_tensor(out=th, in0=th, in1=cr[:, 0:8], op=ALU.add), reads=[bth, bcars], writes=[bth])
        P.op("dve", lambda e: e.tensor_tensor(out=th, in0=th, in1=hc, op=ALU.subtract), reads=[bth, bhc], writes=[bth])
        P.op("dve", lambda e, mr=mr: e.scalar_tensor_tensor(out=hc, in0=th, scalar=mr, in1=hc, op0=ALU.mult, op1=ALU.add), reads=[bth, bhc, bcm], writes=[bhc])
        P.op("dve", lambda e, cr=cr: e.tensor_tensor(out=tS3, in0=Sc3, in1=cr[:, 16:24].unsqueeze(2).to_broadcast([128, 8, 128]), op=ALU.mult),
             reads=[bSc, bcars], writes=[btS])
        P.op("dve", lambda e, cr=cr: e.tensor_tensor(out=tS, in0=tS, in1=cr[:, 24:CW], op=ALU.add), reads=[btS, bcars], writes=[btS])
        P.op("dve", lambda e: e.tensor_tensor(out=tS, in0=tS, in1=Sc, op=ALU.subtract), reads=[btS, bSc], writes=[btS])
        P.op("dve", lambda e, mr=mr: e.scalar_tensor_tensor(out=Sc, in0=tS, scalar=mr, in1=Sc, op0=ALU.mult, op1=ALU.add), reads=[btS, bSc, bcm], writes=[bSc])
    P.op("act", lambda e: e.activation(out=Scb, in_=Sc, func=AF.Copy), reads=[bSc], writes=[bScb])
    A.reset(mk)
    ra = [A.f32(T) for _ in range(2)]; bra = [P.buf("ra0"), P.buf("ra1")]
    ru = [A.f32(T) for _ in range(2)]; bru = [P.buf("ru0"), P.buf("ru1")]
    rg = [A.f32(T) for _ in range(2)]; brg = [P.buf("rg0"), P.buf("rg1")]
    rq = [A.bf16(T) for _ in range(2)]; brq = [P.buf("rq0"), P.buf("rq1")]
    hs = A.f32(T); bhs = P.buf("hs")
    for j in range(8):
        s = j % 2
        P.op("sp", lambda e, s=s, j=j: e.dma_start(out=ra[s], in_=ins["a"][j]), writes=[bra[s]], semkey=f"ra{s}")
        P.op("actq", lambda e, s=s, j=j: e.dma_start(out=ru[s], in_=ins["u"][j]), writes=[bru[s]], semkey=f"ru{s}")
        P.op("sp", lambda e, s=s, j=j: e.dma_start(out=rg[s], in_=ins["gy"][j]), writes=[brg[s]], semkey=f"rg{s}")
        P.op("dve", lambda e, s=s, j=j: e.tensor_tensor_scan(out=hs, data0=ra[s], data1=ru[s], initial=hc[:, j:j + 1], op0=ALU.mult, op1=ALU.add),
             reads=[bra[s], bru[s], bhc], writes=[bhs])
        P.op("dve", lambda e, s=s, j=j: e.tensor_tensor(out=mixT[:, j, :], in0=hs, in1=rg[s], op=ALU.mult), reads=[bhs, brg[s]], writes=[bmix])
    osq = [A.f32(512) for _ in range(2)]; bosq = [P.buf("osq0"), P.buf("osq1")]
    rsd = [A.f32(512) for _ in range(2)]; brsd = [P.buf("rsd0"), P.buf("rsd1")]
    for h in range(8):
        s = h % 2
        P.op("sp", lambda e, s=s, h=h: e.dma_start(out=ra[s], in_=ins["ol"][h]), writes=[bra[s]], semkey=f"ra{s}")
        P.op("actq", lambda e, s=s, h=h: e.dma_start(out=rg[s], in_=ins["sg"][h]), writes=[brg[s]], semkey=f"rg{s}")
        P.op("sp", lambda e, s=s, h=h: e.dma_start(out=rq[s], in_=ins["qh"][h]), writes=[brq[s]], semkey=f"rq{s}")
        for n in range(nT5):
            tn = min(512, T - n * 512)
            sl = slice(n * 512, n * 512 + tn)
            k = n % 2
            bank = k
            P.op("pe", lambda e, bank=bank, h=h, s=s, sl=sl, tn=tn: e.matmul(cx.ps[bank][:, 0:tn], lhsT=Scb[:, h * 128:(h + 1) * 128], rhs=rq[s][:, sl], start=True, stop=True),
                 reads=[bScb, brq[s]], writes=[cx.psb[bank]])
            o_ = ra[s][:, sl]
            P.op("dve", lambda e, bank=bank, o_=o_, tn=tn: e.tensor_tensor(out=o_, in0=cx.ps[bank][:, 0:tn], in1=o_, op=ALU.add), reads=[cx.psb[bank], bra[s]], writes=[bra[s]])
            q_ = osq[k][:, 0:tn]; r_ = rsd[k][:, 0:tn]
            P.op("act", lambda e, o_=o_, q_=q_: e.activation(out=q_, in_=o_, func=AF.Square), reads=[bra[s]], writes=[bosq[k]])
            b2 = 2 + k
            P.op("pe", lambda e, b2=b2, q_=q_, tn=tn: e.matmul(cx.ps[b2][:, 0:tn], lhsT=cx.ones_f[:], rhs=q_, start=True, stop=True),
                 reads=[cx.bconst, bosq[k]], writes=[cx.psb[b2]])
            P.op("act", lambda e, b2=b2, r_=r_, tn=tn: e.activation(out=r_, in_=cx.ps[b2][:, 0:tn], func=AF.Sqrt, scale=1.0 / 128, bias=epsb[:, 0:1]),
                 reads=[cx.psb[b2], bvec], writes=[brsd[k]])
            P.op("dve", lambda e, r_=r_: e.reciprocal(out=r_, in_=r_), reads=[brsd[k]], writes=[brsd[k]])
            P.op("pool", lambda e, r_=r_, o_=o_: e.tensor_tensor(out=r_, in0=o_, in1=r_, op=ALU.mult), reads=[brsd[k], bra[s]], writes=[brsd[k]])
            P.op("dve", lambda e, r_=r_, h=h, s=s, sl=sl: e.scalar_tensor_tensor(out=mixT[:, 8 + h, sl], in0=r_, scalar=vec[:, V_GN + h:V_GN + h + 1], in1=rg[s][:, sl],
                                                                               op0=ALU.mult, op1=ALU.mult), reads=[brsd[k], bvec, brg[s]], writes=[bmix])
    A.reset(mk)
    proj_tm_residual(cx, mixT, bmix, w_out, x, h1_out, T)
    A.reset(m0)


PI = 3.141592653589793


def stage_qkv(cx, T, h_in, gain, w_qkv, pos, rconst, outs, gather=None, after_q=None):
    P = cx.P
    A = cx.arena
    m0 = A.mark()
    nT5 = (T + 511) // 512
    gain_bc = A.f32(D); bgain = P.buf("gain")
    P.op("sp", lambda e: e.dma_start(out=gain_bc, in_=gain.partition_broadcast(128)), writes=[bgain], semkey="gain")
    rc = A.f32(36); brc = P.buf("rc")
    P.op("sp", lambda e: e.dma_start(out=rc[0:32, :], in_=rconst), writes=[brc], semkey="rc")
    cosT = A.f32(T); sinT = A.f32(T); btab = P.buf("tab")
    hnT = A.bf16(16 * T).rearrange("p (c t) -> p c t", t=T); bhnT = P.buf("hnT")
    mk1 = A.mark()
    posi = A.f32(T).bitcast(mybir.dt.int32); bpos = P.buf("pos")
    P.op("sp", lambda e: e.dma_start(out=posi[0:32, :], in_=pos.partition_broadcast(32)), writes=[bpos], semkey="pos")
    ang = A.f32(T); bang = P.buf("ang")
    tq = A.f32(T); btq = P.buf("tq")
    P.op("dve", lambda e: e.tensor_copy(out=ang[0:32, :], in_=posi[0:32, :]), reads=[bpos], writes=[bang])
    P.op("dve", lambda e: e.tensor_scalar(out=ang[0:32, :], in0=ang[0:32, :], scalar1=rc[0:32, 0:1], scalar2=None, op0=ALU.mult), reads=[bang, brc], writes=[bang])
    ti = A.f32(T).bitcast(mybir.dt.int32); bti = P.buf("ti")
    tf = A.f32(T); btf = P.buf("tf")

    def sin_table(dst, off):
        a32, q32, i32, f32_ = ang[0:32, :], tq[0:32, :], ti[0:32, :], tf[0:32, :]
        P.op("dve", lambda e: e.tensor_scalar(out=q32, in0=a32, scalar1=1.0 / (2 * PI), scalar2=off, op0=ALU.mult, op1=ALU.add), reads=[bang, btq], writes=[btq])
        P.op("dve", lambda e: e.tensor_copy(out=i32, in_=q32), reads=[btq], writes=[bti])
        P.op("dve", lambda e: e.tensor_copy(out=f32_, in_=i32), reads=[bti], writes=[btf])
        P.op("dve", lambda e: e.tensor_tensor(out=q32, in0=q32, in1=f32_, op=ALU.subtract), reads=[btq, btf], writes=[btq])
        P.op("dve", lambda e: e.tensor_scalar(out=f32_, in0=q32, scalar1=0.5, scalar2=None, op0=ALU.is_gt), reads=[btq, btf], writes=[btf])
        P.op("dve", lambda e: e.tensor_tensor(out=q32, in0=q32, in1=f32_, op=ALU.subtract), reads=[btq, btf], writes=[btq])
        P.op("dve", lambda e: e.tensor_scalar(out=f32_, in0=q32, scalar1=-0.5, scalar2=None, op0=ALU.is_lt), reads=[btq, btf], writes=[btf])
        P.op("dve", lambda e: e.tensor_tensor(out=q32, in0=q32, in1=f32_, op=ALU.add), reads=[btq, btf], writes=[btq])
        P.op("dve", lambda e: e.tensor_scalar(out=q32, in0=q32, scalar1=-0.49999, scalar2=0.49999, op0=ALU.max, op1=ALU.min), reads=[btq], writes=[btq])
        P.op("act", lambda e: e.activation(out=dst[0:32, :], in_=q32, func=AF.Sin, scale=2 * PI), reads=[btq], writes=[btab])

    sin_table(sinT, 0.0)
    P.op("dve", lambda e: e.tensor_scalar(out=sinT[0:32, :], in0=sinT[0:32, :], scalar1=rc[0:32, 1:2], scalar2=None, op0=ALU.mult), reads=[btab, brc], writes=[btab])
    sin_table(cosT, 0.25)
    xt = [A.f32(D) for _ in range(2)]; bxt = [P.buf("xt0"), P.buf("xt1")]
    hn_tmp = [A.bf16(D) for _ in range(2)]; bhn_tmp = [P.buf("hntmp0"), P.buf("hntmp1")]
    stat = A.f32(8); bstat = P.buf("stat")
    for i in range(T // 128):
        s = i % 2
        P.op(cx.hwq(), lambda e, s=s, i=i: e.dma_start(out=xt[s], in_=h_in[i * 128:(i + 1) * 128, :]), writes=[bxt[s]], semkey=f"xt{s}")
        rmsnorm_T(cx, [(xt[s], bxt[s])], gain_bc, bgain, hnT, bhnT, 1, [hn_tmp[s]], [bhn_tmp[s]], stat[:, 4 * s:4 * s + 4], bstat, col0=i * 128)
    A.reset(mk1)
    wv = w_qkv.rearrange("(kc p) n -> p kc n", p=128)
    ws = WStream(cx, "wqkv", 2, 16 * 256)
    rows = [A.bf16(T) for _ in range(2)]; brows = [P.buf("orow0"), P.buf("orow1")]
    qf = [A.f32(512) for _ in range(2)]; bqf = [P.buf("qf0"), P.buf("qf1")]
    t1 = [A.f32(512) for _ in range(2)]; bt1 = [P.buf("t10"), P.buf("t11")]
    t2 = [A.f32(512) for _ in range(2)]; bt2 = [P.buf("t20"), P.buf("t21")]
    cnt = [0]
    names = ["q", "k", "v"]

    which_box = [0, None, None]
    deferred = []

    def run_deferred():
        while deferred:
            deferred.pop(0)()
    bkv = [P.buf(f"kvd{h}") for h in range(16)]

    def consume(m, n, tn, bank):
        which = which_box[0]
        hh = m % 16 if which_box[1] is None else which_box[1]
        ri = m % 2 if which_box[2] is None else which_box[2]
        row, brow = rows[ri], brows[ri]
        sl = slice(n * 512, n * 512 + tn)
        ps = cx.ps[bank][:, 0:tn]; bps = cx.psb[bank]
        if which == 2:
            run_deferred()
            P.op("act", lambda e: e.activation(out=row[:, sl], in_=ps, func=AF.Copy), reads=[bps], writes=[brow])
        else:
            k = cnt[0] % 2; cnt[0] += 1
            q_ = qf[k][:, 0:tn]
            sc = (128 ** -0.5) if which == 0 else 1.0
            run_deferred()
            P.op("act", lambda e: e.activation(out=q_, in_=ps, func=AF.Copy, scale=sc), reads=[bps], writes=[bqf[k]])
            deferred.append(lambda: rope_part(which, hh, ri, row, brow, sl, tn, n, k, q_))
            return
        finish_row(which, hh, ri, row, brow, n)

    def rope_part(which, hh, ri, row, brow, sl, tn, n, k, q_):
        if True:
            b2 = 2 + k
            P.op("pe", lambda e: e.matmul(cx.ps[b2][0:32, 0:tn], lhsT=rc[0:32, 4:36], rhs=q_[0:32, :], start=True, stop=True), reads=[brc, bqf[k]], writes=[cx.psb[b2]])
            a_ = t1[k][0:32, 0:tn]; b_ = t2[k][0:32, 0:tn]
            P.op("dve", lambda e: e.tensor_tensor(out=a_, in0=q_[0:32, :], in1=cosT[0:32, sl], op=ALU.mult), reads=[bqf[k], btab], writes=[bt1[k]])
            P.op("dve", lambda e: e.tensor_tensor(out=b_, in0=cx.ps[b2][0:32, 0:tn], in1=sinT[0:32, sl], op=ALU.mult), reads=[cx.psb[b2], btab], writes=[bt2[k]])
            P.op("act", lambda e: e.activation(out=row[:, sl], in_=q_, func=AF.Copy), reads=[bqf[k]], writes=[brow])
            P.op("dve", lambda e: e.tensor_tensor(out=row[0:32, sl], in0=a_, in1=b_, op=ALU.add), reads=[bt1[k], bt2[k]], writes=[brow])
        finish_row(which, hh, ri, row, brow, n)

    def finish_row(which, hh, ri, row, brow, n):
        if n == nT5 - 1:
            if which == 0:
                P.op("sp", lambda e: e.dma_start(out=outs["q"][hh], in_=row), reads=[brow], semkey=f"oqkv{ri}")
                if after_q is not None:
                    after_q(hh)
            else:
                r0 = 0 if which == 1 else 128
                P.op("sp", lambda e: e.dma_start(out=outs["kv"][hh][r0:r0 + 128, :], in_=row), reads=[brow], writes=[bkv[hh]], semkey=f"oqkv{ri}")
                if which == 2 and gather is not None:
                    pending.append(hh)

    pending = []

    def flush(keep=0):
        while len(pending) > keep:
            hh = pending.pop(0)
            gather(hh, bkv[hh])

    ws2 = WStream(cx, "wkv", 3, 16 * 512)
    cnt2 = 0
    for hp in range(8):
        s_ = ws2.i % ws2.n
        ws2.i += 1
        wt = ws2.slots[s_].rearrange("p (c n) -> p c n", n=512); bw = ws2.bufs[s_]
        P.op("poolq", lambda e, wt=wt, hp=hp: e.dma_start(out=wt[:, :, 0:256], in_=wv[:, :, 2048 + hp * 256: 2048 + (hp + 1) * 256]), writes=[bw], semkey=f"wkv{s_}")
        P.op("poolq", lambda e, wt=wt, hp=hp: e.dma_start(out=wt[:, :, 256:512], in_=wv[:, :, 4096 + hp * 256: 4096 + (hp + 1) * 256]), writes=[bw], semkey=f"wkv{s_}")
        flush(keep=2)
        for hl in range(2):
            hh = 2 * hp + hl
            for which, mm in ((1, 0), (2, 1)):
                which_box[0], which_box[1], which_box[2] = which, hh, mm
                c0 = mm * 256 + hl * 128
                for n in range(nT5):
                    tn = min(512, T - n * 512)
                    bank = cnt2 % 2
                    cnt2 += 1
                    for kc in range(16):
                        P.op("pe", lambda e, bank=bank, wt=wt, kc=kc, c0=c0, n=n, tn=tn: e.matmul(
                            cx.ps[bank][:, 0:tn], lhsT=wt[:, kc, c0:c0 + 128], rhs=hnT[:, kc, n * 512: n * 512 + tn],
                            start=(kc == 0), stop=(kc == 15)), reads=[bw, bhnT], writes=[cx.psb[bank]], sig=(kc == 15))
                    consume(hh, n, tn, bank)
    which_box[0], which_box[1], which_box[2] = 0, None, None
    proj_fm(cx, hnT, bhnT, 0, T, wv, 0, 2048, ws, consume, wtile=256, after_load=flush)
    run_deferred()
    flush()
    A.reset(m0)


DILS = (1, 4, 16)


def stage_attn(cx, T, HL, qT, kv, kvg, bkvg, flag, w_o, res_in, out):
    P = cx.P
    A = cx.arena
    m0 = A.mark()
    assert T % 2048 == 0 and HL == 2048
    TK = HL + T
    attnT = A.bf16(16 * T).rearrange("p (c t) -> p c t", t=T); battn = P.buf("attnT")
    mk = A.mark()
    mf = A.f32(256); bm = P.buf("mask")
    mk_n = A.bf16(256); mk_h = A.bf16(256)
    fl = A.f32(1)
    P.op("sp", lambda e: e.dma_start(out=fl, in_=flag), writes=[bm], semkey="flag")
    P.op("pool", lambda e: e.memset(mf, 1.0), reads=[bm], writes=[bm])
    P.op("pool", lambda e: e.affine_select(out=mf[:, 0:128], in_=mf[:, 0:128], pattern=[[-1, 128]], compare_op=ALU.is_ge, fill=0.0, base=0, channel_multiplier=1),
         reads=[bm], writes=[bm])
    P.op("pool", lambda e: e.affine_select(out=mf[:, 128:256], in_=mf[:, 128:256], pattern=[[1, 128]], compare_op=ALU.is_ge, fill=0.0, base=0, channel_multiplier=-1),
         reads=[bm], writes=[bm])
    P.op("dve", lambda e: e.tensor_scalar(out=mk_n, in0=mf, scalar1=30000.0, scalar2=-30000.0, op0=ALU.mult, op1=ALU.add), reads=[bm], writes=[bm])
    P.op("dve", lambda e: e.tensor_scalar(out=mf[:, 0:128], in0=mf[:, 0:128], scalar1=fl[:, 0:1], scalar2=None, op0=ALU.mult), reads=[bm], writes=[bm])
    P.op("dve", lambda e: e.tensor_scalar(out=mk_h, in0=mf, scalar1=30000.0, scalar2=-30000.0, op0=ALU.mult, op1=ALU.add), reads=[bm], writes=[bm])
    Q = [A.bf16(T) for _ in range(2)]; bQ = [P.buf("Q0"), P.buf("Q1")]
    K = [A.bf16(TK) for _ in range(2)]; bK = [P.buf("K0"), P.buf("K1")]
    V = [A.bf16(TK) for _ in range(2)]; bV = [P.buf("V0"), P.buf("V1")]
    sqb = A.bf16(TK); bsq = P.buf("sq")
    NVB = 17 + 20 + 32
    Vord = A.bf16(NVB * 128).rearrange("p (b d) -> p b d", d=128); bVo = P.buf("Vord")
    acc_o = A.f32(T); bao = P.buf("acc_o")
    acc_d = A.f32(T); bad = P.buf("acc_d")
    rows = A.f32(TK); brow = P.buf("nrow")
    biasC = A.f32(1); bbias = P.buf("biasC")
    kmx = A.f32(2); bkmx = P.buf("kmx")
    PT = [A.bf16(256) for _ in range(4)]; bPT = [P.buf(f"PT{i}") for i in range(4)]
    bsc = [P.buf(f"sc{i}") for i in range(4)]
    uid = [0]

    def load_head(h):
        s = h % 2
        P.op("sp", lambda e: e.dma_start(out=Q[s], in_=qT[h]), writes=[bQ[s]], semkey=f"Q{s}")
        P.op("actq", lambda e: e.dma_start(out=K[s][:, 0:HL], in_=kvg[h][0:128, :]), reads=[bkvg[h]], writes=[bK[s]], semkey=f"K{s}")
        P.op("sp", lambda e: e.dma_start(out=K[s][:, HL:TK], in_=kv[h][0:128, :]), writes=[bK[s]], semkey=f"K{s}")
        P.op("actq", lambda e: e.dma_start(out=V[s][:, 0:HL], in_=kvg[h][128:256, :]), reads=[bkvg[h]], writes=[bV[s]], semkey=f"V{s}")
        P.op("sp", lambda e: e.dma_start(out=V[s][:, HL:TK], in_=kv[h][128:256, :]), writes=[bV[s]], semkey=f"V{s}")

    def key_cols(d, r, blk):
        start = HL + r + d * 128 * blk
        return slice(start, start + d * 127 + 1, d)

    load_head(0)
    for h in range(16):
        s = h % 2
        if h + 1 < 16:
            load_head(h + 1)
        Qh, Kh, Vh = Q[s], K[s], V[s]
        P.op("act", lambda e, Kh=Kh: e.activation(out=sqb, in_=Kh, func=AF.Square), reads=[bK[s]], writes=[bsq])
        for n in range(TK // 512):
            bank = n % 2
            P.op("pe", lambda e, bank=bank, n=n: e.matmul(cx.ps[bank][:, :], lhsT=cx.ones_bf[:, :], rhs=sqb[:, n * 512:(n + 1) * 512], start=True, stop=True),
                 reads=[bsq, cx.bconst], writes=[cx.psb[bank]])
            P.op("dve", lambda e, bank=bank, n=n: e.tensor_copy(out=rows[:, n * 512:(n + 1) * 512], in_=cx.ps[bank][:, :]), reads=[cx.psb[bank]], writes=[brow])
        P.op("dve", lambda e: e.tensor_reduce(out=kmx[:, 0:1], in_=rows[:, 0:TK], axis=AX.X, op=ALU.max), reads=[brow], writes=[bkmx])
        P.op("act", lambda e, Qh=Qh: e.activation(out=sqb[:, 0:T], in_=Qh, func=AF.Square), reads=[bQ[s], bsq], writes=[bsq])
        for n in range(T // 512):
            bank = n % 2
            P.op("pe", lambda e, bank=bank, n=n: e.matmul(cx.ps[bank][:, :], lhsT=cx.ones_bf[:, :], rhs=sqb[:, n * 512:(n + 1) * 512], start=True, stop=True),
                 reads=[bsq, cx.bconst], writes=[cx.psb[bank]])
            P.op("dve", lambda e, bank=bank, n=n: e.tensor_copy(out=rows[:, n * 512:(n + 1) * 512], in_=cx.ps[bank][:, :]), reads=[cx.psb[bank], brow], writes=[brow])
        P.op("dve", lambda e: e.tensor_reduce(out=kmx[:, 1:2], in_=rows[:, 0:T], axis=AX.X, op=ALU.max), reads=[brow], writes=[bkmx])
        P.op("dve", lambda e: e.tensor_scalar(out=kmx[:, 1:2], in0=kmx[:, 1:2], scalar1=kmx[:, 0:1], scalar2=1.0404, op0=ALU.mult, op1=ALU.mult), reads=[bkmx], writes=[bkmx])
        P.op("act", lambda e: e.activation(out=kmx[:, 1:2], in_=kmx[:, 1:2], func=AF.Sqrt), reads=[bkmx], writes=[bkmx])
        P.op("dve", lambda e: e.tensor_scalar(out=kmx[:, 1:2], in0=kmx[:, 1:2], scalar1=-1.0, scalar2=None, op0=ALU.mult), reads=[bkmx], writes=[bkmx])
        P.op("dve", lambda e: e.tensor_copy(out=biasC[:, 0:1], in_=kmx[:, 1:2]), reads=[bkmx], writes=[bbias])
        vblocks = {}
        lst = []
        for d in DILS:
            for r in range(d):
                for blk in range(-1, T // (128 * d)):
                    vblocks[(d, r, blk)] = len(lst)
                    lst.append((d, r, blk))
        assert len(lst) == NVB
        for g0 in range(0, NVB, 8):
            bank = 2 + (g0 // 8) % 2
            pv = cx.psbf(bank).rearrange("p (c t) -> p c t", t=128)
            ng = min(8, NVB - g0)
            for gi in range(ng):
                d, r, blk = lst[g0 + gi]
                P.op("pe", lambda e, pv=pv, gi=gi, cols=key_cols(d, r, blk), Vh=Vh: e.transpose(pv[:, gi, :], Vh[:, cols], cx.ident[:]),
                     reads=[bV[s], cx.bconst], writes=[cx.psb[bank]], sig=(gi == ng - 1))
            if (g0 // 8) % 2:
                P.op("act", lambda e, pv=pv, g0=g0, ng=ng: e.activation(out=Vord[:, g0:g0 + ng, :], in_=pv[:, 0:ng, :], func=AF.Copy), reads=[cx.psb[bank]], writes=[bVo])
            else:
                P.op("dve", lambda e, pv=pv, g0=g0, ng=ng: e.tensor_copy(out=Vord[:, g0:g0 + ng, :], in_=pv[:, 0:ng, :]), reads=[cx.psb[bank]], writes=[bVo])
        ulist = []
        for di, d in enumerate(DILS):
            nb = T // (128 * d)
            groups = []
            if d == 1:
                for b0 in range(0, nb, 4):
                    groups.append(([(0, b0 + i) for i in range(4)], lambda acc, b0=b0: acc[:, b0 * 128:(b0 + 4) * 128]))
            elif d == 4:
                for r in range(4):
                    groups.append(([(r, b) for b in range(4)], lambda acc, r=r: acc[:, r:T:4]))
            else:
                for g in range(4):
                    groups.append(([(4 * g + i, 0) for i in range(4)],
                                   lambda acc, g=g: acc.rearrange("p (m r) -> p r m", r=16)[:, 4 * g:4 * g + 4, :]))
            for gi, (units, dview) in enumerate(groups):
                for ui, (r, blk) in enumerate(units):
                    ulist.append(dict(di=di, d=d, gi=gi, ui=ui, r=r, blk=blk, dview=dview, last=(ui == len(units) - 1)))
        gcount = [0]

        def emit_scores(U):
            u = uid[0]; uid[0] += 1
            d, r, blk = U["d"], U["r"], U["blk"]
            slot = u % 4
            sb = slot; so = 0
            bss = cx.psb[slot]
            pt = PT[u % 4]; bpt = bPT[u % 4]
            U["pt"], U["bpt"] = pt, bpt
            qstart = r + d * 128 * blk
            qcols = slice(qstart, qstart + d * 127 + 1, d)
            ps_s = cx.ps[sb][:, so:so + 256]
            msk = mk_h if blk == 0 else mk_n
            P.op("pe", lambda e, sb=sb, so=so, msk=msk: e.matmul(cx.ps[sb][:, so:so + 256], lhsT=cx.ident[:], rhs=msk, start=True, stop=False),
                 reads=[bm, cx.bconst], writes=[bss], sig=False)
            for kb, kblk in enumerate((blk - 1, blk)):
                kc = key_cols(d, r, kblk)
                P.op("pe", lambda e, sb=sb, so=so, kb=kb, kc=kc, qcols=qcols, Kh=Kh, Qh=Qh: e.matmul(cx.ps[sb][:, so + kb * 128:so + (kb + 1) * 128], lhsT=Kh[:, kc], rhs=Qh[:, qcols],
                                                                                                start=False, stop=(kb == 1)),
                     reads=[bK[s], bQ[s]], writes=[bss], sig=(kb == 1))
            P.op("act", lambda e, pt=pt, ps_s=ps_s: e.activation(out=pt, in_=ps_s, func=AF.Exp, bias=biasC[:, 0:1]), reads=[bss, bbias], writes=[bpt])

        def emit_pv(U):
            d, r, blk, ui, di = U["d"], U["r"], U["blk"], U["ui"], U["di"]
            pt, bpt = U["pt"], U["bpt"]
            g = gcount[0]
            ob = 4 + g % 2
            db = 6 + g % 2
            for kb, kblk in enumerate((blk - 1, blk)):
                vb = vblocks[(d, r, kblk)]
                P.op("pe", lambda e, ob=ob, ui=ui, vb=vb, pt=pt, kb=kb: e.matmul(cx.ps[ob][:, ui * 128:(ui + 1) * 128], lhsT=Vord[:, vb, :], rhs=pt[:, kb * 128:(kb + 1) * 128],
                                                                                start=(kb == 0), stop=(kb == 1)),
                     reads=[bVo, bpt], writes=[cx.psb[ob]], sig=False)
            for kb in range(2):
                P.op("pe", lambda e, db=db, ui=ui, pt=pt, kb=kb: e.matmul(cx.ps[db][:, ui * 128:(ui + 1) * 128], lhsT=cx.ones_bf[:], rhs=pt[:, kb * 128:(kb + 1) * 128],
                                                                         start=(kb == 0), stop=(kb == 1)),
                     reads=[cx.bconst, bpt], writes=[cx.psb[db]], sig=(kb == 1))
            if U["last"]:
                gcount[0] += 1
                dview = U["dview"]
                ov = dview(acc_o); dvw = dview(acc_d)
                pso = cx.ps[ob][:] if d != 16 else cx.ps[ob][:].rearrange("p (r m) -> p r m", m=128)
                psd = cx.ps[db][:] if d != 16 else cx.ps[db][:].rearrange("p (r m) -> p r m", m=128)
                if di == 0:
                    P.op("act", lambda e, ov=ov, pso=pso: e.activation(out=ov, in_=pso, func=AF.Copy), reads=[cx.psb[ob]], writes=[bao])
                    P.op("dve", lambda e, dvw=dvw, psd=psd: e.tensor_copy(out=dvw, in_=psd), reads=[cx.psb[db]], writes=[bad])
                else:
                    P.op("dve", lambda e, ov=ov, pso=pso: e.tensor_tensor(out=ov, in0=pso, in1=ov, op=ALU.add), reads=[cx.psb[ob], bao], writes=[bao])
                    P.op("dve", lambda e, dvw=dvw, psd=psd: e.tensor_tensor(out=dvw, in0=psd, in1=dvw, op=ALU.add), reads=[cx.psb[db], bad], writes=[bad])

        SK = 3
        for i in range(len(ulist) + SK):
            if i < len(ulist):
                emit_scores(ulist[i])
            if i >= SK:
                emit_pv(ulist[i - SK])
        P.op("dve", lambda e: e.reciprocal(out=acc_d, in_=acc_d), reads=[bad], writes=[bad])
        P.op("dve", lambda e, h=h: e.tensor_tensor(out=attnT[:, h, :], in0=acc_o, in1=acc_d, op=ALU.mult), reads=[bao, bad], writes=[battn])
    A.reset(mk)
    proj_tm_residual(cx, attnT, battn, w_o, res_in, out, T)
    A.reset(m0)


def mlp_block2(cx, h_in, h_out, gain_dram, w1, w2, T, final_gain=None):
    P = cx.P
    A = cx.arena
    m0 = A.mark()
    FF = 4 * D
    TT = 1024
    NTT = TT // 128
    NPART = 4
    FCP = 64 // NPART
    gain_bc = A.f32(D); bgain = P.buf("gain")
    P.op("sp", lambda e: e.dma_start(out=gain_bc, in_=gain_dram.partition_broadcast(128)), writes=[bgain], semkey="gain")
    if final_gain is not None:
        fg_bc = A.f32(D); bfg = P.buf("fgain")
        P.op("sp", lambda e: e.dma_start(out=fg_bc, in_=final_gain.partition_broadcast(128)), writes=[bfg], semkey="fgain")
    hres = [A.f32(D) for _ in range(NTT)]
    bres = [P.buf(f"hres{i}") for i in range(NTT)]
    hn_tmp = [A.bf16(D) for _ in range(2)]
    bhn_tmp = [P.buf("hntmp0"), P.buf("hntmp1")]
    stat = A.f32(4 * NTT); bstat = P.buf("stat")
    hnT = A.bf16(16 * TT).rearrange("p (c t) -> p c t", t=TT); bhnT = P.buf("hnT")
    aT = A.bf16(FCP * TT).rearrange("p (c t) -> p c t", t=TT)
    baT = [P.buf(f"aT{i}") for i in range(FCP)]
    sq = [A.f32(512) for _ in range(2)]; bsq = [P.buf("sq0"), P.buf("sq1")]
    W1C = 256
    w1s = WStream(cx, "w1s", 2, 16 * W1C)
    W2K = 8
    w2s = WStream(cx, "w2s", 2, W2K * 512)
    w1v = w1.rearrange("(kc p) n -> p kc n", p=128)
    w2v = w2.rearrange("(fc p) n -> p fc n", p=128)
    for blk in range(T // TT):
        t0 = blk * TT
        for i in range(NTT):
            P.op(cx.hwq(), lambda e, i=i, t0=t0: e.dma_start(out=hres[i], in_=h_in[t0 + i * 128: t0 + (i + 1) * 128, :]),
                 writes=[bres[i]], semkey=f"hres{i}")
        rmsnorm_T(cx, [(hres[i], bres[i]) for i in range(NTT)], gain_bc, bgain, hnT, bhnT, NTT, hn_tmp, bhn_tmp, stat, bstat)
        ei = 0
        for part in range(NPART):
            for g in range(FCP * 128 // W1C):
                c0 = part * FCP * 128 + g * W1C
                wt, bw = w1s.load(w1v[:, :, c0:c0 + W1C], lambda s: s.rearrange("p (c n) -> p c n", n=W1C))
                for mm in range(W1C // 128):
                    ml = g * (W1C // 128) + mm
                    for n in range(TT // 512):
                        bank = ei % 4
                        for kc in range(16):
                            P.op("pe", lambda e, bank=bank, wt=wt, kc=kc, mm=mm, n=n: e.matmul(cx.ps[bank][:], lhsT=wt[:, kc, mm * 128:(mm + 1) * 128],
                                                                                               rhs=hnT[:, kc, n * 512:(n + 1) * 512], start=(kc == 0), stop=(kc == 15)),
                                 reads=[bw, bhnT], writes=[cx.psb[bank]], sig=(kc == 15))
                        s = sq[ei % 2]; bs = bsq[ei % 2]
                        P.op("act", lambda e, s=s, bank=bank: e.activation(out=s, in_=cx.ps[bank][:], func=AF.Square), reads=[cx.psb[bank]], writes=[bs])
                        P.op("dve", lambda e, s=s, bank=bank, ml=ml, n=n: e.scalar_tensor_tensor(out=aT[:, ml, n * 512:(n + 1) * 512], in0=cx.ps[bank][:], scalar=0.0, in1=s,
                                                                                                 op0=ALU.is_gt, op1=ALU.mult),
                             reads=[cx.psb[bank], bs], writes=[baT[ml]])
                        ei += 1
            for dq in range(4):
                for fh in range(FCP // W2K):
                    f0 = part * FCP + fh * W2K
                    wt, bw = w2s.load(w2v[:, f0:f0 + W2K, dq * 512:(dq + 1) * 512], lambda s: s.rearrange("p (c n) -> p c n", n=512))
                    for tt in range(NTT):
                        for fl in range(W2K):
                            fc = fh * W2K + fl
                            P.op("pe", lambda e, tt=tt, fc=fc, fl=fl, wt=wt: e.matmul(cx.ps[tt][:], lhsT=aT[:, fc, tt * 128:(tt + 1) * 128], rhs=wt[:, fl, :],
                                                                                     start=(fc == 0), stop=(fc == FCP - 1)),
                                 reads=[bw, baT[fc]], writes=[cx.psb[tt]], sig=(fc == FCP - 1))
                for tt in range(NTT):
                    dst = hres[tt][:, dq * 512:(dq + 1) * 512]
                    P.op("dve", lambda e, dst=dst, tt=tt: e.tensor_tensor(out=dst, in0=cx.ps[tt][:], in1=dst, op=ALU.add),
                         reads=[cx.psb[tt], bres[tt]], writes=[bres[tt]])
        for i in range(NTT):
            src = hres[i]
            if final_gain is not None:
                st = stat[:, 0:4]
                junk = hn_tmp[0]
                P.op("act", lambda e, src=src, junk=junk, st=st: e.activation(out=junk, in_=src, func=AF.Square, accum_out=st[:, 0:1]),
                     reads=[bres[i]], writes=[bhn_tmp[0], bstat])
                P.op("dve", lambda e, st=st: e.tensor_scalar(out=st[:, 1:2], in0=st[:, 0:1], scalar1=1.0 / D, scalar2=EPS,
                                                            op0=ALU.mult, op1=ALU.add), reads=[bstat], writes=[bstat])
                P.op("act", lambda e, st=st: e.activation(out=st[:, 2:3], in_=st[:, 1:2], func=AF.Sqrt), reads=[bstat], writes=[bstat])
                P.op("dve", lambda e, st=st: e.reciprocal(out=st[:, 3:4], in_=st[:, 2:3]), reads=[bstat], writes=[bstat])
                P.op("dve", lambda e, src=src, st=st: e.scalar_tensor_tensor(out=src, in0=src, scalar=st[:, 3:4], in1=fg_bc,
                                                                            op0=ALU.mult, op1=ALU.mult),
                     reads=[bres[i], bstat, bfg], writes=[bres[i]])
            o = P.op(cx.hwq(), lambda e, i=i, t0=t0, src=src: e.dma_start(out=h_out[t0 + i * 128: t0 + (i + 1) * 128, :], in_=src),
                     reads=[bres[i]], semkey=f"hout{i}")
            cx.out_ops.append(o)
    A.reset(m0)


import ml_dtypes
from concourse.bass_utils import run_bass_kernel_spmd

NCORES = 8
TPC = 2048
HLK = 2048
NRK = 4
I32 = mybir.dt.int32
R1_OUTS = [("a", F32), ("u", F32), ("gy", F32), ("sg", F32), ("ol", F32), ("qh", BF16)]
CW = 24 + 1024
RG = [[0, 1, 2, 3], [4, 5, 6, 7]]


def _pm(v):
    return np.ascontiguousarray(np.asarray(v).reshape(8, 128).T)


def make_vecs(conv_w, conv_b, b_a, b_i, lam, lb_logits, g_norm):
    cols = [_pm(conv_w[j]) for j in range(4)] + [_pm(conv_b), _pm(b_a), _pm(b_i), _pm(lam)] + [_pm(lb_logits[k]) for k in range(3)] + [_pm(g_norm)]
    return np.ascontiguousarray(np.concatenate(cols, axis=1).astype(np.float32))


def make_rconst():
    half = 16
    inv = (1.0 / (500000.0 ** (np.arange(half, dtype=np.float32) * np.float32(2.0 / 32)))).astype(np.float32)
    rc = np.zeros((32, 36), np.float32)
    rc[:, 0] = np.concatenate([inv, inv]); rc[:16, 1] = -1; rc[16:, 1] = 1
    for m in range(32):
        rc[(m + 16) % 32, 4 + m] = 1
    return rc


def build_fused():
    nc = bass.Bass("TRN2", target_bir_lowering=False)
    T = TPC
    I = lambda n, s, dt=F32: nc.dram_tensor(n, list(s), dt, kind="ExternalInput").ap()
    N = lambda n, s, dt=F32: nc.dram_tensor(n, list(s), dt, kind="Internal").ap()
    x = I("x", [T, D]); xh = I("xh", [128, D]); gain0 = I("gain0", [D]); w_in = I("w_in", [D, 6144]); vecs = I("vecs", [128, NVEC])
    w_a = I("w_a", [4, 256, 256]); w_i = I("w_i", [4, 256, 256]); cmask = I("cmask", [128, NRK]); w_out = I("w_out", [D, D])
    gm0 = I("gmlp0", [D]); w1_0 = I("w1_0", [D, 4 * D]); w2_0 = I("w2_0", [4 * D, D])
    g1 = I("gain1", [D]); w_qkv = I("w_qkv", [D, 6144]); pos = I("pos", [T], I32); rc = I("rconst", [32, 36])
    flag = I("flag", [128, 1]); w_o = I("w_o", [D, D])
    gm1 = I("gmlp1", [D]); w1_1 = I("w1_1", [D, 4 * D]); w2_1 = I("w2_1", [4 * D, D]); fg = I("fgain", [D])
    out = nc.dram_tensor("out", [T, D], F32, kind="ExternalOutput").ap()
    r1 = {n: N("s_" + n, [8, 128, T], dt) for n, dt in R1_OUTS}
    r1["car"] = N("s_car", [128, CW])
    car_all = N("s_car_all", [NRK * 128, CW])
    h1 = N("s_h1", [T, D]); h2 = N("s_h2", [T, D]); h3 = N("s_h3", [T, D])
    q = N("s_q", [16, 128, T], BF16); kvg = N("s_kvg", [16, NRK * 256, T], BF16)
    kv = [N(f"s_kv{h}", [256, T], BF16) for h in range(16)]
    kvgs = [N(f"s_kvgs{h}", [NRK * 256, T], BF16) for h in range(16)]
    cx = Ctx(nc)
    P = cx.P
    stage_r1(cx, T, x, xh, gain0, w_in, vecs, w_a, w_i, r1)
    P.op("poolq", lambda e: e.collective_compute("AllGather", ALU.bypass, replica_groups=RG, ins=[r1["car"]], outs=[car_all]), semkey="cc_car", inc=1)
    P.barrier()
    stage_r2(cx, T, car_all.rearrange("(r p) c -> r p c", p=128), cmask, r1, x, vecs, w_out, h1, NR=NRK)
    mlp_block2(cx, h1, h2, gm0, w1_0, w2_0, T)
    bkvg = [P.buf(f"kvg{h}") for h in range(16)]

    def gather(hh, bkv_h):
        P.op("poolq", lambda e: e.collective_compute("AllGather", ALU.bypass, replica_groups=RG, ins=[kv[hh]], outs=[kvgs[hh]]),
             reads=[bkv_h], writes=[bkvg[hh]], semkey=f"cck{hh}", inc=1)

    def after_q(hh):
        P.op("sp", lambda e: e.dma_start(out=kvg[hh], in_=kvgs[hh]), reads=[bkvg[hh]], writes=[bkvg[hh]], semkey=f"kvgc{hh % 2}")

    stage_qkv(cx, T, h2, g1, w_qkv, pos, rc, {"q": q, "kv": kv}, gather=gather, after_q=after_q)
    halo = N("s_halo", [16, 256, T], BF16)
    bhalo = [P.buf(f"halo{h}") for h in range(16)]
    kvg4 = kvg.rearrange("h (r k) t -> h r k t", r=NRK)

    def halo_copy(e):
        prev = e.snap((e.partition_id() + (NRK - 1)) % NRK, min_val=0, max_val=NRK - 1)
        return e.dma_start(out=halo.rearrange("h (o k) t -> h o k t", o=1), in_=kvg4[:, bass.ds(prev, 1), :, :])

    P.op("sp", halo_copy, reads=bkvg, writes=bhalo, semkey="halo")
    stage_attn(cx, T, HLK, q, kv, halo, bhalo, flag, w_o, h2, h3)
    mlp_block2(cx, h3, out, gm1, w1_1, w2_1, T, final_gain=fg)
    P.barrier()
    P.emit()
    P.close()
    return nc


def kernel(x, positions, norm_mix, norm_mlp, final_norm, rec_w_in, rec_conv_w, rec_conv_b, lru_w_a, lru_b_a, lru_w_i, lru_b_i,
           lru_lambda, hgrn_lb_logits, hgrn_g_norm, rec_w_out, attn_w_qkv, attn_w_o, mlp_w1, mlp_w2):
    f32 = np.float32
    x = np.asarray(x, f32); positions = np.asarray(positions, np.int32)
    A = lambda a: np.ascontiguousarray(np.asarray(a, f32))
    B, S, _ = x.shape
    T = TPC
    PPB = S // T
    assert PPB == NRK and B * PPB == NCORES
    vecs = make_vecs(A(rec_conv_w)[0], A(rec_conv_b)[0], A(lru_b_a)[0], A(lru_b_i)[0], A(lru_lambda)[0], A(hgrn_lb_logits), A(hgrn_g_norm)[0])
    rc = make_rconst()
    cores = list(range(NCORES))
    shared = {"gain0": A(norm_mix[0]), "w_in": A(rec_w_in[0]), "vecs": vecs, "w_a": A(lru_w_a[0]), "w_i": A(lru_w_i[0]), "w_out": A(rec_w_out[0]),
              "gmlp0": A(norm_mlp[0]), "w1_0": A(mlp_w1[0]), "w2_0": A(mlp_w2[0]), "gain1": A(norm_mix[1]), "w_qkv": A(attn_w_qkv[0]), "rconst": rc,
              "w_o": A(attn_w_o[0]), "gmlp1": A(norm_mlp[1]), "w1_1": A(mlp_w1[1]), "w2_1": A(mlp_w2[1]), "fgain": A(final_norm)}
    maps = []
    for c in cores:
        b, p = divmod(c, PPB)
        m = dict(shared)
        m["x"] = np.ascontiguousarray(x[b, p * T:(p + 1) * T])
        m["xh"] = np.zeros((128, D), f32) if p == 0 else np.ascontiguousarray(x[b, p * T - 128:p * T])
        cm = np.zeros((128, NRK), f32); cm[:, :p] = 1.0
        m["cmask"] = cm
        m["pos"] = np.ascontiguousarray(positions[b, p * T:(p + 1) * T])
        m["flag"] = np.full((128, 1), 0.0 if p == 0 else 1.0, f32)
        maps.append(m)
    res = run_bass_kernel_spmd(build_fused(), maps, core_ids=cores).results
    out = np.zeros((B, S, D), f32)
    for c in cores:
        b, p = divmod(c, PPB)
        out[b, p * T:(p + 1) * T] = res[c]["out"]
    return out
```

```python
import numpy as np
import concourse.bass as bass
import concourse.mybir as mybir
from contextlib import ExitStack

F32 = mybir.dt.float32
BF16 = mybir.dt.bfloat16
AF = mybir.ActivationFunctionType
ALU = mybir.AluOpType
AX = mybir.AxisListType

COMPUTE = ("pe", "act", "dve", "pool")
QUEUES = ("sp", "actq", "poolq")
STREAM_OF = {"pe": "pe", "act": "act", "dve": "dve", "pool": "pool",
             "sp": "sp", "actq": "act", "poolq": "pool"}


class Buf:
    __slots__ = ("name", "last_w", "readers")

    def __init__(self, name):
        self.name = name
        self.last_w = None
        self.readers = []


class Op:
    __slots__ = ("eng", "stream", "fn", "deps", "sig", "tick", "semkey", "idx", "is_dma", "inc")

    def __init__(self, eng, fn, sig, semkey):
        self.eng = eng
        self.stream = STREAM_OF[eng]
        self.fn = fn
        self.deps = []
        self.sig = sig
        self.tick = None
        self.semkey = semkey
        self.is_dma = eng in QUEUES
        self.inc = 16


class Prog:
    def __init__(self, nc):
        self.nc = nc
        self.ops = []
        self.streams = {"pe": [], "act": [], "dve": [], "pool": [], "sp": []}
        self.stack = ExitStack()
        self.nbuf = 0

    def sbuf(self, name, shape, dtype):
        return self.stack.enter_context(self.nc.sbuf_tensor(name, list(shape), dtype))

    def psum(self, name, shape, dtype=F32):
        return self.stack.enter_context(self.nc.psum_tensor(name, list(shape), dtype))

    def buf(self, name=None):
        self.nbuf += 1
        return Buf(name or f"b{self.nbuf}")

    def bufs(self, n, name="b"):
        return [self.buf(f"{name}{i}") for i in range(n)]

    def op(self, eng, fn, reads=(), writes=(), sig=True, semkey=None, inc=16):
        o = Op(eng, fn, sig, semkey)
        o.inc = inc
        if o.is_dma:
            assert semkey is not None
        deps = set()
        for b in reads:
            if b.last_w is not None:
                deps.add(b.last_w)
        for b in writes:
            if b.last_w is not None:
                deps.add(b.last_w)
            lastr = {}
            for r in b.readers:
                if r.is_dma:
                    deps.add(r)
                else:
                    lastr[r.eng] = r
            for r in lastr.values():
                deps.add(r)
        deps.discard(o)
        for b in reads:
            b.readers.append(o)
        for b in writes:
            b.last_w = o
            b.readers = []
        o.deps = list(deps)
        self.ops.append(o)
        self.streams[o.stream].append(o)
        return o

    def barrier(self):
        last = {}
        for o in self.ops:
            if o.fn is None:
                continue
            if o.is_dma:
                last[("dma", o.semkey)] = o
            else:
                last[("eng", o.eng)] = o
        for st in self.streams:
            o = Op(st, None, False, None)
            o.is_dma = False
            o.deps = list(last.values())
            self.ops.append(o)
            self.streams[st].append(o)

    def emit(self, final_wait_ops=()):
        nc = self.nc
        for o in self.ops:
            for d in o.deps:
                if d.stream == o.stream == "pe" and not d.is_dma:
                    continue
                if not d.is_dma:
                    d.sig = True
        counters = {}
        semnames = {}
        for o in self.ops:
            if o.fn is None:
                continue
            if o.is_dma:
                key = ("dma", o.semkey)
                counters[key] = counters.get(key, 0) + o.inc
                o.tick = (key, counters[key])
            elif o.sig:
                key = ("eng", o.eng)
                counters[key] = counters.get(key, 0) + 1
                o.tick = (key, counters[key])
        for st, lst in self.streams.items():
            nxt = {}
            for o in reversed(lst):
                if o.fn is None or o.is_dma:
                    continue
                if o.sig:
                    nxt[o.eng] = o.tick
                else:
                    o.tick = nxt.get(o.eng)
        self._nosig = True
        sems = {}
        for key in counters:
            nm = "s_" + "_".join(str(k) for k in key)
            sems[key] = self.stack.enter_context(nc.semaphore(nm))
        self.sems = sems
        maxcnt = max(counters.values()) if counters else 0
        engobj = {"pe": "tensor", "act": "scalar", "dve": "vector", "pool": "gpsimd", "sp": "sync"}
        block = self.stack.enter_context(nc.Block())

        def make_stream(st):
            lst = self.streams[st]

            def body(eng):
                waited = {}
                for o in lst:
                    need = {}
                    for d in o.deps:
                        if d.tick is None:
                            raise RuntimeError("dependency on non-signalling op")
                        k, v = d.tick
                        if d.stream == o.stream and not d.is_dma:
                            if st == "pe":
                                continue
                        if v > need.get(k, 0):
                            need[k] = v
                    for k, v in need.items():
                        if waited.get(k, 0) >= v:
                            continue
                        eng.wait_ge(sems[k], v)
                        waited[k] = v
                    if o.fn is None:
                        continue
                    ins = o.fn(eng)
                    if o.tick is not None and (o.is_dma or o.sig):
                        k, v = o.tick
                        ins.then_inc(sems[k], o.inc if o.is_dma else 1)
                if st == "sp":
                    for o in final_wait_ops:
                        k, v = o.tick
                        eng.wait_ge(sems[k], v)
            return body

        for st in ("sp", "pe", "act", "dve", "pool"):
            if not self.streams[st] and st != "sp":
                continue
            getattr(block, engobj[st])(make_stream(st))
        return maxcnt

    def close(self):
        self.stack.close()


D = 2048
EPS = 1e-6


class Arena:
    def __init__(self, P, nfloats=52500):
        self.P = P
        self.t = P.sbuf("arena", [128, nfloats], F32)
        self.n = nfloats
        self.off = 0

    def f32(self, n):
        assert self.off + n <= self.n, f"arena overflow {self.off}+{n}"
        ap = self.t[:, self.off:self.off + n]
        self.off += n
        return ap

    def bf16(self, n):
        m = (n + 1) // 2
        assert self.off + m <= self.n, f"arena overflow {self.off}+{m}"
        ap = self.t[:, self.off:self.off + m].bitcast(BF16)
        self.off += m
        return ap[:, 0:n]

    def mark(self):
        return self.off

    def reset(self, m=0):
        self.P.barrier()
        self.off = m


class Ctx:
    def __init__(self, nc):
        self.nc = nc
        self.P = Prog(nc)
        P = self.P
        self.arena = Arena(P)
        self.ps = [P.psum(f"ps{i}", [128, 512], F32) for i in range(8)]
        self.psb = [P.buf(f"psb{i}") for i in range(8)]
        self.identf = P.sbuf("identf", [128, 128], F32)
        self.ident = P.sbuf("ident", [128, 128], BF16)
        self.ones_bf = P.sbuf("ones_bf", [128, 128], BF16)
        self.ones_f = P.sbuf("ones_f", [128, 128], F32)
        self.bconst = P.buf("const")
        b = self.bconst
        P.op("pool", lambda e: e.memset(self.identf[:], 1.0), writes=[b])
        P.op("pool", lambda e: e.affine_select(out=self.identf[:], in_=self.identf[:], pattern=[[-1, 128]],
                                               compare_op=ALU.is_equal, fill=0.0, base=0, channel_multiplier=1),
             reads=[b], writes=[b])
        P.op("pool", lambda e: e.memset(self.ones_f[:], 1.0), writes=[b])
        P.op("dve", lambda e: e.tensor_copy(out=self.ident[:], in_=self.identf[:]), reads=[b], writes=[b])
        P.op("dve", lambda e: e.tensor_copy(out=self.ones_bf[:], in_=self.ones_f[:]), reads=[b], writes=[b])
        self.dma_rr = 0
        self.out_ops = []

    def psbf(self, i):
        return self.ps[i][:].bitcast(BF16)

    def hwq(self):
        self.dma_rr += 1
        return "sp" if self.dma_rr % 2 else "actq"


class WStream:
    def __init__(self, cx, name, nslots, nelem):
        self.cx = cx
        self.name = name
        self.n = nslots
        self.slots = [cx.arena.bf16(nelem) for _ in range(nslots)]
        self.bufs = [cx.P.buf(f"{name}{i}") for i in range(nslots)]
        self.i = 0

    def load(self, src_ap, view):
        cx = self.cx
        s = self.i % self.n
        self.i += 1
        dst = view(self.slots[s])
        b = self.bufs[s]
        cx.P.op("poolq", lambda e: e.dma_start(out=dst, in_=src_ap), writes=[b], semkey=f"{self.name}{s}")
        return dst, b


def rmsnorm_T(cx, src_tiles, gain_bc, bgain, hnT, bhnT, ntiles, hn_tmp, bhn_tmp, stat, bstat, col0=0):
    P = cx.P
    psi = 0
    for i in range(ntiles):
        x_ap, bx = src_tiles[i]
        tmp = hn_tmp[i % 2]
        btmp = bhn_tmp[i % 2]
        st = stat[:, 4 * i:4 * i + 4]
        P.op("act", lambda e, x_ap=x_ap, tmp=tmp, st=st: e.activation(out=tmp, in_=x_ap, func=AF.Square, accum_out=st[:, 0:1]),
             reads=[bx], writes=[btmp, bstat])
        P.op("dve", lambda e, st=st: e.tensor_scalar(out=st[:, 1:2], in0=st[:, 0:1], scalar1=1.0 / D, scalar2=EPS,
                                                    op0=ALU.mult, op1=ALU.add), reads=[bstat], writes=[bstat])
        P.op("act", lambda e, st=st: e.activation(out=st[:, 2:3], in_=st[:, 1:2], func=AF.Sqrt), reads=[bstat], writes=[bstat])
        P.op("dve", lambda e, st=st: e.reciprocal(out=st[:, 3:4], in_=st[:, 2:3]), reads=[bstat], writes=[bstat])
        P.op("dve", lambda e, x_ap=x_ap, tmp=tmp, st=st: e.scalar_tensor_tensor(out=tmp, in0=x_ap, scalar=st[:, 3:4], in1=gain_bc,
                                                                              op0=ALU.mult, op1=ALU.mult),
             reads=[bx, bstat, bgain], writes=[btmp])
        for half in range(2):
            bank = 6 + (psi % 2)
            psi += 1
            pv = cx.psbf(bank).rearrange("p (c t) -> p c t", t=128)
            for j in range(8):
                kc = half * 8 + j
                P.op("pe", lambda e, pv=pv, j=j, tmp=tmp, kc=kc: e.transpose(pv[:, j, :], tmp[:, kc * 128:(kc + 1) * 128], cx.ident[:]),
                     reads=[btmp, cx.bconst], writes=[cx.psb[bank]], sig=(j == 7))
            dst = hnT[:, half * 8:(half + 1) * 8, col0 + i * 128: col0 + (i + 1) * 128]
            if (i + half) % 2 == 0:
                P.op("act", lambda e, dst=dst, pv=pv: e.activation(out=dst, in_=pv, func=AF.Copy), reads=[cx.psb[bank]], writes=[bhnT])
            else:
                P.op("dve", lambda e, dst=dst, pv=pv: e.tensor_copy(out=dst, in_=pv), reads=[cx.psb[bank]], writes=[bhnT])


def mlp_block(cx, h_in, h_out, gain_dram, w1, w2, T, final_gain=None):
    P = cx.P
    A = cx.arena
    m0 = A.mark()
    FF = 4 * D
    TT = 512
    gain_bc = A.f32(D); bgain = P.buf("gain")
    P.op("sp", lambda e: e.dma_start(out=gain_bc, in_=gain_dram.partition_broadcast(128)), writes=[bgain], semkey="gain")
    if final_gain is not None:
        fg_bc = A.f32(D); bfg = P.buf("fgain")
        P.op("sp", lambda e: e.dma_start(out=fg_bc, in_=final_gain.partition_broadcast(128)), writes=[bfg], semkey="fgain")
    hres = [A.f32(D) for _ in range(4)]
    bres = [P.buf(f"hres{i}") for i in range(4)]
    hn_tmp = [A.bf16(D) for _ in range(2)]
    bhn_tmp = [P.buf("hntmp0"), P.buf("hntmp1")]
    stat = A.f32(16); bstat = P.buf("stat")
    hnT = A.bf16(16 * TT).rearrange("p (c t) -> p c t", t=TT); bhnT = P.buf("hnT")
    aT = A.bf16(64 * TT).rearrange("p (c t) -> p c t", t=TT)
    baT = [P.buf(f"aT{i}") for i in range(64)]
    sq = [A.f32(TT) for _ in range(2)]; bsq = [P.buf("sq0"), P.buf("sq1")]
    W1C = 256
    w1s = WStream(cx, "w1s", 2, 16 * W1C)
    W2K = 8
    w2s = WStream(cx, "w2s", 2, W2K * 512)
    w1v = w1.rearrange("(kc p) n -> p kc n", p=128)
    w2v = w2.rearrange("(fc p) n -> p fc n", p=128)
    nblk = T // TT
    for blk in range(nblk):
        t0 = blk * TT
        for i in range(4):
            P.op(cx.hwq(), lambda e, i=i, t0=t0: e.dma_start(out=hres[i], in_=h_in[t0 + i * 128: t0 + (i + 1) * 128, :]),
                 writes=[bres[i]], semkey=f"hres{i}")
        rmsnorm_T(cx, [(hres[i], bres[i]) for i in range(4)], gain_bc, bgain, hnT, bhnT, 4, hn_tmp, bhn_tmp, stat, bstat)
        ei = 0
        for g in range(FF // W1C):
            wt, bw = w1s.load(w1v[:, :, g * W1C:(g + 1) * W1C], lambda s: s.rearrange("p (c n) -> p c n", n=W1C))
            for mm in range(W1C // 128):
                m = g * (W1C // 128) + mm
                bank = ei % 4
                for kc in range(16):
                    P.op("pe", lambda e, bank=bank, wt=wt, kc=kc, mm=mm: e.matmul(cx.ps[bank][:], lhsT=wt[:, kc, mm * 128:(mm + 1) * 128],
                                                                                   rhs=hnT[:, kc, :], start=(kc == 0), stop=(kc == 15)),
                         reads=[bw, bhnT], writes=[cx.psb[bank]], sig=(kc == 15))
                s = sq[ei % 2]; bs = bsq[ei % 2]
                P.op("act", lambda e, s=s, bank=bank: e.activation(out=s, in_=cx.ps[bank][:], func=AF.Square), reads=[cx.psb[bank]], writes=[bs])
                P.op("dve", lambda e, s=s, bank=bank, m=m: e.scalar_tensor_tensor(out=aT[:, m, :], in0=cx.ps[bank][:], scalar=0.0, in1=s,
                                                                                  op0=ALU.is_gt, op1=ALU.mult),
                     reads=[cx.psb[bank], bs], writes=[baT[m]])
                ei += 1
        for dq in range(4):
            for fg in range(64 // W2K):
                wt, bw = w2s.load(w2v[:, fg * W2K:(fg + 1) * W2K, dq * 512:(dq + 1) * 512], lambda s: s.rearrange("p (c n) -> p c n", n=512))
                for tt in range(4):
                    for fl in range(W2K):
                        fc = fg * W2K + fl
                        P.op("pe", lambda e, tt=tt, fc=fc, fl=fl, wt=wt: e.matmul(cx.ps[tt][:], lhsT=aT[:, fc, tt * 128:(tt + 1) * 128],
                                                                                 rhs=wt[:, fl, :], start=(fc == 0), stop=(fc == 63)),
                             reads=[bw, baT[fc]], writes=[cx.psb[tt]], sig=(fc == 63))
            for tt in range(4):
                dst = hres[tt][:, dq * 512:(dq + 1) * 512]
                P.op("dve", lambda e, dst=dst, tt=tt: e.tensor_tensor(out=dst, in0=cx.ps[tt][:], in1=dst, op=ALU.add),
                     reads=[cx.psb[tt], bres[tt]], writes=[bres[tt]])
        for i in range(4):
            src = hres[i]
            if final_gain is not None:
                st = stat[:, 0:4]
                tmp = sq[0].bitcast(BF16)
                junk = hn_tmp[0]
                P.op("act", lambda e, src=src, junk=junk, st=st: e.activation(out=junk, in_=src, func=AF.Square, accum_out=st[:, 0:1]),
                     reads=[bres[i]], writes=[bhn_tmp[0], bstat])
                P.op("dve", lambda e, st=st: e.tensor_scalar(out=st[:, 1:2], in0=st[:, 0:1], scalar1=1.0 / D, scalar2=EPS,
                                                            op0=ALU.mult, op1=ALU.add), reads=[bstat], writes=[bstat])
                P.op("act", lambda e, st=st: e.activation(out=st[:, 2:3], in_=st[:, 1:2], func=AF.Sqrt), reads=[bstat], writes=[bstat])
                P.op("dve", lambda e, st=st: e.reciprocal(out=st[:, 3:4], in_=st[:, 2:3]), reads=[bstat], writes=[bstat])
                P.op("dve", lambda e, src=src, st=st: e.scalar_tensor_tensor(out=src, in0=src, scalar=st[:, 3:4], in1=fg_bc,
                                                                            op0=ALU.mult, op1=ALU.mult),
                     reads=[bres[i], bstat, bfg], writes=[bres[i]])
            o = P.op(cx.hwq(), lambda e, i=i, t0=t0, src=src: e.dma_start(out=h_out[t0 + i * 128: t0 + (i + 1) * 128, :], in_=src),
                     reads=[bres[i]], semkey=f"hout{i}")
            cx.out_ops.append(o)
    A.reset(m0)


NVEC = 96
V_CW, V_CB, V_BA, V_BI, V_LAM, V_LB, V_GN = 0, 32, 40, 48, 56, 64, 88
GELU_C = 0.7978845608028654


def proj_fm(cx, hnT, bhnT, col0, T, wv, wcol0, ncol, ws, consume, wtile=None, after_load=None):
    P = cx.P
    wtile = wtile or ncol
    cnt = 0
    for g in range(ncol // wtile):
        wt, bw = ws.load(wv[:, :, wcol0 + g * wtile: wcol0 + (g + 1) * wtile], lambda s: s[:, 0:16 * wtile].rearrange("p (c n) -> p c n", n=wtile))
        if after_load is not None:
            after_load()
        for mm in range(wtile // 128):
            m = g * (wtile // 128) + mm
            for n in range((T + 511) // 512):
                tn = min(512, T - n * 512)
                bank = cnt % 2
                cnt += 1
                for kc in range(16):
                    P.op("pe", lambda e, bank=bank, wt=wt, kc=kc, mm=mm, n=n, tn=tn: e.matmul(
                        cx.ps[bank][:, 0:tn], lhsT=wt[:, kc, mm * 128:(mm + 1) * 128], rhs=hnT[:, kc, col0 + n * 512: col0 + n * 512 + tn],
                        start=(kc == 0), stop=(kc == 15)), reads=[bw, bhnT], writes=[cx.psb[bank]], sig=(kc == 15))
                consume(m, n, tn, bank)


def stage_r1(cx, T, x, xh, gain, w_in, vecs, w_a, w_i, outs):
    P = cx.P
    A = cx.arena
    m0 = A.mark()
    NT = T // 128
    TC = 128 + T
    vec = A.f32(NVEC); bvec = P.buf("vec")
    P.op("sp", lambda e: e.dma_start(out=vec, in_=vecs), writes=[bvec], semkey="vec")
    gain_bc = A.f32(D); bgain = P.buf("gain")
    P.op("sp", lambda e: e.dma_start(out=gain_bc, in_=gain.partition_broadcast(128)), writes=[bgain], semkey="gain")
    der = A.f32(64); bder = P.buf("der")
    lb, oml, noml, sca = der[:, 0:8], der[:, 8:16], der[:, 16:24], der[:, 24:32]
    t0_, t1_, t2_ = der[:, 32:40], der[:, 40:48], der[:, 48:56]
    l0, l1, l2 = vec[:, V_LB:V_LB + 8], vec[:, V_LB + 8:V_LB + 16], vec[:, V_LB + 16:V_LB + 24]
    dv = lambda fn, r=(bvec, bder), w=(bder,): P.op("dve", fn, reads=list(r), writes=list(w))
    av = lambda fn, r=(bvec, bder), w=(bder,): P.op("act", fn, reads=list(r), writes=list(w))
    dv(lambda e: e.tensor_max(out=t0_, in0=l0, in1=l1))
    dv(lambda e: e.tensor_max(out=t0_, in0=t0_, in1=l2))
    dv(lambda e: e.tensor_sub(out=t1_, in0=l0, in1=t0_))
    av(lambda e: e.activation(out=lb, in_=t1_, func=AF.Exp))
    dv(lambda e: e.tensor_sub(out=t1_, in0=l1, in1=t0_))
    av(lambda e: e.activation(out=t2_, in_=t1_, func=AF.Exp))
    dv(lambda e: e.tensor_add(out=oml, in0=lb, in1=t2_))
    dv(lambda e: e.tensor_sub(out=t1_, in0=l2, in1=t0_))
    av(lambda e: e.activation(out=t2_, in_=t1_, func=AF.Exp))
    dv(lambda e: e.tensor_add(out=oml, in0=oml, in1=t2_))
    dv(lambda e: e.reciprocal(out=t2_, in_=oml))
    dv(lambda e: e.tensor_mul(out=lb, in0=lb, in1=t2_))
    dv(lambda e: e.tensor_scalar(out=oml, in0=lb, scalar1=-1.0, scalar2=1.0, op0=ALU.mult, op1=ALU.add))
    dv(lambda e: e.tensor_scalar(out=noml, in0=oml, scalar1=-1.0, scalar2=None, op0=ALU.mult))
    lam = vec[:, V_LAM:V_LAM + 8]
    dv(lambda e: e.tensor_scalar(out=t1_, in0=lam, scalar1=-1.0, scalar2=None, op0=ALU.mult))
    dv(lambda e: e.tensor_max(out=t0_, in0=lam, in1=t1_))
    av(lambda e: e.activation(out=t0_, in_=t0_, func=AF.Exp, scale=-1.0))
    dv(lambda e: e.tensor_scalar(out=t1_, in0=t0_, scalar1=2.0, scalar2=None, op0=ALU.add))
    dv(lambda e: e.reciprocal(out=t1_, in_=t1_))
    dv(lambda e: e.tensor_mul(out=t0_, in0=t0_, in1=t1_))
    dv(lambda e: e.tensor_mul(out=t1_, in0=t0_, in1=t0_))
    dv(lambda e: e.tensor_scalar(out=t2_, in0=t1_, scalar1=1.0 / 15, scalar2=1.0 / 13, op0=ALU.mult, op1=ALU.add))
    for cst in (1.0 / 11, 1.0 / 9, 1.0 / 7, 1.0 / 5, 1.0 / 3, 1.0):
        dv(lambda e: e.tensor_mul(out=t2_, in0=t2_, in1=t1_))
        dv(lambda e, cst=cst: e.tensor_scalar(out=t2_, in0=t2_, scalar1=cst, scalar2=None, op0=ALU.add))
    dv(lambda e: e.tensor_mul(out=t2_, in0=t2_, in1=t0_))
    dv(lambda e: e.tensor_scalar(out=t0_, in0=lam, scalar1=-1.0, scalar2=0.0, op0=ALU.mult, op1=ALU.max))
    dv(lambda e: e.scalar_tensor_tensor(out=sca, in0=t2_, scalar=2.0, in1=t0_, op0=ALU.mult, op1=ALU.add))
    dv(lambda e: e.tensor_scalar(out=sca, in0=sca, scalar1=-8.0, scalar2=None, op0=ALU.mult))
    if "dbg" in outs:
        P.op("sp", lambda e: e.dma_start(out=outs["dbg"][:, 0:64], in_=der), reads=[bder], semkey="dbg")
        P.op("sp", lambda e: e.dma_start(out=outs["dbg"][:, 64:64 + NVEC], in_=vec), reads=[bvec], semkey="dbg2")
    hnT = A.bf16(16 * TC).rearrange("p (c t) -> p c t", t=TC); bhnT = P.buf("hnT")
    mk1 = A.mark()
    xt = [A.f32(D) for _ in range(4)]; bxt = [P.buf(f"xt{i}") for i in range(4)]
    hn_tmp = [A.bf16(D) for _ in range(2)]; bhn_tmp = [P.buf("hntmp0"), P.buf("hntmp1")]
    stat = A.f32(4 * 4); bstat = P.buf("stat")
    for i in range(NT + 1):
        s = i % 4
        src = xh if i == 0 else x[(i - 1) * 128: i * 128, :]
        P.op(cx.hwq(), lambda e, s=s, src=src: e.dma_start(out=xt[s], in_=src), writes=[bxt[s]], semkey=f"xt{s}")
        rmsnorm_T(cx, [(xt[s], bxt[s])], gain_bc, bgain, hnT, bhnT, 1, [hn_tmp[s % 2]], [bhn_tmp[s % 2]], stat[:, 4 * s:4 * s + 4], bstat, col0=i * 128)
    A.reset(mk1)
    w_in_v = w_in.rearrange("(kc p) n -> p kc n", p=128)
    ws = WStream(cx, "wst", 2, 16 * 256)
    nT5 = (T + 511) // 512
    mk2 = A.mark()
    rows = [A.f32(T) for _ in range(2)]; brows = [P.buf("row0"), P.buf("row1")]
    tmpa = [A.f32(512) for _ in range(2)]; btmpa = [P.buf("tmpa0"), P.buf("tmpa1")]
    tmpb = [A.f32(512) for _ in range(2)]; btmpb = [P.buf("tmpb0"), P.buf("tmpb1")]
    cnt = [0]

    def consume_gy(m, n, tn, bank):
        k = cnt[0] % 2; cnt[0] += 1
        row, brow = rows[m % 2], brows[m % 2]
        ps = cx.ps[bank][:, 0:tn]; bps = cx.psb[bank]
        ta, tb = tmpa[k][:, 0:tn], tmpb[k][:, 0:tn]
        P.op("act", lambda e: e.activation(out=ta, in_=ps, func=AF.Square), reads=[bps], writes=[btmpa[k]])
        P.op("dve", lambda e: e.tensor_scalar(out=ta, in0=ta, scalar1=0.044715, scalar2=1.0, op0=ALU.mult, op1=ALU.add), reads=[btmpa[k]], writes=[btmpa[k]])
        P.op("dve", lambda e: e.tensor_tensor(out=ta, in0=ta, in1=ps, op=ALU.mult), reads=[btmpa[k], bps], writes=[btmpa[k]])
        P.op("act", lambda e: e.activation(out=tb, in_=ta, func=AF.Sigmoid, scale=2.0 * GELU_C), reads=[btmpa[k]], writes=[btmpb[k]])
        P.op("dve", lambda e: e.tensor_tensor(out=row[:, n * 512:n * 512 + tn], in0=tb, in1=ps, op=ALU.mult), reads=[btmpb[k], bps], writes=[brow])
        if n == nT5 - 1:
            P.op("sp", lambda e: e.dma_start(out=outs["gy"][m], in_=row), reads=[brow], semkey=f"orow{m % 2}")

    proj_fm(cx, hnT, bhnT, 128, T, w_in_v, 1024, 1024, ws, consume_gy, wtile=256)

    def consume_sg(m, n, tn, bank):
        row, brow = rows[m % 2], brows[m % 2]
        ps = cx.ps[bank][:, 0:tn]; bps = cx.psb[bank]
        P.op("act", lambda e: e.activation(out=row[:, n * 512:n * 512 + tn], in_=ps, func=AF.Silu), reads=[bps], writes=[brow])
        if n == nT5 - 1:
            P.op("sp", lambda e: e.dma_start(out=outs["sg"][m], in_=row), reads=[brow], semkey=f"orow{m % 2}")

    proj_fm(cx, hnT, bhnT, 128, T, w_in_v, 5120, 1024, ws, consume_sg, wtile=256)
    A.reset(mk2)
    car = A.f32(24 + 1024); bcar = P.buf("car")
    mk3 = A.mark()
    TL = T + 4
    xl = [A.f32(TL) for _ in range(2)]; bxl = [P.buf("xl0"), P.buf("xl1")]
    xc = [A.f32(T) for _ in range(2)]; bxc = [P.buf("xc0"), P.buf("xc1")]
    xcb = [A.bf16(T) for _ in range(2)]; bxcb = [P.buf("xcb0"), P.buf("xcb1")]
    arow = [A.f32(T) for _ in range(2)]; barow = [P.buf("arow0"), P.buf("arow1")]
    urow = [A.f32(T) for _ in range(2)]; burow = [P.buf("urow0"), P.buf("urow1")]
    hsc = A.f32(T); bhsc = P.buf("hsc")
    rt = [A.f32(512) for _ in range(2)]; brt = [P.buf("rt0"), P.buf("rt1")]
    it = [A.f32(512) for _ in range(2)]; bit = [P.buf("it0"), P.buf("it1")]
    tt_ = [A.f32(512) for _ in range(2)]; btt = [P.buf("tt0"), P.buf("tt1")]
    sumr = A.f32(8); bsumr = P.buf("sumr")
    wg = WStream(cx, "wg", 2, 2 * 2 * 256)
    for h in range(4):
        def consume_xl(m, n, tn, bank, h=h):
            cc = m
            P.op("act", lambda e: e.activation(out=xl[cc][:, 4 + n * 512: 4 + n * 512 + tn], in_=cx.ps[bank][:, 0:tn], func=AF.Copy),
                 reads=[cx.psb[bank]], writes=[bxl[cc]])
        wt, bw = ws.load(w_in_v[:, :, h * 256:(h + 1) * 256], lambda s: s[:, 0:16 * 256].rearrange("p (c n) -> p c n", n=256))
        for cc in range(2):
            for n in range(nT5):
                tn = min(512, T - n * 512)
                bank = (cc * nT5 + n) % 2
                for kc in range(16):
                    P.op("pe", lambda e, bank=bank, kc=kc, cc=cc, n=n, tn=tn, wt=wt: e.matmul(
                        cx.ps[bank][:, 0:tn], lhsT=wt[:, kc, cc * 128:(cc + 1) * 128], rhs=hnT[:, kc, 128 + n * 512: 128 + n * 512 + tn],
                        start=(kc == 0), stop=(kc == 15)), reads=[bw, bhnT], writes=[cx.psb[bank]], sig=(kc == 15))
                consume_xl(cc, n, tn, bank)
            bank = 2 + cc
            for kc in range(16):
                P.op("pe", lambda e, bank=bank, kc=kc, cc=cc, wt=wt: e.matmul(
                    cx.ps[bank][:, 0:4], lhsT=wt[:, kc, cc * 128:(cc + 1) * 128], rhs=hnT[:, kc, 124:128],
                    start=(kc == 0), stop=(kc == 15)), reads=[bw, bhnT], writes=[cx.psb[bank]], sig=(kc == 15))
            P.op("act", lambda e, cc=cc, bank=bank: e.activation(out=xl[cc][:, 0:4], in_=cx.ps[bank][:, 0:4], func=AF.Copy),
                 reads=[cx.psb[bank]], writes=[bxl[cc]])
            c = 2 * h + cc
            cwj = lambda j, c=c: vec[:, V_CW + 8 * j + c: V_CW + 8 * j + c + 1]
            P.op("act", lambda e, cc=cc, c=c, cwj=cwj: e.activation(out=xc[cc], in_=xl[cc][:, 1:1 + T], func=AF.Identity, scale=cwj(0),
                                                                   bias=vec[:, V_CB + c:V_CB + c + 1]), reads=[bxl[cc], bvec], writes=[bxc[cc]])
            for j in range(1, 4):
                P.op("dve", lambda e, cc=cc, j=j, cwj=cwj: e.scalar_tensor_tensor(out=xc[cc], in0=xl[cc][:, 1 + j:1 + j + T], scalar=cwj(j), in1=xc[cc],
                                                                                op0=ALU.mult, op1=ALU.add), reads=[bxl[cc], bvec, bxc[cc]], writes=[bxc[cc]])
            P.op("act", lambda e, cc=cc: e.activation(out=xcb[cc], in_=xc[cc], func=AF.Copy), reads=[bxc[cc]], writes=[bxcb[cc]])
        s = wg.i % wg.n
        wgt = wg.slots[s].rearrange("p (g i n) -> p g i n", g=2, i=2); bwg = wg.bufs[s]
        wg.i += 1
        P.op("poolq", lambda e, wgt=wgt, h=h: e.dma_start(out=wgt[:, 0], in_=w_a[h].rearrange("(i p) n -> p i n", p=128)), writes=[bwg], semkey=f"wg{s}")
        P.op("poolq", lambda e, wgt=wgt, h=h: e.dma_start(out=wgt[:, 1], in_=w_i[h].rearrange("(i p) n -> p i n", p=128)), writes=[bwg], semkey=f"wg{s}")
        for jj in range(2):
            j = 2 * h + jj
            ar, bar = arow[j % 2], barow[j % 2]
            ur, bur = urow[j % 2], burow[j % 2]
            for n in range(nT5):
                tn = min(512, T - n * 512)
                k = n % 2
                sl = slice(n * 512, n * 512 + tn)
                for g in range(2):
                    bank = 4 + g
                    for ii in range(2):
                        P.op("pe", lambda e, bank=bank, g=g, ii=ii, jj=jj, sl=sl, tn=tn, wgt=wgt: e.matmul(
                            cx.ps[bank][:, 0:tn], lhsT=wgt[:, g, ii, jj * 128:(jj + 1) * 128], rhs=xcb[ii][:, sl], start=(ii == 0), stop=(ii == 1)),
                            reads=[bwg, bxcb[ii]], writes=[cx.psb[bank]], sig=(ii == 1))
                r_, i_, t_ = rt[k][:, 0:tn], it[k][:, 0:tn], tt_[k][:, 0:tn]
                P.op("act", lambda e, r_=r_, j=j, n=n: e.activation(out=r_, in_=cx.ps[4][:, 0:r_.shape[1]], func=AF.Sigmoid, bias=vec[:, V_BA + j:V_BA + j + 1],
                                                                  accum_out=sumr[:, n:n + 1]), reads=[cx.psb[4], bvec], writes=[brt[k], bsumr])
                P.op("act", lambda e, i_=i_, j=j: e.activation(out=i_, in_=cx.ps[5][:, 0:i_.shape[1]], func=AF.Sigmoid, bias=vec[:, V_BI + j:V_BI + j + 1]),
                     reads=[cx.psb[5], bvec], writes=[bit[k]])
                P.op("act", lambda e, r_=r_, j=j, sl=sl, ar=ar: e.activation(out=ar[:, sl], in_=r_, func=AF.Exp, scale=sca[:, j:j + 1]),
                     reads=[brt[k], bder], writes=[bar])
                P.op("dve", lambda e, t_=t_, sl=sl, ar=ar: e.tensor_tensor(out=t_, in0=ar[:, sl], in1=ar[:, sl], op=ALU.mult), reads=[bar], writes=[btt[k]])
                P.op("dve", lambda e, t_=t_: e.tensor_scalar(out=t_, in0=t_, scalar1=-1.0, scalar2=1.0, op0=ALU.mult, op1=ALU.add), reads=[btt[k]], writes=[btt[k]])
                P.op("act", lambda e, t_=t_: e.activation(out=t_, in_=t_, func=AF.Sqrt), reads=[btt[k]], writes=[btt[k]])
                P.op("dve", lambda e, t_=t_, i_=i_: e.tensor_tensor(out=t_, in0=t_, in1=i_, op=ALU.mult), reads=[btt[k], bit[k]], writes=[btt[k]])
                P.op("dve", lambda e, t_=t_, sl=sl, ur=ur, jj=jj: e.tensor_tensor(out=ur[:, sl], in0=t_, in1=xc[jj][:, sl], op=ALU.mult),
                     reads=[btt[k], bxc[jj]], writes=[bur])
            P.op("sp", lambda e, j=j, ar=ar: e.dma_start(out=outs["a"][j], in_=ar), reads=[bar], semkey=f"oa{j % 2}")
            P.op("sp", lambda e, j=j, ur=ur: e.dma_start(out=outs["u"][j], in_=ur), reads=[bur], semkey=f"ou{j % 2}")
            P.op("dve", lambda e, ar=ar, ur=ur: e.tensor_tensor_scan(out=hsc, data0=ar, data1=ur, initial=0.0, op0=ALU.mult, op1=ALU.add),
                 reads=[bar, bur], writes=[bhsc])
            P.op("dve", lambda e, j=j: e.tensor_copy(out=car[:, j:j + 1], in_=hsc[:, T - 1:T]), reads=[bhsc], writes=[bcar])
            P.op("dve", lambda e, j=j: e.tensor_reduce(out=car[:, 8 + j:9 + j], in_=sumr[:, 0:nT5], axis=AX.X, op=ALU.add), reads=[bsumr], writes=[bcar])
            P.op("act", lambda e, j=j: e.activation(out=car[:, 8 + j:9 + j], in_=car[:, 8 + j:9 + j], func=AF.Exp, scale=sca[:, j:j + 1]),
                 reads=[bcar, bder], writes=[bcar])
    A.reset(mk3)
    NC_ = T // 64
    ones_row = A.bf16(T); mreset = A.bf16(T); bcm = P.buf("cmask")
    P.op("pool", lambda e: e.memset(ones_row, 1.0), writes=[bcm])
    P.op("pool", lambda e: e.memset(mreset, 1.0), writes=[bcm])
    P.op("pool", lambda e: e.memset(mreset.rearrange("p (c t) -> p c t", t=64)[:, :, 0:1], 0.0), reads=[bcm], writes=[bcm])
    mut = A.bf16(64); mutf = A.f32(64); bmut = P.buf("mut")
    P.op("pool", lambda e: e.memset(mutf[0:64, :], 1.0), writes=[bmut])
    P.op("pool", lambda e: e.affine_select(out=mutf[0:64, :], in_=mutf[0:64, :], pattern=[[1, 64]], compare_op=ALU.is_ge, fill=0.0, base=0, channel_multiplier=-1),
         reads=[bmut], writes=[bmut])
    P.op("dve", lambda e: e.tensor_copy(out=mut[0:64, :], in_=mutf[0:64, :]), reads=[bmut], writes=[bmut])
    qf = A.f32(T); bqf = P.buf("qf")
    sig = A.f32(T); bsig = P.buf("sig")
    lgf = A.f32(T); blgf = P.buf("lgf")
    bb = A.f32(T); bbb = P.buf("bb")
    Bs = A.f32(T); bBs = P.buf("Bs")
    eb = A.f32(T); beb = P.buf("eb")
    qt = A.bf16(T); bqt = P.buf("qt")
    kt = A.bf16(T); bkt = P.buf("kt")
    qh = A.bf16(T); bqh = P.buf("qh")
    qt2 = A.bf16(T); bqt2 = P.buf("qt2")
    ktok = A.bf16(NC_ * 128).rearrange("p (c k) -> p c k", k=128); bktok = P.buf("ktok")
    vtok = A.bf16(NC_ * 128).rearrange("p (c k) -> p c k", k=128); bvtok = P.buf("vtok")
    oloc = A.f32(T); boloc = P.buf("oloc")
    S = A.f32(128); bS = P.buf("S")
    Sb = A.bf16(128); bSb = P.buf("Sb")
    stmp = A.f32(128); bstmp = P.buf("stmp")
    scm = [A.bf16(64) for _ in range(2)]; bscm = [P.buf("scm0"), P.buf("scm1")]
    wq = ws
    for h in range(8):
        lbh, omlh, nomlh = lb[:, h:h + 1], oml[:, h:h + 1], noml[:, h:h + 1]

        def consume_q(m, n, tn, bank):
            P.op("act", lambda e: e.activation(out=qf[:, n * 512:n * 512 + tn], in_=cx.ps[bank][:, 0:tn], func=AF.Silu), reads=[cx.psb[bank]], writes=[bqf])

        def consume_f(m, n, tn, bank):
            P.op("act", lambda e: e.activation(out=sig[:, n * 512:n * 512 + tn], in_=cx.ps[bank][:, 0:tn], func=AF.Sigmoid), reads=[cx.psb[bank]], writes=[bsig])

        proj_fm(cx, hnT, bhnT, 128, T, w_in_v, 2048 + h * 128, 128, wq, consume_q)
        proj_fm(cx, hnT, bhnT, 128, T, w_in_v, 3072 + h * 128, 128, wq, consume_f)
        wt, bw = wq.load(w_in_v[:, :, 4096 + h * 128: 4096 + (h + 1) * 128], lambda s: s[:, 0:16 * 128].rearrange("p (c n) -> p c n", n=128))
        for c4 in range(0, NC_, 4):
            bank = (c4 // 4) % 2
            pv = cx.ps[bank][0:64, :].rearrange("p (c k) -> p c k", k=128)
            for cj in range(4):
                c = c4 + cj
                for kc in range(16):
                    P.op("pe", lambda e, pv=pv, cj=cj, c=c, kc=kc, wt=wt: e.matmul(pv[:, cj, :], lhsT=hnT[:, kc, 128 + c * 64: 128 + (c + 1) * 64], rhs=wt[:, kc, :],
                                                                                  start=(kc == 0), stop=(kc == 15)),
                         reads=[bw, bhnT], writes=[cx.psb[bank]], sig=(kc == 15 and cj == 3))
            P.op("act" if (c4 // 4) % 2 else "dve",
                 (lambda e, pv=pv, c4=c4: e.activation(out=vtok[0:64, c4:c4 + 4, :], in_=pv, func=AF.Copy)) if (c4 // 4) % 2 else
                 (lambda e, pv=pv, c4=c4: e.tensor_copy(out=vtok[0:64, c4:c4 + 4, :], in_=pv)),
                 reads=[cx.psb[bank]], writes=[bvtok])
        P.op("act", lambda e, omlh=omlh, lbh=lbh: e.activation(out=lgf, in_=sig, func=AF.Ln, scale=omlh, bias=lbh), reads=[bsig, bder], writes=[blgf])
        P.op("dve", lambda e, nomlh=nomlh, omlh=omlh: e.tensor_scalar(out=sig, in0=sig, scalar1=nomlh, scalar2=omlh, op0=ALU.mult, op1=ALU.add),
             reads=[bsig, bder], writes=[bsig])
        P.op("dve", lambda e: e.tensor_tensor_scan(out=bb, data0=mreset, data1=lgf, initial=0.0, op0=ALU.mult, op1=ALU.add), reads=[bcm, blgf], writes=[bbb])
        P.op("dve", lambda e: e.tensor_tensor_scan(out=Bs, data0=ones_row, data1=lgf, initial=0.0, op0=ALU.mult, op1=ALU.add), reads=[bcm, blgf], writes=[bBs])
        P.op("act", lambda e: e.activation(out=eb, in_=bb, func=AF.Exp), reads=[bbb], writes=[beb])
        P.op("dve", lambda e: e.tensor_scalar(out=lgf, in0=bb, scalar1=-1.0, scalar2=80.0, op0=ALU.mult, op1=ALU.min), reads=[bbb, blgf], writes=[blgf])
        P.op("act", lambda e: e.activation(out=lgf, in_=lgf, func=AF.Exp), reads=[blgf], writes=[blgf])
        P.op("dve", lambda e: e.tensor_tensor(out=kt, in0=sig, in1=lgf, op=ALU.mult), reads=[bsig, blgf], writes=[bkt])
        P.op("dve", lambda e: e.tensor_tensor(out=lgf, in0=qf, in1=eb, op=ALU.mult), reads=[bqf, beb, bkt], writes=[blgf])
        P.op("act", lambda e: e.activation(out=qt, in_=lgf, func=AF.Copy), reads=[blgf], writes=[bqt])
        P.op("pool", lambda e: e.memset(qt2[:, 0:64], 0.0), writes=[bqt2])
        P.op("dve", lambda e: e.tensor_tensor(out=qt2.rearrange("p (c t) -> p c t", t=64)[:, 1:NC_, :], in0=lgf.rearrange("p (c t) -> p c t", t=64)[:, 1:NC_, :],
                                              in1=eb.rearrange("p (c t) -> p c t", t=64)[:, 0:NC_ - 1, 63:64].to_broadcast([128, NC_ - 1, 64]), op=ALU.mult),
             reads=[blgf, beb], writes=[bqt2])
        P.op("act", lambda e: e.activation(out=Bs, in_=Bs, func=AF.Exp), reads=[bBs], writes=[bBs])
        P.op("dve", lambda e: e.tensor_tensor(out=qh, in0=qf, in1=Bs, op=ALU.mult), reads=[bqf, bBs], writes=[bqh])
        P.op("sp", lambda e, h=h: e.dma_start(out=outs["qh"][h], in_=qh), reads=[bqh], semkey="oqh")
        P.op("dve", lambda e, h=h: e.tensor_copy(out=car[:, 16 + h:17 + h], in_=Bs[:, T - 1:T]), reads=[bBs], writes=[bcar])
        for c8 in range(0, NC_, 8):
            bank = 2 + (c8 // 8) % 2
            pv = cx.psbf(bank)[0:64, :].rearrange("p (c k) -> p c k", k=128)
            for cj in range(8):
                c = c8 + cj
                P.op("pe", lambda e, pv=pv, cj=cj, c=c: e.transpose(pv[:, cj, :], kt[:, c * 64:(c + 1) * 64], cx.ident[:]),
                     reads=[bkt, cx.bconst], writes=[cx.psb[bank]], sig=(cj == 7))
            P.op("dve", lambda e, pv=pv, c8=c8: e.tensor_copy(out=ktok[0:64, c8:c8 + 8, :], in_=pv), reads=[cx.psb[bank]], writes=[bktok])
        P.op("pool", lambda e: e.memset(S, 0.0), writes=[bS])
        P.op("pool", lambda e: e.memset(Sb, 0.0), writes=[bSb])
        ebv = eb.rearrange("p (c t) -> p c t", t=64)
        for c in range(NC_):
            cs = slice(c * 64, (c + 1) * 64)
            k2 = c % 2
            sbank = 4 + k2
            P.op("pe", lambda e, sbank=sbank, cs=cs: e.matmul(cx.ps[sbank][0:64, 0:64], lhsT=kt[:, cs], rhs=qt[:, cs], start=True, stop=True),
                 reads=[bkt, bqt], writes=[cx.psb[sbank]])
            P.op("dve", lambda e, sbank=sbank, k2=k2: e.tensor_tensor(out=scm[k2][0:64, :], in0=cx.ps[sbank][0:64, 0:64], in1=mut[0:64, :], op=ALU.mult),
                 reads=[cx.psb[sbank], bmut], writes=[bscm[k2]])
            pbank = 2 + k2
            P.op("pe", lambda e, pbank=pbank, c=c: e.matmul(cx.ps[pbank][:, 0:128], lhsT=ktok[0:64, c, :], rhs=vtok[0:64, c, :], start=True, stop=True),
                 reads=[bktok, bvtok], writes=[cx.psb[pbank]])
            obank = 6 + (c // 8) % 2
            oc = slice((c % 8) * 64, (c % 8 + 1) * 64)
            P.op("pe", lambda e, obank=obank, oc=oc, cs=cs: e.matmul(cx.ps[obank][:, oc], lhsT=Sb, rhs=qt2[:, cs], start=True, stop=False),
                 reads=[bSb, bqt2], writes=[cx.psb[obank]], sig=False)
            P.op("pe", lambda e, obank=obank, oc=oc, c=c, k2=k2: e.matmul(cx.ps[obank][:, oc], lhsT=vtok[0:64, c, :], rhs=scm[k2][0:64, :], start=False, stop=True),
                 reads=[bvtok, bscm[k2]], writes=[cx.psb[obank]])
            if c % 8 == 7 or c == NC_ - 1:
                c0 = (c // 8) * 8
                w = (c - c0 + 1) * 64
                P.op("act", lambda e, obank=obank, c0=c0, w=w: e.activation(out=oloc[:, c0 * 64: c0 * 64 + w], in_=cx.ps[obank][:, 0:w], func=AF.Copy),
                     reads=[cx.psb[obank]], writes=[boloc])
            ep = ebv[:, max(c - 1, 0), 63:64]
            P.op("dve", lambda e, pbank=pbank, ep=ep: e.scalar_tensor_tensor(out=S, in0=S, scalar=ep, in1=cx.ps[pbank][:, 0:128], op0=ALU.mult, op1=ALU.add),
                 reads=[bS, beb, cx.psb[pbank]], writes=[bS])
            P.op("act", lambda e: e.activation(out=Sb, in_=S, func=AF.Copy), reads=[bS], writes=[bSb])
        P.op("dve", lambda e: e.tensor_scalar(out=S, in0=S, scalar1=ebv[:, NC_ - 1, 63:64], scalar2=None, op0=ALU.mult), reads=[bS, beb], writes=[bS])
        P.op("sp", lambda e, h=h: e.dma_start(out=outs["ol"][h], in_=oloc), reads=[boloc], semkey="ool")
        P.op("dve", lambda e, h=h: e.tensor_copy(out=car[:, 24 + h * 128: 24 + (h + 1) * 128], in_=S), reads=[bS], writes=[bcar])
    o = P.op("sp", lambda e: e.dma_start(out=outs["car"], in_=car), reads=[bcar], semkey="ocar")
    cx.out_ops.append(o)
    A.reset(m0)


def proj_tm_residual(cx, actT, bactT, w, res_in, out, T):
    P = cx.P
    A = cx.arena
    wv = w.rearrange("(kc p) n -> p kc n", p=128)
    wt = A.bf16(16 * D).rearrange("p (c n) -> p c n", n=D); bwt = [P.buf(f"wres{dq}") for dq in range(4)]
    for dq in range(4):
        P.op("poolq", lambda e, dq=dq: e.dma_start(out=wt[:, :, dq * 512:(dq + 1) * 512], in_=wv[:, :, dq * 512:(dq + 1) * 512]), writes=[bwt[dq]], semkey=f"wres{dq}")
    xt = [A.f32(D) for _ in range(2)]; bxt = [P.buf("rx0"), P.buf("rx1")]
    for i in range(T // 128):
        s = i % 2
        P.op(cx.hwq(), lambda e, s=s, i=i: e.dma_start(out=xt[s], in_=res_in[i * 128:(i + 1) * 128, :]), writes=[bxt[s]], semkey=f"rx{s}")
        for dq in range(4):
            bank = (i % 2) * 4 + dq
            for c in range(16):
                P.op("pe", lambda e, bank=bank, c=c, i=i, dq=dq: e.matmul(cx.ps[bank][:], lhsT=actT[:, c, i * 128:(i + 1) * 128], rhs=wt[:, c, dq * 512:(dq + 1) * 512],
                                                                          start=(c == 0), stop=(c == 15)), reads=[bactT, bwt[dq]], writes=[cx.psb[bank]], sig=(c == 15))
            dst = xt[s][:, dq * 512:(dq + 1) * 512]
            P.op("dve", lambda e, dst=dst, bank=bank: e.tensor_tensor(out=dst, in0=cx.ps[bank][:], in1=dst, op=ALU.add), reads=[cx.psb[bank], bxt[s]], writes=[bxt[s]])
        P.op(cx.hwq(), lambda e, s=s, i=i: e.dma_start(out=out[i * 128:(i + 1) * 128, :], in_=xt[s]), reads=[bxt[s]], semkey=f"ro{s}")


def stage_r2(cx, T, car_all, cmask, ins, x, vecs, w_out, h1_out, NR=8):
    P = cx.P
    A = cx.arena
    m0 = A.mark()
    nT5 = (T + 511) // 512
    vec = A.f32(NVEC); bvec = P.buf("vec")
    P.op("sp", lambda e: e.dma_start(out=vec, in_=vecs), writes=[bvec], semkey="vec")
    epsb = A.f32(1)
    P.op("pool", lambda e: e.memset(epsb, EPS), writes=[bvec])
    cm = A.f32(NR); bcm = P.buf("cm")
    P.op("sp", lambda e: e.dma_start(out=cm, in_=cmask), writes=[bcm], semkey="cm")
    mixT = A.bf16(16 * T).rearrange("p (c t) -> p c t", t=T); bmix = P.buf("mixT")
    hc = A.f32(8); bhc = P.buf("hc")
    Sc = A.f32(1024); bSc = P.buf("Sc")
    Scb = A.bf16(1024); bScb = P.buf("Scb")
    mk = A.mark()
    CW = 24 + 1024
    cars = A.f32(NR * CW).rearrange("p (r c) -> p r c", c=CW); bcars = P.buf("cars")
    P.op("sp", lambda e: e.dma_start(out=cars, in_=car_all.rearrange("r p c -> p r c")), writes=[bcars], semkey="cars")
    th = A.f32(8); bth = P.buf("th")
    tS = A.f32(1024); btS = P.buf("tS")
    P.op("pool", lambda e: e.memset(hc, 0.0), writes=[bhc])
    P.op("pool", lambda e: e.memset(Sc, 0.0), writes=[bSc])
    Sc3 = Sc.rearrange("p (h v) -> p h v", v=128)
    tS3 = tS.rearrange("p (h v) -> p h v", v=128)
    for r in range(NR):
        cr = cars[:, r, :]
        mr = cm[:, r:r + 1]
        P.op("dve", lambda e, cr=cr: e.tensor_tensor(out=th, in0=hc, in1=cr[:, 8:16], op=ALU.mult), reads=[bhc, bcars], writes=[bth])
        P.op("dve", lambda e, cr=cr: e.tensor_tensor(out=th, in0=th, in1=cr[:, 0:8], op=ALU.add), reads=[bth, bcars], writes=[bth])
        P.op("dve", lambda e: e.tensor_tensor(out=th, in0=th, in1=hc, op=ALU.subtract), reads=[bth, bhc], writes=[bth])
        P.op("dve", lambda e, mr=mr: e.scalar_tensor_tensor(out=hc, in0=th, scalar=mr, in1=hc, op0=ALU.mult, op1=ALU.add), reads=[bth, bhc, bcm], writes=[bhc])
        P.op("dve", lambda e, cr=cr: e.tensor_tensor(out=tS3, in0=Sc3, in1=cr[:, 16:24].unsqueeze(2).to_broadcast([128, 8, 128]), op=ALU.mult),
             reads=[bSc, bcars], writes=[btS])
        P.op("dve", lambda e, cr=cr: e.tensor_tensor(out=tS, in0=tS, in1=cr[:, 24:CW], op=ALU.add), reads=[btS, bcars], writes=[btS])
        P.op("dve", lambda e: e.tensor_tensor(out=tS, in0=tS, in1=Sc, op=ALU.subtract), reads=[btS, bSc], writes=[btS])
        P.op("dve", lambda e, mr=mr: e.scalar_tensor_tensor(out=Sc, in0=tS, scalar=mr, in1=Sc, op0=ALU.mult, op1=ALU.add), reads=[btS, bSc, bcm], writes=[bSc])
    P.op("act", lambda e: e.activation(out=Scb, in_=Sc, func=AF.Copy), reads=[bSc], writes=[bScb])
    A.reset(mk)
    ra = [A.f32(T) for _ in range(2)]; bra = [P.buf("ra0"), P.buf("ra1")]
    ru = [A.f32(T) for _ in range(2)]; bru = [P.buf("ru0"), P.buf("ru1")]
    rg = [A.f32(T) for _ in range(2)]; brg = [P.buf("rg0"), P.buf("rg1")]
    rq = [A.bf16(T) for _ in range(2)]; brq = [P.buf("rq0"), P.buf("rq1")]
    hs = A.f32(T); bhs = P.buf("hs")
    for j in range(8):
        s = j % 2
        P.op("sp", lambda e, s=s, j=j: e.dma_start(out=ra[s], in_=ins["a"][j]), writes=[bra[s]], semkey=f"ra{s}")
        P.op("actq", lambda e, s=s, j=j: e.dma_start(out=ru[s], in_=ins["u"][j]), writes=[bru[s]], semkey=f"ru{s}")
        P.op("sp", lambda e, s=s, j=j: e.dma_start(out=rg[s], in_=ins["gy"][j]), writes=[brg[s]], semkey=f"rg{s}")
        P.op("dve", lambda e, s=s, j=j: e.tensor_tensor_scan(out=hs, data0=ra[s], data1=ru[s], initial=hc[:, j:j + 1], op0=ALU.mult, op1=ALU.add),
             reads=[bra[s], bru[s], bhc], writes=[bhs])
        P.op("dve", lambda e, s=s, j=j: e.tensor_tensor(out=mixT[:, j, :], in0=hs, in1=rg[s], op=ALU.mult), reads=[bhs, brg[s]], writes=[bmix])
    osq = [A.f32(512) for _ in range(4)]; bosq = [P.buf(f"osq{i}") for i in range(4)]
    rsd = [A.f32(512) for _ in range(4)]; brsd = [P.buf(f"rsd{i}") for i in range(4)]
    assert nT5 <= 4
    for h in range(8):
        s = h % 2
        P.op("sp", lambda e, s=s, h=h: e.dma_start(out=ra[s], in_=ins["ol"][h]), writes=[bra[s]], semkey=f"ra{s}")
        P.op("actq", lambda e, s=s, h=h: e.dma_start(out=rg[s], in_=ins["sg"][h]), writes=[brg[s]], semkey=f"rg{s}")
        P.op("sp", lambda e, s=s, h=h: e.dma_start(out=rq[s], in_=ins["qh"][h]), writes=[brq[s]], semkey=f"rq{s}")
        tl = []
        for n in range(nT5):
            tn = min(512, T - n * 512)
            tl.append((n, tn, slice(n * 512, n * 512 + tn)))
        for n, tn, sl in tl:
            P.op("pe", lambda e, n=n, h=h, s=s, sl=sl, tn=tn: e.matmul(cx.ps[n][:, 0:tn], lhsT=Scb[:, h * 128:(h + 1) * 128], rhs=rq[s][:, sl], start=True, stop=True),
                 reads=[bScb, brq[s]], writes=[cx.psb[n]])
            o_ = ra[s][:, sl]
            P.op("dve", lambda e, n=n, o_=o_, tn=tn: e.tensor_tensor(out=o_, in0=cx.ps[n][:, 0:tn], in1=o_, op=ALU.add), reads=[cx.psb[n], bra[s]], writes=[bra[s]])
        for n, tn, sl in tl:
            o_ = ra[s][:, sl]; q_ = osq[n][:, 0:tn]
            P.op("act", lambda e, o_=o_, q_=q_: e.activation(out=q_, in_=o_, func=AF.Square), reads=[bra[s]], writes=[bosq[n]])
            P.op("pe", lambda e, n=n, q_=q_, tn=tn: e.matmul(cx.ps[4 + n][:, 0:tn], lhsT=cx.ones_f[:], rhs=q_, start=True, stop=True),
                 reads=[cx.bconst, bosq[n]], writes=[cx.psb[4 + n]])
        for n, tn, sl in tl:
            r_ = rsd[n][:, 0:tn]
            P.op("act", lambda e, n=n, r_=r_, tn=tn: e.activation(out=r_, in_=cx.ps[4 + n][:, 0:tn], func=AF.Sqrt, scale=1.0 / 128, bias=epsb[:, 0:1]),
                 reads=[cx.psb[4 + n], bvec], writes=[brsd[n]])
            P.op("dve", lambda e, r_=r_: e.reciprocal(out=r_, in_=r_), reads=[brsd[n]], writes=[brsd[n]])
        for n, tn, sl in tl:
            o_ = ra[s][:, sl]; r_ = rsd[n][:, 0:tn]
            P.op("pool", lambda e, r_=r_, o_=o_: e.tensor_tensor(out=r_, in0=o_, in1=r_, op=ALU.mult), reads=[brsd[n], bra[s]], writes=[brsd[n]])
            P.op("dve", lambda e, r_=r_, h=h, s=s, sl=sl: e.scalar_tensor_tensor(out=mixT[:, 8 + h, sl], in0=r_, scalar=vec[:, V_GN + h:V_GN + h + 1], in1=rg[s][:, sl],
                                                                               op0=ALU.mult, op1=ALU.mult), reads=[brsd[n], bvec, brg[s]], writes=[bmix])
    A.reset(mk)
    proj_tm_residual(cx, mixT, bmix, w_out, x, h1_out, T)
    A.reset(m0)


PI = 3.141592653589793


def stage_qkv(cx, T, h_in, gain, w_qkv, pos, rconst, outs, gather=None, after_q=None):
    P = cx.P
    A = cx.arena
    m0 = A.mark()
    nT5 = (T + 511) // 512
    gain_bc = A.f32(D); bgain = P.buf("gain")
    P.op("sp", lambda e: e.dma_start(out=gain_bc, in_=gain.partition_broadcast(128)), writes=[bgain], semkey="gain")
    rc = A.f32(36); brc = P.buf("rc")
    P.op("sp", lambda e: e.dma_start(out=rc[0:32, :], in_=rconst), writes=[brc], semkey="rc")
    cosT = A.f32(T); sinT = A.f32(T); btab = P.buf("tab")
    hnT = A.bf16(16 * T).rearrange("p (c t) -> p c t", t=T); bhnT = P.buf("hnT")
    mk1 = A.mark()
    posi = A.f32(T).bitcast(mybir.dt.int32); bpos = P.buf("pos")
    P.op("sp", lambda e: e.dma_start(out=posi[0:32, :], in_=pos.partition_broadcast(32)), writes=[bpos], semkey="pos")
    ang = A.f32(T); bang = P.buf("ang")
    tq = A.f32(T); btq = P.buf("tq")
    P.op("dve", lambda e: e.tensor_copy(out=ang[0:32, :], in_=posi[0:32, :]), reads=[bpos], writes=[bang])
    P.op("dve", lambda e: e.tensor_scalar(out=ang[0:32, :], in0=ang[0:32, :], scalar1=rc[0:32, 0:1], scalar2=None, op0=ALU.mult), reads=[bang, brc], writes=[bang])
    ti = A.f32(T).bitcast(mybir.dt.int32); bti = P.buf("ti")
    tf = A.f32(T); btf = P.buf("tf")

    def sin_table(dst, off):
        a32, q32, i32, f32_ = ang[0:32, :], tq[0:32, :], ti[0:32, :], tf[0:32, :]
        P.op("dve", lambda e: e.tensor_scalar(out=q32, in0=a32, scalar1=1.0 / (2 * PI), scalar2=off, op0=ALU.mult, op1=ALU.add), reads=[bang, btq], writes=[btq])
        P.op("dve", lambda e: e.tensor_copy(out=i32, in_=q32), reads=[btq], writes=[bti])
        P.op("dve", lambda e: e.tensor_copy(out=f32_, in_=i32), reads=[bti], writes=[btf])
        P.op("dve", lambda e: e.tensor_tensor(out=q32, in0=q32, in1=f32_, op=ALU.subtract), reads=[btq, btf], writes=[btq])
        P.op("dve", lambda e: e.tensor_scalar(out=f32_, in0=q32, scalar1=0.5, scalar2=None, op0=ALU.is_gt), reads=[btq, btf], writes=[btf])
        P.op("dve", lambda e: e.tensor_tensor(out=q32, in0=q32, in1=f32_, op=ALU.subtract), reads=[btq, btf], writes=[btq])
        P.op("dve", lambda e: e.tensor_scalar(out=f32_, in0=q32, scalar1=-0.5, scalar2=None, op0=ALU.is_lt), reads=[btq, btf], writes=[btf])
        P.op("dve", lambda e: e.tensor_tensor(out=q32, in0=q32, in1=f32_, op=ALU.add), reads=[btq, btf], writes=[btq])
        P.op("dve", lambda e: e.tensor_scalar(out=q32, in0=q32, scalar1=-0.49999, scalar2=0.49999, op0=ALU.max, op1=ALU.min), reads=[btq], writes=[btq])
        P.op("act", lambda e: e.activation(out=dst[0:32, :], in_=q32, func=AF.Sin, scale=2 * PI), reads=[btq], writes=[btab])

    sin_table(sinT, 0.0)
    P.op("dve", lambda e: e.tensor_scalar(out=sinT[0:32, :], in0=sinT[0:32, :], scalar1=rc[0:32, 1:2], scalar2=None, op0=ALU.mult), reads=[btab, brc], writes=[btab])
    sin_table(cosT, 0.25)
    xt = [A.f32(D) for _ in range(4)]; bxt = [P.buf(f"xt{i}") for i in range(4)]
    hn_tmp = [A.bf16(D) for _ in range(2)]; bhn_tmp = [P.buf("hntmp0"), P.buf("hntmp1")]
    stat = A.f32(16); bstat = P.buf("stat")
    for i in range(T // 128):
        s = i % 4
        P.op(cx.hwq(), lambda e, s=s, i=i: e.dma_start(out=xt[s], in_=h_in[i * 128:(i + 1) * 128, :]), writes=[bxt[s]], semkey=f"xt{s}")
        rmsnorm_T(cx, [(xt[s], bxt[s])], gain_bc, bgain, hnT, bhnT, 1, [hn_tmp[s % 2]], [bhn_tmp[s % 2]], stat[:, 4 * s:4 * s + 4], bstat, col0=i * 128)
    A.reset(mk1)
    wv = w_qkv.rearrange("(kc p) n -> p kc n", p=128)
    ws = WStream(cx, "wqkv", 2, 16 * 256)
    rows = [A.bf16(T) for _ in range(2)]; brows = [P.buf("orow0"), P.buf("orow1")]
    qf = [A.f32(512) for _ in range(2)]; bqf = [P.buf("qf0"), P.buf("qf1")]
    t1 = [A.f32(512) for _ in range(2)]; bt1 = [P.buf("t10"), P.buf("t11")]
    t2 = [A.f32(512) for _ in range(2)]; bt2 = [P.buf("t20"), P.buf("t21")]
    cnt = [0]
    names = ["q", "k", "v"]

    which_box = [0, None, None]
    deferred = []

    def run_deferred():
        while deferred:
            deferred.pop(0)()
    bkv = [P.buf(f"kvd{h}") for h in range(16)]

    def consume(m, n, tn, bank):
        which = which_box[0]
        hh = m % 16 if which_box[1] is None else which_box[1]
        ri = m % 2 if which_box[2] is None else which_box[2]
        row, brow = rows[ri], brows[ri]
        sl = slice(n * 512, n * 512 + tn)
        ps = cx.ps[bank][:, 0:tn]; bps = cx.psb[bank]
        if which == 2:
            run_deferred()
            P.op("act", lambda e: e.activation(out=row[:, sl], in_=ps, func=AF.Copy), reads=[bps], writes=[brow])
        else:
            k = cnt[0] % 2; cnt[0] += 1
            q_ = qf[k][:, 0:tn]
            sc = (128 ** -0.5) if which == 0 else 1.0
            run_deferred()
            P.op("act", lambda e: e.activation(out=q_, in_=ps, func=AF.Copy, scale=sc), reads=[bps], writes=[bqf[k]])
            deferred.append(lambda: rope_part(which, hh, ri, row, brow, sl, tn, n, k, q_))
            return
        finish_row(which, hh, ri, row, brow, n)

    def rope_part(which, hh, ri, row, brow, sl, tn, n, k, q_):
        if True:
            b2 = 2 + k
            P.op("pe", lambda e: e.matmul(cx.ps[b2][0:32, 0:tn], lhsT=rc[0:32, 4:36], rhs=q_[0:32, :], start=True, stop=True), reads=[brc, bqf[k]], writes=[cx.psb[b2]])
            a_ = t1[k][0:32, 0:tn]; b_ = t2[k][0:32, 0:tn]
            P.op("dve", lambda e: e.tensor_tensor(out=a_, in0=q_[0:32, :], in1=cosT[0:32, sl], op=ALU.mult), reads=[bqf[k], btab], writes=[bt1[k]])
            P.op("dve", lambda e: e.tensor_tensor(out=b_, in0=cx.ps[b2][0:32, 0:tn], in1=sinT[0:32, sl], op=ALU.mult), reads=[cx.psb[b2], btab], writes=[bt2[k]])
            P.op("act", lambda e: e.activation(out=row[:, sl], in_=q_, func=AF.Copy), reads=[bqf[k]], writes=[brow])
            P.op("dve", lambda e: e.tensor_tensor(out=row[0:32, sl], in0=a_, in1=b_, op=ALU.add), reads=[bt1[k], bt2[k]], writes=[brow])
        finish_row(which, hh, ri, row, brow, n)

    def finish_row(which, hh, ri, row, brow, n):
        if n == nT5 - 1:
            if which == 0:
                P.op("sp", lambda e: e.dma_start(out=outs["q"][hh], in_=row), reads=[brow], semkey=f"oqkv{ri}")
                if after_q is not None:
                    after_q(hh)
            else:
                r0 = 0 if which == 1 else 128
                P.op("sp", lambda e: e.dma_start(out=outs["kv"][hh][r0:r0 + 128, :], in_=row), reads=[brow], writes=[bkv[hh]], semkey=f"oqkv{ri}")
                if which == 2 and gather is not None:
                    pending.append(hh)

    pending = []

    def flush(keep=0):
        while len(pending) > keep:
            hh = pending.pop(0)
            gather(hh, bkv[hh])

    ws2 = WStream(cx, "wkv", 3, 16 * 512)
    cnt2 = 0
    for hp in range(8):
        s_ = ws2.i % ws2.n
        ws2.i += 1
        wt = ws2.slots[s_].rearrange("p (c n) -> p c n", n=512); bw = ws2.bufs[s_]
        P.op("poolq", lambda e, wt=wt, hp=hp: e.dma_start(out=wt[:, :, 0:256], in_=wv[:, :, 2048 + hp * 256: 2048 + (hp + 1) * 256]), writes=[bw], semkey=f"wkv{s_}")
        P.op("poolq", lambda e, wt=wt, hp=hp: e.dma_start(out=wt[:, :, 256:512], in_=wv[:, :, 4096 + hp * 256: 4096 + (hp + 1) * 256]), writes=[bw], semkey=f"wkv{s_}")
        flush(keep=2)
        for hl in range(2):
            hh = 2 * hp + hl
            for which, mm in ((1, 0), (2, 1)):
                which_box[0], which_box[1], which_box[2] = which, hh, mm
                c0 = mm * 256 + hl * 128
                for n in range(nT5):
                    tn = min(512, T - n * 512)
                    bank = cnt2 % 2
                    cnt2 += 1
                    for kc in range(16):
                        P.op("pe", lambda e, bank=bank, wt=wt, kc=kc, c0=c0, n=n, tn=tn: e.matmul(
                            cx.ps[bank][:, 0:tn], lhsT=wt[:, kc, c0:c0 + 128], rhs=hnT[:, kc, n * 512: n * 512 + tn],
                            start=(kc == 0), stop=(kc == 15)), reads=[bw, bhnT], writes=[cx.psb[bank]], sig=(kc == 15))
                    consume(hh, n, tn, bank)
    which_box[0], which_box[1], which_box[2] = 0, None, None
    proj_fm(cx, hnT, bhnT, 0, T, wv, 0, 2048, ws, consume, wtile=256, after_load=flush)
    run_deferred()
    flush()
    A.reset(m0)


DILS = (1, 4, 16)


def stage_attn(cx, T, HL, qT, kv, kvg, bkvg, flag, w_o, res_in, out):
    P = cx.P
    A = cx.arena
    m0 = A.mark()
    assert T % 2048 == 0 and HL == 2048
    TK = HL + T
    attnT = A.bf16(16 * T).rearrange("p (c t) -> p c t", t=T); battn = P.buf("attnT")
    mk = A.mark()
    mf = A.f32(256); bm = P.buf("mask")
    mk_n = A.bf16(256); mk_h = A.bf16(256)
    fl = A.f32(1)
    P.op("sp", lambda e: e.dma_start(out=fl, in_=flag), writes=[bm], semkey="flag")
    P.op("pool", lambda e: e.memset(mf, 1.0), reads=[bm], writes=[bm])
    P.op("pool", lambda e: e.affine_select(out=mf[:, 0:128], in_=mf[:, 0:128], pattern=[[-1, 128]], compare_op=ALU.is_ge, fill=0.0, base=0, channel_multiplier=1),
         reads=[bm], writes=[bm])
    P.op("pool", lambda e: e.affine_select(out=mf[:, 128:256], in_=mf[:, 128:256], pattern=[[1, 128]], compare_op=ALU.is_ge, fill=0.0, base=0, channel_multiplier=-1),
         reads=[bm], writes=[bm])
    P.op("dve", lambda e: e.tensor_scalar(out=mk_n, in0=mf, scalar1=30000.0, scalar2=-30000.0, op0=ALU.mult, op1=ALU.add), reads=[bm], writes=[bm])
    P.op("dve", lambda e: e.tensor_scalar(out=mf[:, 0:128], in0=mf[:, 0:128], scalar1=fl[:, 0:1], scalar2=None, op0=ALU.mult), reads=[bm], writes=[bm])
    P.op("dve", lambda e: e.tensor_scalar(out=mk_h, in0=mf, scalar1=30000.0, scalar2=-30000.0, op0=ALU.mult, op1=ALU.add), reads=[bm], writes=[bm])
    Q = [A.bf16(T) for _ in range(2)]; bQ = [P.buf("Q0"), P.buf("Q1")]
    K = [A.bf16(TK) for _ in range(2)]; bK = [P.buf("K0"), P.buf("K1")]
    V = [A.bf16(TK) for _ in range(2)]; bV = [P.buf("V0"), P.buf("V1")]
    sqb = A.bf16(TK); bsq = P.buf("sq")
    NVB = 17 + 20 + 32
    Vord = A.bf16(NVB * 128).rearrange("p (b d) -> p b d", d=128); bVo = P.buf("Vord")
    acc_o = A.f32(T); bao = P.buf("acc_o")
    acc_d = A.f32(T); bad = P.buf("acc_d")
    rows = A.f32(TK); brow = P.buf("nrow")
    biasC = A.f32(1); bbias = P.buf("biasC")
    kmx = A.f32(2); bkmx = P.buf("kmx")
    PT = [A.bf16(256) for _ in range(4)]; bPT = [P.buf(f"PT{i}") for i in range(4)]
    bsc = [P.buf(f"sc{i}") for i in range(4)]
    uid = [0]

    def load_head(h):
        s = h % 2
        P.op("sp", lambda e: e.dma_start(out=Q[s], in_=qT[h]), writes=[bQ[s]], semkey=f"Q{s}")
        P.op("actq", lambda e: e.dma_start(out=K[s][:, 0:HL], in_=kvg[h][0:128, :]), reads=[bkvg[h]], writes=[bK[s]], semkey=f"K{s}")
        P.op("sp", lambda e: e.dma_start(out=K[s][:, HL:TK], in_=kv[h][0:128, :]), writes=[bK[s]], semkey=f"K{s}")
        P.op("actq", lambda e: e.dma_start(out=V[s][:, 0:HL], in_=kvg[h][128:256, :]), reads=[bkvg[h]], writes=[bV[s]], semkey=f"V{s}")
        P.op("sp", lambda e: e.dma_start(out=V[s][:, HL:TK], in_=kv[h][128:256, :]), writes=[bV[s]], semkey=f"V{s}")

    def key_cols(d, r, blk):
        start = HL + r + d * 128 * blk
        return slice(start, start + d * 127 + 1, d)

    load_head(0)
    for h in range(16):
        s = h % 2
        if h + 1 < 16:
            load_head(h + 1)
        Qh, Kh, Vh = Q[s], K[s], V[s]
        P.op("act", lambda e, Kh=Kh: e.activation(out=sqb, in_=Kh, func=AF.Square), reads=[bK[s]], writes=[bsq])
        for n in range(TK // 512):
            bank = n % 2
            P.op("pe", lambda e, bank=bank, n=n: e.matmul(cx.ps[bank][:, :], lhsT=cx.ones_bf[:, :], rhs=sqb[:, n * 512:(n + 1) * 512], start=True, stop=True),
                 reads=[bsq, cx.bconst], writes=[cx.psb[bank]])
            P.op("dve", lambda e, bank=bank, n=n: e.tensor_copy(out=rows[:, n * 512:(n + 1) * 512], in_=cx.ps[bank][:, :]), reads=[cx.psb[bank]], writes=[brow])
        P.op("dve", lambda e: e.tensor_reduce(out=kmx[:, 0:1], in_=rows[:, 0:TK], axis=AX.X, op=ALU.max), reads=[brow], writes=[bkmx])
        P.op("act", lambda e, Qh=Qh: e.activation(out=sqb[:, 0:T], in_=Qh, func=AF.Square), reads=[bQ[s], bsq], writes=[bsq])
        for n in range(T // 512):
            bank = n % 2
            P.op("pe", lambda e, bank=bank, n=n: e.matmul(cx.ps[bank][:, :], lhsT=cx.ones_bf[:, :], rhs=sqb[:, n * 512:(n + 1) * 512], start=True, stop=True),
                 reads=[bsq, cx.bconst], writes=[cx.psb[bank]])
            P.op("dve", lambda e, bank=bank, n=n: e.tensor_copy(out=rows[:, n * 512:(n + 1) * 512], in_=cx.ps[bank][:, :]), reads=[cx.psb[bank], brow], writes=[brow])
        P.op("dve", lambda e: e.tensor_reduce(out=kmx[:, 1:2], in_=rows[:, 0:T], axis=AX.X, op=ALU.max), reads=[brow], writes=[bkmx])
        P.op("dve", lambda e: e.tensor_scalar(out=kmx[:, 1:2], in0=kmx[:, 1:2], scalar1=kmx[:, 0:1], scalar2=1.0404, op0=ALU.mult, op1=ALU.mult), reads=[bkmx], writes=[bkmx])
        P.op("act", lambda e: e.activation(out=kmx[:, 1:2], in_=kmx[:, 1:2], func=AF.Sqrt), reads=[bkmx], writes=[bkmx])
        P.op("dve", lambda e: e.tensor_scalar(out=kmx[:, 1:2], in0=kmx[:, 1:2], scalar1=-1.0, scalar2=None, op0=ALU.mult), reads=[bkmx], writes=[bkmx])
        P.op("dve", lambda e: e.tensor_copy(out=biasC[:, 0:1], in_=kmx[:, 1:2]), reads=[bkmx], writes=[bbias])
        vblocks = {}
        lst = []
        for d in DILS:
            for r in range(d):
                for blk in range(-1, T // (128 * d)):
                    vblocks[(d, r, blk)] = len(lst)
                    lst.append((d, r, blk))
        assert len(lst) == NVB
        for g0 in range(0, NVB, 8):
            bank = 2 + (g0 // 8) % 2
            pv = cx.psbf(bank).rearrange("p (c t) -> p c t", t=128)
            ng = min(8, NVB - g0)
            for gi in range(ng):
                d, r, blk = lst[g0 + gi]
                P.op("pe", lambda e, pv=pv, gi=gi, cols=key_cols(d, r, blk), Vh=Vh: e.transpose(pv[:, gi, :], Vh[:, cols], cx.ident[:]),
                     reads=[bV[s], cx.bconst], writes=[cx.psb[bank]], sig=(gi == ng - 1))
            if (g0 // 8) % 2:
                P.op("act", lambda e, pv=pv, g0=g0, ng=ng: e.activation(out=Vord[:, g0:g0 + ng, :], in_=pv[:, 0:ng, :], func=AF.Copy), reads=[cx.psb[bank]], writes=[bVo])
            else:
                P.op("dve", lambda e, pv=pv, g0=g0, ng=ng: e.tensor_copy(out=Vord[:, g0:g0 + ng, :], in_=pv[:, 0:ng, :]), reads=[cx.psb[bank]], writes=[bVo])
        ulist = []
        for di, d in enumerate(DILS):
            nb = T // (128 * d)
            groups = []
            if d == 1:
                for b0 in range(0, nb, 4):
                    groups.append(([(0, b0 + i) for i in range(4)], lambda acc, b0=b0: acc[:, b0 * 128:(b0 + 4) * 128]))
            elif d == 4:
                for r in range(4):
                    groups.append(([(r, b) for b in range(4)], lambda acc, r=r: acc[:, r:T:4]))
            else:
                for g in range(4):
                    groups.append(([(4 * g + i, 0) for i in range(4)],
                                   lambda acc, g=g: acc.rearrange("p (m r) -> p r m", r=16)[:, 4 * g:4 * g + 4, :]))
            for gi, (units, dview) in enumerate(groups):
                for ui, (r, blk) in enumerate(units):
                    ulist.append(dict(di=di, d=d, gi=gi, ui=ui, r=r, blk=blk, dview=dview, last=(ui == len(units) - 1)))
        gcount = [0]

        def emit_scores(U):
            u = uid[0]; uid[0] += 1
            d, r, blk = U["d"], U["r"], U["blk"]
            slot = u % 4
            sb = slot; so = 0
            bss = cx.psb[slot]
            pt = PT[u % 4]; bpt = bPT[u % 4]
            U["pt"], U["bpt"] = pt, bpt
            qstart = r + d * 128 * blk
            qcols = slice(qstart, qstart + d * 127 + 1, d)
            ps_s = cx.ps[sb][:, so:so + 256]
            msk = mk_h if blk == 0 else mk_n
            P.op("pe", lambda e, sb=sb, so=so, msk=msk: e.matmul(cx.ps[sb][:, so:so + 256], lhsT=cx.ident[:], rhs=msk, start=True, stop=False),
                 reads=[bm, cx.bconst], writes=[bss], sig=False)
            for kb, kblk in enumerate((blk - 1, blk)):
                kc = key_cols(d, r, kblk)
                P.op("pe", lambda e, sb=sb, so=so, kb=kb, kc=kc, qcols=qcols, Kh=Kh, Qh=Qh: e.matmul(cx.ps[sb][:, so + kb * 128:so + (kb + 1) * 128], lhsT=Kh[:, kc], rhs=Qh[:, qcols],
                                                                                                start=False, stop=(kb == 1)),
                     reads=[bK[s], bQ[s]], writes=[bss], sig=(kb == 1))
            P.op("act", lambda e, pt=pt, ps_s=ps_s: e.activation(out=pt, in_=ps_s, func=AF.Exp, bias=biasC[:, 0:1]), reads=[bss, bbias], writes=[bpt])

        def emit_pv(U):
            d, r, blk, ui, di = U["d"], U["r"], U["blk"], U["ui"], U["di"]
            pt, bpt = U["pt"], U["bpt"]
            g = gcount[0]
            ob = 4 + g % 2
            db = 6 + g % 2
            for kb, kblk in enumerate((blk - 1, blk)):
                vb = vblocks[(d, r, kblk)]
                P.op("pe", lambda e, ob=ob, ui=ui, vb=vb, pt=pt, kb=kb: e.matmul(cx.ps[ob][:, ui * 128:(ui + 1) * 128], lhsT=Vord[:, vb, :], rhs=pt[:, kb * 128:(kb + 1) * 128],
                                                                                start=(kb == 0), stop=(kb == 1)),
                     reads=[bVo, bpt], writes=[cx.psb[ob]], sig=False)
            for kb in range(2):
                P.op("pe", lambda e, db=db, ui=ui, pt=pt, kb=kb: e.matmul(cx.ps[db][:, ui * 128:(ui + 1) * 128], lhsT=cx.ones_bf[:], rhs=pt[:, kb * 128:(kb + 1) * 128],
                                                                         start=(kb == 0), stop=(kb == 1)),
                     reads=[cx.bconst, bpt], writes=[cx.psb[db]], sig=(kb == 1))
            if U["last"]:
                gcount[0] += 1
                dview = U["dview"]
                ov = dview(acc_o); dvw = dview(acc_d)
                pso = cx.ps[ob][:] if d != 16 else cx.ps[ob][:].rearrange("p (r m) -> p r m", m=128)
                psd = cx.ps[db][:] if d != 16 else cx.ps[db][:].rearrange("p (r m) -> p r m", m=128)
                if di == 0:
                    P.op("act", lambda e, ov=ov, pso=pso: e.activation(out=ov, in_=pso, func=AF.Copy), reads=[cx.psb[ob]], writes=[bao])
                    P.op("dve", lambda e, dvw=dvw, psd=psd: e.tensor_copy(out=dvw, in_=psd), reads=[cx.psb[db]], writes=[bad])
                else:
                    P.op("dve", lambda e, ov=ov, pso=pso: e.tensor_tensor(out=ov, in0=pso, in1=ov, op=ALU.add), reads=[cx.psb[ob], bao], writes=[bao])
                    P.op("dve", lambda e, dvw=dvw, psd=psd: e.tensor_tensor(out=dvw, in0=psd, in1=dvw, op=ALU.add), reads=[cx.psb[db], bad], writes=[bad])

        SK = 2
        for i in range(len(ulist) + SK):
            if i < len(ulist):
                emit_scores(ulist[i])
            if i >= SK:
                emit_pv(ulist[i - SK])
        P.op("dve", lambda e: e.reciprocal(out=acc_d, in_=acc_d), reads=[bad], writes=[bad])
        P.op("dve", lambda e, h=h: e.tensor_tensor(out=attnT[:, h, :], in0=acc_o, in1=acc_d, op=ALU.mult), reads=[bao, bad], writes=[battn])
    A.reset(mk)
    proj_tm_residual(cx, attnT, battn, w_o, res_in, out, T)
    A.reset(m0)


def mlp_block2(cx, h_in, h_out, gain_dram, w1, w2, T, final_gain=None):
    P = cx.P
    A = cx.arena
    m0 = A.mark()
    FF = 4 * D
    TT = 1024
    NTT = TT // 128
    NPART = 4
    FCP = 64 // NPART
    gain_bc = A.f32(D); bgain = P.buf("gain")
    P.op("sp", lambda e: e.dma_start(out=gain_bc, in_=gain_dram.partition_broadcast(128)), writes=[bgain], semkey="gain")
    if final_gain is not None:
        fg_bc = A.f32(D); bfg = P.buf("fgain")
        P.op("sp", lambda e: e.dma_start(out=fg_bc, in_=final_gain.partition_broadcast(128)), writes=[bfg], semkey="fgain")
    hres = [A.f32(D) for _ in range(NTT)]
    bres = [P.buf(f"hres{i}") for i in range(NTT)]
    hn_tmp = [A.bf16(D) for _ in range(2)]
    bhn_tmp = [P.buf("hntmp0"), P.buf("hntmp1")]
    stat = A.f32(4 * NTT); bstat = P.buf("stat")
    hnT = A.bf16(16 * TT).rearrange("p (c t) -> p c t", t=TT); bhnT = P.buf("hnT")
    aT = A.bf16(FCP * TT).rearrange("p (c t) -> p c t", t=TT)
    baT = [P.buf(f"aT{i}") for i in range(FCP)]
    sq = [A.f32(512) for _ in range(2)]; bsq = [P.buf("sq0"), P.buf("sq1")]
    W1C = 256
    w1s = WStream(cx, "w1s", 2, 16 * W1C)
    W2K = 8
    w2s = WStream(cx, "w2s", 2, W2K * 512)
    w1v = w1.rearrange("(kc p) n -> p kc n", p=128)
    w2v = w2.rearrange("(fc p) n -> p fc n", p=128)
    for blk in range(T // TT):
        t0 = blk * TT
        for i in range(NTT):
            P.op(cx.hwq(), lambda e, i=i, t0=t0: e.dma_start(out=hres[i], in_=h_in[t0 + i * 128: t0 + (i + 1) * 128, :]),
                 writes=[bres[i]], semkey=f"hres{i}")
        rmsnorm_T(cx, [(hres[i], bres[i]) for i in range(NTT)], gain_bc, bgain, hnT, bhnT, NTT, hn_tmp, bhn_tmp, stat, bstat)
        ei = 0
        for part in range(NPART):
            for g in range(FCP * 128 // W1C):
                c0 = part * FCP * 128 + g * W1C
                wt, bw = w1s.load(w1v[:, :, c0:c0 + W1C], lambda s: s.rearrange("p (c n) -> p c n", n=W1C))
                for mm in range(W1C // 128):
                    ml = g * (W1C // 128) + mm
                    for n in range(TT // 512):
                        bank = ei % 4
                        for kc in range(16):
                            P.op("pe", lambda e, bank=bank, wt=wt, kc=kc, mm=mm, n=n: e.matmul(cx.ps[bank][:], lhsT=wt[:, kc, mm * 128:(mm + 1) * 128],
                                                                                               rhs=hnT[:, kc, n * 512:(n + 1) * 512], start=(kc == 0), stop=(kc == 15)),
                                 reads=[bw, bhnT], writes=[cx.psb[bank]], sig=(kc == 15))
                        s = sq[ei % 2]; bs = bsq[ei % 2]
                        P.op("act", lambda e, s=s, bank=bank: e.activation(out=s, in_=cx.ps[bank][:], func=AF.Square), reads=[cx.psb[bank]], writes=[bs])
                        P.op("dve", lambda e, s=s, bank=bank, ml=ml, n=n: e.scalar_tensor_tensor(out=aT[:, ml, n * 512:(n + 1) * 512], in0=cx.ps[bank][:], scalar=0.0, in1=s,
                                                                                                 op0=ALU.is_gt, op1=ALU.mult),
                             reads=[cx.psb[bank], bs], writes=[baT[ml]])
                        ei += 1
            for dq in range(4):
                for fh in range(FCP // W2K):
                    f0 = part * FCP + fh * W2K
                    wt, bw = w2s.load(w2v[:, f0:f0 + W2K, dq * 512:(dq + 1) * 512], lambda s: s.rearrange("p (c n) -> p c n", n=512))
                    for tt in range(NTT):
                        for fl in range(W2K):
                            fc = fh * W2K + fl
                            P.op("pe", lambda e, tt=tt, fc=fc, fl=fl, wt=wt: e.matmul(cx.ps[tt][:], lhsT=aT[:, fc, tt * 128:(tt + 1) * 128], rhs=wt[:, fl, :],
                                                                                     start=(fc == 0), stop=(fc == FCP - 1)),
                                 reads=[bw, baT[fc]], writes=[cx.psb[tt]], sig=(fc == FCP - 1))
                for tt in range(NTT):
                    dst = hres[tt][:, dq * 512:(dq + 1) * 512]
                    P.op("dve", lambda e, dst=dst, tt=tt: e.tensor_tensor(out=dst, in0=cx.ps[tt][:], in1=dst, op=ALU.add),
                         reads=[cx.psb[tt], bres[tt]], writes=[bres[tt]])
        for i in range(NTT):
            src = hres[i]
            if final_gain is not None:
                st = stat[:, 0:4]
                junk = hn_tmp[0]
                P.op("act", lambda e, src=src, junk=junk, st=st: e.activation(out=junk, in_=src, func=AF.Square, accum_out=st[:, 0:1]),
                     reads=[bres[i]], writes=[bhn_tmp[0], bstat])
                P.op("dve", lambda e, st=st: e.tensor_scalar(out=st[:, 1:2], in0=st[:, 0:1], scalar1=1.0 / D, scalar2=EPS,
                                                            op0=ALU.mult, op1=ALU.add), reads=[bstat], writes=[bstat])
                P.op("act", lambda e, st=st: e.activation(out=st[:, 2:3], in_=st[:, 1:2], func=AF.Sqrt), reads=[bstat], writes=[bstat])
                P.op("dve", lambda e, st=st: e.reciprocal(out=st[:, 3:4], in_=st[:, 2:3]), reads=[bstat], writes=[bstat])
                P.op("dve", lambda e, src=src, st=st: e.scalar_tensor_tensor(out=src, in0=src, scalar=st[:, 3:4], in1=fg_bc,
                                                                            op0=ALU.mult, op1=ALU.mult),
                     reads=[bres[i], bstat, bfg], writes=[bres[i]])
            o = P.op(cx.hwq(), lambda e, i=i, t0=t0, src=src: e.dma_start(out=h_out[t0 + i * 128: t0 + (i + 1) * 128, :], in_=src),
                     reads=[bres[i]], semkey=f"hout{i}")
            cx.out_ops.append(o)
    A.reset(m0)


import ml_dtypes
from concourse.bass_utils import run_bass_kernel_spmd

NCORES = 8
TPC = 2048
HLK = 2048
NRK = 4
I32 = mybir.dt.int32
R1_OUTS = [("a", F32), ("u", F32), ("gy", F32), ("sg", F32), ("ol", F32), ("qh", BF16)]
CW = 24 + 1024
RG = [[0, 1, 2, 3], [4, 5, 6, 7]]


def _pm(v):
    return np.ascontiguousarray(np.asarray(v).reshape(8, 128).T)


def make_vecs(conv_w, conv_b, b_a, b_i, lam, lb_logits, g_norm):
    cols = [_pm(conv_w[j]) for j in range(4)] + [_pm(conv_b), _pm(b_a), _pm(b_i), _pm(lam)] + [_pm(lb_logits[k]) for k in range(3)] + [_pm(g_norm)]
    return np.ascontiguousarray(np.concatenate(cols, axis=1).astype(np.float32))


def make_rconst():
    half = 16
    inv = (1.0 / (500000.0 ** (np.arange(half, dtype=np.float32) * np.float32(2.0 / 32)))).astype(np.float32)
    rc = np.zeros((32, 36), np.float32)
    rc[:, 0] = np.concatenate([inv, inv]); rc[:16, 1] = -1; rc[16:, 1] = 1
    for m in range(32):
        rc[(m + 16) % 32, 4 + m] = 1
    return rc


def build_fused():
    nc = bass.Bass("TRN2", target_bir_lowering=False)
    T = TPC
    I = lambda n, s, dt=F32: nc.dram_tensor(n, list(s), dt, kind="ExternalInput").ap()
    N = lambda n, s, dt=F32: nc.dram_tensor(n, list(s), dt, kind="Internal").ap()
    x = I("x", [T, D]); xh = I("xh", [128, D]); gain0 = I("gain0", [D]); w_in = I("w_in", [D, 6144]); vecs = I("vecs", [128, NVEC])
    w_a = I("w_a", [4, 256, 256]); w_i = I("w_i", [4, 256, 256]); cmask = I("cmask", [128, NRK]); w_out = I("w_out", [D, D])
    gm0 = I("gmlp0", [D]); w1_0 = I("w1_0", [D, 4 * D]); w2_0 = I("w2_0", [4 * D, D])
    g1 = I("gain1", [D]); w_qkv = I("w_qkv", [D, 6144]); pos = I("pos", [T], I32); rc = I("rconst", [32, 36])
    flag = I("flag", [128, 1]); w_o = I("w_o", [D, D])
    gm1 = I("gmlp1", [D]); w1_1 = I("w1_1", [D, 4 * D]); w2_1 = I("w2_1", [4 * D, D]); fg = I("fgain", [D])
    out = nc.dram_tensor("out", [T, D], F32, kind="ExternalOutput").ap()
    r1 = {n: N("s_" + n, [8, 128, T], dt) for n, dt in R1_OUTS}
    r1["car"] = N("s_car", [128, CW])
    car_all = N("s_car_all", [NRK * 128, CW])
    h1 = N("s_h1", [T, D]); h2 = N("s_h2", [T, D]); h3 = N("s_h3", [T, D])
    q = N("s_q", [16, 128, T], BF16); kvg = N("s_kvg", [16, NRK * 256, T], BF16)
    kv = [N(f"s_kv{h}", [256, T], BF16) for h in range(16)]
    kvgs = [N(f"s_kvgs{h}", [NRK * 256, T], BF16) for h in range(16)]
    cx = Ctx(nc)
    P = cx.P
    stage_r1(cx, T, x, xh, gain0, w_in, vecs, w_a, w_i, r1)
    P.op("poolq", lambda e: e.collective_compute("AllGather", ALU.bypass, replica_groups=RG, ins=[r1["car"]], outs=[car_all]), semkey="cc_car", inc=1)
    P.barrier()
    stage_r2(cx, T, car_all.rearrange("(r p) c -> r p c", p=128), cmask, r1, x, vecs, w_out, h1, NR=NRK)
    mlp_block2(cx, h1, h2, gm0, w1_0, w2_0, T)
    bkvg = [P.buf(f"kvg{h}") for h in range(16)]

    def gather(hh, bkv_h):
        P.op("poolq", lambda e: e.collective_compute("AllGather", ALU.bypass, replica_groups=RG, ins=[kv[hh]], outs=[kvgs[hh]]),
             reads=[bkv_h], writes=[bkvg[hh]], semkey=f"cck{hh}", inc=1)

    def after_q(hh):
        P.op("sp", lambda e: e.dma_start(out=kvg[hh], in_=kvgs[hh]), reads=[bkvg[hh]], writes=[bkvg[hh]], semkey=f"kvgc{hh % 2}")

    stage_qkv(cx, T, h2, g1, w_qkv, pos, rc, {"q": q, "kv": kv}, gather=gather, after_q=after_q)
    halo = N("s_halo", [16, 256, T], BF16)
    bhalo = [P.buf(f"halo{h}") for h in range(16)]
    kvg4 = kvg.rearrange("h (r k) t -> h r k t", r=NRK)

    def halo_copy(e):
        prev = e.snap((e.partition_id() + (NRK - 1)) % NRK, min_val=0, max_val=NRK - 1)
        return e.dma_start(out=halo.rearrange("h (o k) t -> h o k t", o=1), in_=kvg4[:, bass.ds(prev, 1), :, :])

    P.op("sp", halo_copy, reads=bkvg, writes=bhalo, semkey="halo")
    stage_attn(cx, T, HLK, q, kv, halo, bhalo, flag, w_o, h2, h3)
    mlp_block2(cx, h3, out, gm1, w1_1, w2_1, T, final_gain=fg)
    P.barrier()
    P.emit()
    P.close()
    return nc


def kernel(x, positions, norm_mix, norm_mlp, final_norm, rec_w_in, rec_conv_w, rec_conv_b, lru_w_a, lru_b_a, lru_w_i, lru_b_i,
           lru_lambda, hgrn_lb_logits, hgrn_g_norm, rec_w_out, attn_w_qkv, attn_w_o, mlp_w1, mlp_w2):
    f32 = np.float32
    x = np.asarray(x, f32); positions = np.asarray(positions, np.int32)
    A = lambda a: np.ascontiguousarray(np.asarray(a, f32))
    B, S, _ = x.shape
    T = TPC
    PPB = S // T
    assert PPB == NRK and B * PPB == NCORES
    vecs = make_vecs(A(rec_conv_w)[0], A(rec_conv_b)[0], A(lru_b_a)[0], A(lru_b_i)[0], A(lru_lambda)[0], A(hgrn_lb_logits), A(hgrn_g_norm)[0])
    rc = make_rconst()
    cores = list(range(NCORES))
    shared = {"gain0": A(norm_mix[0]), "w_in": A(rec_w_in[0]), "vecs": vecs, "w_a": A(lru_w_a[0]), "w_i": A(lru_w_i[0]), "w_out": A(rec_w_out[0]),
              "gmlp0": A(norm_mlp[0]), "w1_0": A(mlp_w1[0]), "w2_0": A(mlp_w2[0]), "gain1": A(norm_mix[1]), "w_qkv": A(attn_w_qkv[0]), "rconst": rc,
              "w_o": A(attn_w_o[0]), "gmlp1": A(norm_mlp[1]), "w1_1": A(mlp_w1[1]), "w2_1": A(mlp_w2[1]), "fgain": A(final_norm)}
    maps = []
    for c in cores:
        b, p = divmod(c, PPB)
        m = dict(shared)
        m["x"] = np.ascontiguousarray(x[b, p * T:(p + 1) * T])
        m["xh"] = np.zeros((128, D), f32) if p == 0 else np.ascontiguousarray(x[b, p * T - 128:p * T])
        cm = np.zeros((128, NRK), f32); cm[:, :p] = 1.0
        m["cmask"] = cm
        m["pos"] = np.ascontiguousarray(positions[b, p * T:(p + 1) * T])
        m["flag"] = np.full((128, 1), 0.0 if p == 0 else 1.0, f32)
        maps.append(m)
    res = run_bass_kernel_spmd(build_fused(), maps, core_ids=cores).results
    out = np.zeros((B, S, D), f32)
    for c in cores:
        b, p = divmod(c, PPB)
        out[b, p * T:(p + 1) * T] = res[c]["out"]
    return out
```

```python
import numpy as np
import concourse.bass as bass
import concourse.mybir as mybir
from contextlib import ExitStack

F32 = mybir.dt.float32
BF16 = mybir.dt.bfloat16
AF = mybir.ActivationFunctionType
ALU = mybir.AluOpType
AX = mybir.AxisListType

COMPUTE = ("pe", "act", "dve", "pool")
QUEUES = ("sp", "actq", "poolq")
STREAM_OF = {"pe": "pe", "act": "act", "dve": "dve", "pool": "pool",
             "sp": "sp", "actq": "act", "poolq": "pool"}


class Buf:
    __slots__ = ("name", "last_w", "readers")

    def __init__(self, name):
        self.name = name
        self.last_w = None
        self.readers = []


class Op:
    __slots__ = ("eng", "stream", "fn", "deps", "sig", "tick", "semkey", "idx", "is_dma", "inc")

    def __init__(self, eng, fn, sig, semkey):
        self.eng = eng
        self.stream = STREAM_OF[eng]
        self.fn = fn
        self.deps = []
        self.sig = sig
        self.tick = None
        self.semkey = semkey
        self.is_dma = eng in QUEUES
        self.inc = 16


class Prog:
    def __init__(self, nc):
        self.nc = nc
        self.ops = []
        self.streams = {"pe": [], "act": [], "dve": [], "pool": [], "sp": []}
        self.stack = ExitStack()
        self.nbuf = 0

    def sbuf(self, name, shape, dtype):
        return self.stack.enter_context(self.nc.sbuf_tensor(name, list(shape), dtype))

    def psum(self, name, shape, dtype=F32):
        return self.stack.enter_context(self.nc.psum_tensor(name, list(shape), dtype))

    def buf(self, name=None):
        self.nbuf += 1
        return Buf(name or f"b{self.nbuf}")

    def bufs(self, n, name="b"):
        return [self.buf(f"{name}{i}") for i in range(n)]

    def op(self, eng, fn, reads=(), writes=(), sig=True, semkey=None, inc=16):
        o = Op(eng, fn, sig, semkey)
        o.inc = inc
        if o.is_dma:
            assert semkey is not None
        deps = set()
        for b in reads:
            if b.last_w is not None:
                deps.add(b.last_w)
        for b in writes:
            if b.last_w is not None:
                deps.add(b.last_w)
            lastr = {}
            for r in b.readers:
                if r.is_dma:
                    deps.add(r)
                else:
                    lastr[r.eng] = r
            for r in lastr.values():
                deps.add(r)
        deps.discard(o)
        for b in reads:
            b.readers.append(o)
        for b in writes:
            b.last_w = o
            b.readers = []
        o.deps = list(deps)
        self.ops.append(o)
        self.streams[o.stream].append(o)
        return o

    def barrier(self):
        last = {}
        for o in self.ops:
            if o.fn is None:
                continue
            if o.is_dma:
                last[("dma", o.semkey)] = o
            else:
                last[("eng", o.eng)] = o
        for st in self.streams:
            o = Op(st, None, False, None)
            o.is_dma = False
            o.deps = list(last.values())
            self.ops.append(o)
            self.streams[st].append(o)

    def emit(self, final_wait_ops=()):
        nc = self.nc
        for o in self.ops:
            for d in o.deps:
                if d.stream == o.stream == "pe" and not d.is_dma:
                    continue
                if not d.is_dma:
                    d.sig = True
        counters = {}
        semnames = {}
        for o in self.ops:
            if o.fn is None:
                continue
            if o.is_dma:
                key = ("dma", o.semkey)
                counters[key] = counters.get(key, 0) + o.inc
                o.tick = (key, counters[key])
            elif o.sig:
                key = ("eng", o.eng)
                counters[key] = counters.get(key, 0) + 1
                o.tick = (key, counters[key])
        for st, lst in self.streams.items():
            nxt = {}
            for o in reversed(lst):
                if o.fn is None or o.is_dma:
                    continue
                if o.sig:
                    nxt[o.eng] = o.tick
                else:
                    o.tick = nxt.get(o.eng)
        self._nosig = True
        sems = {}
        for key in counters:
            nm = "s_" + "_".join(str(k) for k in key)
            sems[key] = self.stack.enter_context(nc.semaphore(nm))
        self.sems = sems
        maxcnt = max(counters.values()) if counters else 0
        engobj = {"pe": "tensor", "act": "scalar", "dve": "vector", "pool": "gpsimd", "sp": "sync"}
        block = self.stack.enter_context(nc.Block())

        def make_stream(st):
            lst = self.streams[st]

            def body(eng):
                waited = {}
                for o in lst:
                    need = {}
                    for d in o.deps:
                        if d.tick is None:
                            raise RuntimeError("dependency on non-signalling op")
                        k, v = d.tick
                        if d.stream == o.stream and not d.is_dma:
                            if st == "pe":
                                continue
                        if v > need.get(k, 0):
                            need[k] = v
                    for k, v in need.items():
                        if waited.get(k, 0) >= v:
                            continue
                        eng.wait_ge(sems[k], v)
                        waited[k] = v
                    if o.fn is None:
                        continue
                    ins = o.fn(eng)
                    if o.tick is not None and (o.is_dma or o.sig):
                        k, v = o.tick
                        ins.then_inc(sems[k], o.inc if o.is_dma else 1)
                if st == "sp":
                    for o in final_wait_ops:
                        k, v = o.tick
                        eng.wait_ge(sems[k], v)
            return body

        for st in ("sp", "pe", "act", "dve", "pool"):
            if not self.streams[st] and st != "sp":
                continue
            getattr(block, engobj[st])(make_stream(st))
        return maxcnt

    def close(self):
        self.stack.close()


D = 2048
EPS = 1e-6


class Arena:
    def __init__(self, P, nfloats=52500):
        self.P = P
        self.t = P.sbuf("arena", [128, nfloats], F32)
        self.n = nfloats
        self.off = 0

    def f32(self, n):
        assert self.off + n <= self.n, f"arena overflow {self.off}+{n}"
        ap = self.t[:, self.off:self.off + n]
        self.off += n
        return ap

    def bf16(self, n):
        m = (n + 1) // 2
        assert self.off + m <= self.n, f"arena overflow {self.off}+{m}"
        ap = self.t[:, self.off:self.off + m].bitcast(BF16)
        self.off += m
        return ap[:, 0:n]

    def mark(self):
        return self.off

    def reset(self, m=0):
        self.P.barrier()
        self.off = m


class Ctx:
    def __init__(self, nc):
        self.nc = nc
        self.P = Prog(nc)
        P = self.P
        self.arena = Arena(P)
        self.ps = [P.psum(f"ps{i}", [128, 512], F32) for i in range(8)]
        self.psb = [P.buf(f"psb{i}") for i in range(8)]
        self.identf = P.sbuf("identf", [128, 128], F32)
        self.ident = P.sbuf("ident", [128, 128], BF16)
        self.ones_bf = P.sbuf("ones_bf", [128, 128], BF16)
        self.ones_f = P.sbuf("ones_f", [128, 128], F32)
        self.bconst = P.buf("const")
        b = self.bconst
        P.op("pool", lambda e: e.memset(self.identf[:], 1.0), writes=[b])
        P.op("pool", lambda e: e.affine_select(out=self.identf[:], in_=self.identf[:], pattern=[[-1, 128]],
                                               compare_op=ALU.is_equal, fill=0.0, base=0, channel_multiplier=1),
             reads=[b], writes=[b])
        P.op("pool", lambda e: e.memset(self.ones_f[:], 1.0), writes=[b])
        P.op("dve", lambda e: e.tensor_copy(out=self.ident[:], in_=self.identf[:]), reads=[b], writes=[b])
        P.op("dve", lambda e: e.tensor_copy(out=self.ones_bf[:], in_=self.ones_f[:]), reads=[b], writes=[b])
        self.dma_rr = 0
        self.out_ops = []

    def psbf(self, i):
        return self.ps[i][:].bitcast(BF16)

    def hwq(self):
        self.dma_rr += 1
        return "sp" if self.dma_rr % 2 else "actq"


class WStream:
    def __init__(self, cx, name, nslots, nelem):
        self.cx = cx
        self.name = name
        self.n = nslots
        self.slots = [cx.arena.bf16(nelem) for _ in range(nslots)]
        self.bufs = [cx.P.buf(f"{name}{i}") for i in range(nslots)]
        self.i = 0

    def load(self, src_ap, view):
        cx = self.cx
        s = self.i % self.n
        self.i += 1
        dst = view(self.slots[s])
        b = self.bufs[s]
        cx.P.op("poolq", lambda e: e.dma_start(out=dst, in_=src_ap), writes=[b], semkey=f"{self.name}{s}")
        return dst, b


def rmsnorm_T(cx, src_tiles, gain_bc, bgain, hnT, bhnT, ntiles, hn_tmp, bhn_tmp, stat, bstat, col0=0):
    P = cx.P
    psi = 0
    for i in range(ntiles):
        x_ap, bx = src_tiles[i]
        tmp = hn_tmp[i % 2]
        btmp = bhn_tmp[i % 2]
        st = stat[:, 4 * i:4 * i + 4]
        P.op("act", lambda e, x_ap=x_ap, tmp=tmp, st=st: e.activation(out=tmp, in_=x_ap, func=AF.Square, accum_out=st[:, 0:1]),
             reads=[bx], writes=[btmp, bstat])
        P.op("dve", lambda e, st=st: e.tensor_scalar(out=st[:, 1:2], in0=st[:, 0:1], scalar1=1.0 / D, scalar2=EPS,
                                                    op0=ALU.mult, op1=ALU.add), reads=[bstat], writes=[bstat])
        P.op("act", lambda e, st=st: e.activation(out=st[:, 2:3], in_=st[:, 1:2], func=AF.Sqrt), reads=[bstat], writes=[bstat])
        P.op("dve", lambda e, st=st: e.reciprocal(out=st[:, 3:4], in_=st[:, 2:3]), reads=[bstat], writes=[bstat])
        P.op("dve", lambda e, x_ap=x_ap, tmp=tmp, st=st: e.scalar_tensor_tensor(out=tmp, in0=x_ap, scalar=st[:, 3:4], in1=gain_bc,
                                                                              op0=ALU.mult, op1=ALU.mult),
             reads=[bx, bstat, bgain], writes=[btmp])
        for half in range(2):
            bank = 6 + (psi % 2)
            psi += 1
            pv = cx.psbf(bank).rearrange("p (c t) -> p c t", t=128)
            for j in range(8):
                kc = half * 8 + j
                P.op("pe", lambda e, pv=pv, j=j, tmp=tmp, kc=kc: e.transpose(pv[:, j, :], tmp[:, kc * 128:(kc + 1) * 128], cx.ident[:]),
                     reads=[btmp, cx.bconst], writes=[cx.psb[bank]], sig=(j == 7))
            dst = hnT[:, half * 8:(half + 1) * 8, col0 + i * 128: col0 + (i + 1) * 128]
            if (i + half) % 2 == 0:
                P.op("act", lambda e, dst=dst, pv=pv: e.activation(out=dst, in_=pv, func=AF.Copy), reads=[cx.psb[bank]], writes=[bhnT])
            else:
                P.op("dve", lambda e, dst=dst, pv=pv: e.tensor_copy(out=dst, in_=pv), reads=[cx.psb[bank]], writes=[bhnT])


def rmsnorm_pipe(cx, ntiles, get_tile, gain_bc, bgain, hnT, bhnT, hn_tmp, bhn_tmp, stat, bstat, col0=0):
    P = cx.P
    nst = stat.shape[1] // 4
    psi = [0]
    info = {}

    def phase1(i):
        x_ap, bx = get_tile(i)
        tmp = hn_tmp[i % 2]; btmp = bhn_tmp[i % 2]
        st = stat[:, 4 * (i % nst):4 * (i % nst) + 4]
        P.op("act", lambda e: e.activation(out=tmp, in_=x_ap, func=AF.Square, accum_out=st[:, 0:1]), reads=[bx], writes=[btmp, bstat])
        P.op("dve", lambda e: e.tensor_scalar(out=st[:, 1:2], in0=st[:, 0:1], scalar1=1.0 / D, scalar2=EPS, op0=ALU.mult, op1=ALU.add), reads=[bstat], writes=[bstat])
        P.op("act", lambda e: e.activation(out=st[:, 2:3], in_=st[:, 1:2], func=AF.Sqrt), reads=[bstat], writes=[bstat])
        P.op("dve", lambda e: e.reciprocal(out=st[:, 3:4], in_=st[:, 2:3]), reads=[bstat], writes=[bstat])
        P.op("dve", lambda e: e.scalar_tensor_tensor(out=tmp, in0=x_ap, scalar=st[:, 3:4], in1=gain_bc, op0=ALU.mult, op1=ALU.mult),
             reads=[bx, bstat, bgain], writes=[btmp])
        info[i] = (tmp, btmp)

    def phase2(i):
        tmp, btmp = info.pop(i)
        for half in range(2):
            bank = 6 + (psi[0] % 2)
            psi[0] += 1
            pv = cx.psbf(bank).rearrange("p (c t) -> p c t", t=128)
            for j in range(8):
                kc = half * 8 + j
                P.op("pe", lambda e, pv=pv, j=j, kc=kc: e.transpose(pv[:, j, :], tmp[:, kc * 128:(kc + 1) * 128], cx.ident[:]),
                     reads=[btmp, cx.bconst], writes=[cx.psb[bank]], sig=(j == 7))
            dst = hnT[:, half * 8:(half + 1) * 8, col0 + i * 128: col0 + (i + 1) * 128]
            if (i + half) % 2 == 0:
                P.op("act", lambda e, dst=dst, pv=pv: e.activation(out=dst, in_=pv, func=AF.Copy), reads=[cx.psb[bank]], writes=[bhnT])
            else:
                P.op("dve", lambda e, dst=dst, pv=pv: e.tensor_copy(out=dst, in_=pv), reads=[cx.psb[bank]], writes=[bhnT])

    for i in range(ntiles + 1):
        if i < ntiles:
            phase1(i)
        if i >= 1:
            phase2(i - 1)


def mlp_block(cx, h_in, h_out, gain_dram, w1, w2, T, final_gain=None):
    P = cx.P
    A = cx.arena
    m0 = A.mark()
    FF = 4 * D
    TT = 512
    gain_bc = A.f32(D); bgain = P.buf("gain")
    P.op("sp", lambda e: e.dma_start(out=gain_bc, in_=gain_dram.partition_broadcast(128)), writes=[bgain], semkey="gain")
    if final_gain is not None:
        fg_bc = A.f32(D); bfg = P.buf("fgain")
        P.op("sp", lambda e: e.dma_start(out=fg_bc, in_=final_gain.partition_broadcast(128)), writes=[bfg], semkey="fgain")
    hres = [A.f32(D) for _ in range(4)]
    bres = [P.buf(f"hres{i}") for i in range(4)]
    hn_tmp = [A.bf16(D) for _ in range(2)]
    bhn_tmp = [P.buf("hntmp0"), P.buf("hntmp1")]
    stat = A.f32(16); bstat = P.buf("stat")
    hnT = A.bf16(16 * TT).rearrange("p (c t) -> p c t", t=TT); bhnT = P.buf("hnT")
    aT = A.bf16(64 * TT).rearrange("p (c t) -> p c t", t=TT)
    baT = [P.buf(f"aT{i}") for i in range(64)]
    sq = [A.f32(TT) for _ in range(2)]; bsq = [P.buf("sq0"), P.buf("sq1")]
    W1C = 256
    w1s = WStream(cx, "w1s", 2, 16 * W1C)
    W2K = 8
    w2s = WStream(cx, "w2s", 2, W2K * 512)
    w1v = w1.rearrange("(kc p) n -> p kc n", p=128)
    w2v = w2.rearrange("(fc p) n -> p fc n", p=128)
    nblk = T // TT
    for blk in range(nblk):
        t0 = blk * TT
        for i in range(4):
            P.op(cx.hwq(), lambda e, i=i, t0=t0: e.dma_start(out=hres[i], in_=h_in[t0 + i * 128: t0 + (i + 1) * 128, :]),
                 writes=[bres[i]], semkey=f"hres{i}")
        rmsnorm_T(cx, [(hres[i], bres[i]) for i in range(4)], gain_bc, bgain, hnT, bhnT, 4, hn_tmp, bhn_tmp, stat, bstat)
        ei = 0
        for g in range(FF // W1C):
            wt, bw = w1s.load(w1v[:, :, g * W1C:(g + 1) * W1C], lambda s: s.rearrange("p (c n) -> p c n", n=W1C))
            for mm in range(W1C // 128):
                m = g * (W1C // 128) + mm
                bank = ei % 4
                for kc in range(16):
                    P.op("pe", lambda e, bank=bank, wt=wt, kc=kc, mm=mm: e.matmul(cx.ps[bank][:], lhsT=wt[:, kc, mm * 128:(mm + 1) * 128],
                                                                                   rhs=hnT[:, kc, :], start=(kc == 0), stop=(kc == 15)),
                         reads=[bw, bhnT], writes=[cx.psb[bank]], sig=(kc == 15))
                s = sq[ei % 2]; bs = bsq[ei % 2]
                P.op("act", lambda e, s=s, bank=bank: e.activation(out=s, in_=cx.ps[bank][:], func=AF.Square), reads=[cx.psb[bank]], writes=[bs])
                P.op("dve", lambda e, s=s, bank=bank, m=m: e.scalar_tensor_tensor(out=aT[:, m, :], in0=cx.ps[bank][:], scalar=0.0, in1=s,
                                                                                  op0=ALU.is_gt, op1=ALU.mult),
                     reads=[cx.psb[bank], bs], writes=[baT[m]])
                ei += 1
        for dq in range(4):
            for fg in range(64 // W2K):
                wt, bw = w2s.load(w2v[:, fg * W2K:(fg + 1) * W2K, dq * 512:(dq + 1) * 512], lambda s: s.rearrange("p (c n) -> p c n", n=512))
                for tt in range(4):
                    for fl in range(W2K):
                        fc = fg * W2K + fl
                        P.op("pe", lambda e, tt=tt, fc=fc, fl=fl, wt=wt: e.matmul(cx.ps[tt][:], lhsT=aT[:, fc, tt * 128:(tt + 1) * 128],
                                                                                 rhs=wt[:, fl, :], start=(fc == 0), stop=(fc == 63)),
                             reads=[bw, baT[fc]], writes=[cx.psb[tt]], sig=(fc == 63))
            for tt in range(4):
                dst = hres[tt][:, dq * 512:(dq + 1) * 512]
                P.op("dve", lambda e, dst=dst, tt=tt: e.tensor_tensor(out=dst, in0=cx.ps[tt][:], in1=dst, op=ALU.add),
                     reads=[cx.psb[tt], bres[tt]], writes=[bres[tt]])
        for i in range(4):
            src = hres[i]
            if final_gain is not None:
                st = stat[:, 0:4]
                tmp = sq[0].bitcast(BF16)
                junk = hn_tmp[0]
                P.op("act", lambda e, src=src, junk=junk, st=st: e.activation(out=junk, in_=src, func=AF.Square, accum_out=st[:, 0:1]),
                     reads=[bres[i]], writes=[bhn_tmp[0], bstat])
                P.op("dve", lambda e, st=st: e.tensor_scalar(out=st[:, 1:2], in0=st[:, 0:1], scalar1=1.0 / D, scalar2=EPS,
                                                            op0=ALU.mult, op1=ALU.add), reads=[bstat], writes=[bstat])
                P.op("act", lambda e, st=st: e.activation(out=st[:, 2:3], in_=st[:, 1:2], func=AF.Sqrt), reads=[bstat], writes=[bstat])
                P.op("dve", lambda e, st=st: e.reciprocal(out=st[:, 3:4], in_=st[:, 2:3]), reads=[bstat], writes=[bstat])
                P.op("dve", lambda e, src=src, st=st: e.scalar_tensor_tensor(out=src, in0=src, scalar=st[:, 3:4], in1=fg_bc,
                                                                            op0=ALU.mult, op1=ALU.mult),
                     reads=[bres[i], bstat, bfg], writes=[bres[i]])
            o = P.op(cx.hwq(), lambda e, i=i, t0=t0, src=src: e.dma_start(out=h_out[t0 + i * 128: t0 + (i + 1) * 128, :], in_=src),
                     reads=[bres[i]], semkey=f"hout{i}")
            cx.out_ops.append(o)
    A.reset(m0)


NVEC = 96
V_CW, V_CB, V_BA, V_BI, V_LAM, V_LB, V_GN = 0, 32, 40, 48, 56, 64, 88
GELU_C = 0.7978845608028654


def proj_fm(cx, hnT, bhnT, col0, T, wv, wcol0, ncol, ws, consume, wtile=None, after_load=None):
    P = cx.P
    wtile = wtile or ncol
    cnt = 0
    for g in range(ncol // wtile):
        wt, bw = ws.load(wv[:, :, wcol0 + g * wtile: wcol0 + (g + 1) * wtile], lambda s: s[:, 0:16 * wtile].rearrange("p (c n) -> p c n", n=wtile))
        if after_load is not None:
            after_load()
        for mm in range(wtile // 128):
            m = g * (wtile // 128) + mm
            for n in range((T + 511) // 512):
                tn = min(512, T - n * 512)
                bank = cnt % 2
                cnt += 1
                for kc in range(16):
                    P.op("pe", lambda e, bank=bank, wt=wt, kc=kc, mm=mm, n=n, tn=tn: e.matmul(
                        cx.ps[bank][:, 0:tn], lhsT=wt[:, kc, mm * 128:(mm + 1) * 128], rhs=hnT[:, kc, col0 + n * 512: col0 + n * 512 + tn],
                        start=(kc == 0), stop=(kc == 15)), reads=[bw, bhnT], writes=[cx.psb[bank]], sig=(kc == 15))
                consume(m, n, tn, bank)


def stage_r1(cx, T, x, xh, gain, w_in, vecs, w_a, w_i, outs):
    P = cx.P
    A = cx.arena
    m0 = A.mark()
    NT = T // 128
    TC = 128 + T
    vec = A.f32(NVEC); bvec = P.buf("vec")
    P.op("sp", lambda e: e.dma_start(out=vec, in_=vecs), writes=[bvec], semkey="vec")
    gain_bc = A.f32(D); bgain = P.buf("gain")
    P.op("sp", lambda e: e.dma_start(out=gain_bc, in_=gain.partition_broadcast(128)), writes=[bgain], semkey="gain")
    der = A.f32(64); bder = P.buf("der")
    lb, oml, noml, sca = der[:, 0:8], der[:, 8:16], der[:, 16:24], der[:, 24:32]
    t0_, t1_, t2_ = der[:, 32:40], der[:, 40:48], der[:, 48:56]
    l0, l1, l2 = vec[:, V_LB:V_LB + 8], vec[:, V_LB + 8:V_LB + 16], vec[:, V_LB + 16:V_LB + 24]
    dv = lambda fn, r=(bvec, bder), w=(bder,): P.op("dve", fn, reads=list(r), writes=list(w))
    av = lambda fn, r=(bvec, bder), w=(bder,): P.op("act", fn, reads=list(r), writes=list(w))
    dv(lambda e: e.tensor_max(out=t0_, in0=l0, in1=l1))
    dv(lambda e: e.tensor_max(out=t0_, in0=t0_, in1=l2))
    dv(lambda e: e.tensor_sub(out=t1_, in0=l0, in1=t0_))
    av(lambda e: e.activation(out=lb, in_=t1_, func=AF.Exp))
    dv(lambda e: e.tensor_sub(out=t1_, in0=l1, in1=t0_))
    av(lambda e: e.activation(out=t2_, in_=t1_, func=AF.Exp))
    dv(lambda e: e.tensor_add(out=oml, in0=lb, in1=t2_))
    dv(lambda e: e.tensor_sub(out=t1_, in0=l2, in1=t0_))
    av(lambda e: e.activation(out=t2_, in_=t1_, func=AF.Exp))
    dv(lambda e: e.tensor_add(out=oml, in0=oml, in1=t2_))
    dv(lambda e: e.reciprocal(out=t2_, in_=oml))
    dv(lambda e: e.tensor_mul(out=lb, in0=lb, in1=t2_))
    dv(lambda e: e.tensor_scalar(out=oml, in0=lb, scalar1=-1.0, scalar2=1.0, op0=ALU.mult, op1=ALU.add))
    dv(lambda e: e.tensor_scalar(out=noml, in0=oml, scalar1=-1.0, scalar2=None, op0=ALU.mult))
    lam = vec[:, V_LAM:V_LAM + 8]
    dv(lambda e: e.tensor_scalar(out=t1_, in0=lam, scalar1=-1.0, scalar2=None, op0=ALU.mult))
    dv(lambda e: e.tensor_max(out=t0_, in0=lam, in1=t1_))
    av(lambda e: e.activation(out=t0_, in_=t0_, func=AF.Exp, scale=-1.0))
    dv(lambda e: e.tensor_scalar(out=t1_, in0=t0_, scalar1=2.0, scalar2=None, op0=ALU.add))
    dv(lambda e: e.reciprocal(out=t1_, in_=t1_))
    dv(lambda e: e.tensor_mul(out=t0_, in0=t0_, in1=t1_))
    dv(lambda e: e.tensor_mul(out=t1_, in0=t0_, in1=t0_))
    dv(lambda e: e.tensor_scalar(out=t2_, in0=t1_, scalar1=1.0 / 15, scalar2=1.0 / 13, op0=ALU.mult, op1=ALU.add))
    for cst in (1.0 / 11, 1.0 / 9, 1.0 / 7, 1.0 / 5, 1.0 / 3, 1.0):
        dv(lambda e: e.tensor_mul(out=t2_, in0=t2_, in1=t1_))
        dv(lambda e, cst=cst: e.tensor_scalar(out=t2_, in0=t2_, scalar1=cst, scalar2=None, op0=ALU.add))
    dv(lambda e: e.tensor_mul(out=t2_, in0=t2_, in1=t0_))
    dv(lambda e: e.tensor_scalar(out=t0_, in0=lam, scalar1=-1.0, scalar2=0.0, op0=ALU.mult, op1=ALU.max))
    dv(lambda e: e.scalar_tensor_tensor(out=sca, in0=t2_, scalar=2.0, in1=t0_, op0=ALU.mult, op1=ALU.add))
    dv(lambda e: e.tensor_scalar(out=sca, in0=sca, scalar1=-8.0, scalar2=None, op0=ALU.mult))
    if "dbg" in outs:
        P.op("sp", lambda e: e.dma_start(out=outs["dbg"][:, 0:64], in_=der), reads=[bder], semkey="dbg")
        P.op("sp", lambda e: e.dma_start(out=outs["dbg"][:, 64:64 + NVEC], in_=vec), reads=[bvec], semkey="dbg2")
    hnT = A.bf16(16 * TC).rearrange("p (c t) -> p c t", t=TC); bhnT = P.buf("hnT")
    mk1 = A.mark()
    xt = [A.f32(D) for _ in range(4)]; bxt = [P.buf(f"xt{i}") for i in range(4)]
    hn_tmp = [A.bf16(D) for _ in range(2)]; bhn_tmp = [P.buf("hntmp0"), P.buf("hntmp1")]
    stat = A.f32(4 * 4); bstat = P.buf("stat")
    def get_tile_r1(i):
        s = i % 4
        src = xh if i == 0 else x[(i - 1) * 128: i * 128, :]
        P.op(cx.hwq(), lambda e: e.dma_start(out=xt[s], in_=src), writes=[bxt[s]], semkey=f"xt{s}")
        return xt[s], bxt[s]

    rmsnorm_pipe(cx, NT + 1, get_tile_r1, gain_bc, bgain, hnT, bhnT, hn_tmp, bhn_tmp, stat, bstat, col0=0)
    A.reset(mk1)
    w_in_v = w_in.rearrange("(kc p) n -> p kc n", p=128)
    ws = WStream(cx, "wst", 2, 16 * 256)
    nT5 = (T + 511) // 512
    mk2 = A.mark()
    rows = [A.f32(T) for _ in range(2)]; brows = [P.buf("row0"), P.buf("row1")]
    tmpa = [A.f32(512) for _ in range(2)]; btmpa = [P.buf("tmpa0"), P.buf("tmpa1")]
    tmpb = [A.f32(512) for _ in range(2)]; btmpb = [P.buf("tmpb0"), P.buf("tmpb1")]
    cnt = [0]

    def consume_gy(m, n, tn, bank):
        k = cnt[0] % 2; cnt[0] += 1
        row, brow = rows[m % 2], brows[m % 2]
        ps = cx.ps[bank][:, 0:tn]; bps = cx.psb[bank]
        ta, tb = tmpa[k][:, 0:tn], tmpb[k][:, 0:tn]
        P.op("act", lambda e: e.activation(out=ta, in_=ps, func=AF.Square), reads=[bps], writes=[btmpa[k]])
        P.op("dve", lambda e: e.tensor_scalar(out=ta, in0=ta, scalar1=0.044715, scalar2=1.0, op0=ALU.mult, op1=ALU.add), reads=[btmpa[k]], writes=[btmpa[k]])
        P.op("dve", lambda e: e.tensor_tensor(out=ta, in0=ta, in1=ps, op=ALU.mult), reads=[btmpa[k], bps], writes=[btmpa[k]])
        P.op("act", lambda e: e.activation(out=tb, in_=ta, func=AF.Sigmoid, scale=2.0 * GELU_C), reads=[btmpa[k]], writes=[btmpb[k]])
        P.op("dve", lambda e: e.tensor_tensor(out=row[:, n * 512:n * 512 + tn], in0=tb, in1=ps, op=ALU.mult), reads=[btmpb[k], bps], writes=[brow])
        if n == nT5 - 1:
            P.op("sp", lambda e: e.dma_start(out=outs["gy"][m], in_=row), reads=[brow], semkey=f"orow{m % 2}")

    proj_fm(cx, hnT, bhnT, 128, T, w_in_v, 1024, 1024, ws, consume_gy, wtile=256)

    def consume_sg(m, n, tn, bank):
        row, brow = rows[m % 2], brows[m % 2]
        ps = cx.ps[bank][:, 0:tn]; bps = cx.psb[bank]
        P.op("act", lambda e: e.activation(out=row[:, n * 512:n * 512 + tn], in_=ps, func=AF.Silu), reads=[bps], writes=[brow])
        if n == nT5 - 1:
            P.op("sp", lambda e: e.dma_start(out=outs["sg"][m], in_=row), reads=[brow], semkey=f"orow{m % 2}")

    proj_fm(cx, hnT, bhnT, 128, T, w_in_v, 5120, 1024, ws, consume_sg, wtile=256)
    A.reset(mk2)
    car = A.f32(24 + 1024); bcar = P.buf("car")
    mk3 = A.mark()
    TL = T + 4
    xl = [A.f32(TL) for _ in range(2)]; bxl = [P.buf("xl0"), P.buf("xl1")]
    xc = [A.f32(T) for _ in range(2)]; bxc = [P.buf("xc0"), P.buf("xc1")]
    xcb = [A.bf16(T) for _ in range(2)]; bxcb = [P.buf("xcb0"), P.buf("xcb1")]
    arow = [A.f32(T) for _ in range(2)]; barow = [P.buf("arow0"), P.buf("arow1")]
    urow = [A.f32(T) for _ in range(2)]; burow = [P.buf("urow0"), P.buf("urow1")]
    hsc = A.f32(T); bhsc = P.buf("hsc")
    rt = [A.f32(512) for _ in range(4)]; brt = [P.buf(f"rt{i}") for i in range(4)]
    it = [A.f32(512) for _ in range(4)]; bit = [P.buf(f"it{i}") for i in range(4)]
    tt_ = [A.f32(512) for _ in range(4)]; btt = [P.buf(f"tt{i}") for i in range(4)]
    assert nT5 <= 4
    sumr = A.f32(8); bsumr = P.buf("sumr")
    wg = WStream(cx, "wg", 2, 2 * 2 * 256)
    for h in range(4):
        def consume_xl(m, n, tn, bank, h=h):
            cc = m
            P.op("act", lambda e: e.activation(out=xl[cc][:, 4 + n * 512: 4 + n * 512 + tn], in_=cx.ps[bank][:, 0:tn], func=AF.Copy),
                 reads=[cx.psb[bank]], writes=[bxl[cc]])
        wt, bw = ws.load(w_in_v[:, :, h * 256:(h + 1) * 256], lambda s: s[:, 0:16 * 256].rearrange("p (c n) -> p c n", n=256))
        for cc in range(2):
            for n in range(nT5):
                tn = min(512, T - n * 512)
                bank = (cc * nT5 + n) % 2
                for kc in range(16):
                    P.op("pe", lambda e, bank=bank, kc=kc, cc=cc, n=n, tn=tn, wt=wt: e.matmul(
                        cx.ps[bank][:, 0:tn], lhsT=wt[:, kc, cc * 128:(cc + 1) * 128], rhs=hnT[:, kc, 128 + n * 512: 128 + n * 512 + tn],
                        start=(kc == 0), stop=(kc == 15)), reads=[bw, bhnT], writes=[cx.psb[bank]], sig=(kc == 15))
                consume_xl(cc, n, tn, bank)
            bank = 2 + cc
            for kc in range(16):
                P.op("pe", lambda e, bank=bank, kc=kc, cc=cc, wt=wt: e.matmul(
                    cx.ps[bank][:, 0:4], lhsT=wt[:, kc, cc * 128:(cc + 1) * 128], rhs=hnT[:, kc, 124:128],
                    start=(kc == 0), stop=(kc == 15)), reads=[bw, bhnT], writes=[cx.psb[bank]], sig=(kc == 15))
            P.op("act", lambda e, cc=cc, bank=bank: e.activation(out=xl[cc][:, 0:4], in_=cx.ps[bank][:, 0:4], func=AF.Copy),
                 reads=[cx.psb[bank]], writes=[bxl[cc]])
            c = 2 * h + cc
            cwj = lambda j, c=c: vec[:, V_CW + 8 * j + c: V_CW + 8 * j + c + 1]
            P.op("act", lambda e, cc=cc, c=c, cwj=cwj: e.activation(out=xc[cc], in_=xl[cc][:, 1:1 + T], func=AF.Identity, scale=cwj(0),
                                                                   bias=vec[:, V_CB + c:V_CB + c + 1]), reads=[bxl[cc], bvec], writes=[bxc[cc]])
            for j in range(1, 4):
                P.op("dve", lambda e, cc=cc, j=j, cwj=cwj: e.scalar_tensor_tensor(out=xc[cc], in0=xl[cc][:, 1 + j:1 + j + T], scalar=cwj(j), in1=xc[cc],
                                                                                op0=ALU.mult, op1=ALU.add), reads=[bxl[cc], bvec, bxc[cc]], writes=[bxc[cc]])
            P.op("act", lambda e, cc=cc: e.activation(out=xcb[cc], in_=xc[cc], func=AF.Copy), reads=[bxc[cc]], writes=[bxcb[cc]])
        s = wg.i % wg.n
        wgt = wg.slots[s].rearrange("p (g i n) -> p g i n", g=2, i=2); bwg = wg.bufs[s]
        wg.i += 1
        P.op("poolq", lambda e, wgt=wgt, h=h: e.dma_start(out=wgt[:, 0], in_=w_a[h].rearrange("(i p) n -> p i n", p=128)), writes=[bwg], semkey=f"wg{s}")
        P.op("poolq", lambda e, wgt=wgt, h=h: e.dma_start(out=wgt[:, 1], in_=w_i[h].rearrange("(i p) n -> p i n", p=128)), writes=[bwg], semkey=f"wg{s}")
        for jj in range(2):
            j = 2 * h + jj
            ar, bar = arow[j % 2], barow[j % 2]
            ur, bur = urow[j % 2], burow[j % 2]
            tl = []
            for n in range(nT5):
                tn = min(512, T - n * 512)
                tl.append((n, tn, slice(n * 512, n * 512 + tn)))
            for n, tn, sl in tl:
                for g in range(2):
                    bank = 4 + 2 * g + n % 2
                    for ii in range(2):
                        P.op("pe", lambda e, bank=bank, g=g, ii=ii, jj=jj, sl=sl, tn=tn, wgt=wgt: e.matmul(
                            cx.ps[bank][:, 0:tn], lhsT=wgt[:, g, ii, jj * 128:(jj + 1) * 128], rhs=xcb[ii][:, sl], start=(ii == 0), stop=(ii == 1)),
                            reads=[bwg, bxcb[ii]], writes=[cx.psb[bank]], sig=(ii == 1))
                r_, i_ = rt[n][:, 0:tn], it[n][:, 0:tn]
                P.op("act", lambda e, r_=r_, j=j, n=n, tn=tn: e.activation(out=r_, in_=cx.ps[4 + n % 2][:, 0:tn], func=AF.Sigmoid, bias=vec[:, V_BA + j:V_BA + j + 1],
                                                                         accum_out=sumr[:, n:n + 1]), reads=[cx.psb[4 + n % 2], bvec], writes=[brt[n], bsumr])
                P.op("act", lambda e, i_=i_, j=j, n=n, tn=tn: e.activation(out=i_, in_=cx.ps[6 + n % 2][:, 0:tn], func=AF.Sigmoid, bias=vec[:, V_BI + j:V_BI + j + 1]),
                     reads=[cx.psb[6 + n % 2], bvec], writes=[bit[n]])
            for n, tn, sl in tl:
                r_, t_ = rt[n][:, 0:tn], tt_[n][:, 0:tn]
                P.op("act", lambda e, r_=r_, j=j, sl=sl, ar=ar: e.activation(out=ar[:, sl], in_=r_, func=AF.Exp, scale=sca[:, j:j + 1]),
                     reads=[brt[n], bder], writes=[bar])
                P.op("dve", lambda e, t_=t_, sl=sl, ar=ar: e.tensor_tensor(out=t_, in0=ar[:, sl], in1=ar[:, sl], op=ALU.mult), reads=[bar], writes=[btt[n]])
                P.op("dve", lambda e, t_=t_: e.tensor_scalar(out=t_, in0=t_, scalar1=-1.0, scalar2=1.0, op0=ALU.mult, op1=ALU.add), reads=[btt[n]], writes=[btt[n]])
            for n, tn, sl in tl:
                i_, t_ = it[n][:, 0:tn], tt_[n][:, 0:tn]
                P.op("act", lambda e, t_=t_: e.activation(out=t_, in_=t_, func=AF.Sqrt), reads=[btt[n]], writes=[btt[n]])
                P.op("dve", lambda e, t_=t_, i_=i_: e.tensor_tensor(out=t_, in0=t_, in1=i_, op=ALU.mult), reads=[btt[n], bit[n]], writes=[btt[n]])
                P.op("dve", lambda e, t_=t_, sl=sl, ur=ur, jj=jj: e.tensor_tensor(out=ur[:, sl], in0=t_, in1=xc[jj][:, sl], op=ALU.mult),
                     reads=[btt[n], bxc[jj]], writes=[bur])
            P.op("sp", lambda e, j=j, ar=ar: e.dma_start(out=outs["a"][j], in_=ar), reads=[bar], semkey=f"oa{j % 2}")
            P.op("sp", lambda e, j=j, ur=ur: e.dma_start(out=outs["u"][j], in_=ur), reads=[bur], semkey=f"ou{j % 2}")
            P.op("dve", lambda e, ar=ar, ur=ur: e.tensor_tensor_scan(out=hsc, data0=ar, data1=ur, initial=0.0, op0=ALU.mult, op1=ALU.add),
                 reads=[bar, bur], writes=[bhsc])
            P.op("dve", lambda e, j=j: e.tensor_copy(out=car[:, j:j + 1], in_=hsc[:, T - 1:T]), reads=[bhsc], writes=[bcar])
            P.op("dve", lambda e, j=j: e.tensor_reduce(out=car[:, 8 + j:9 + j], in_=sumr[:, 0:nT5], axis=AX.X, op=ALU.add), reads=[bsumr], writes=[bcar])
            P.op("act", lambda e, j=j: e.activation(out=car[:, 8 + j:9 + j], in_=car[:, 8 + j:9 + j], func=AF.Exp, scale=sca[:, j:j + 1]),
                 reads=[bcar, bder], writes=[bcar])
    A.reset(mk3)
    NC_ = T // 64
    ones_row = A.bf16(T); mreset = A.bf16(T); bcm = P.buf("cmask")
    P.op("pool", lambda e: e.memset(ones_row, 1.0), writes=[bcm])
    P.op("pool", lambda e: e.memset(mreset, 1.0), writes=[bcm])
    P.op("pool", lambda e: e.memset(mreset.rearrange("p (c t) -> p c t", t=64)[:, :, 0:1], 0.0), reads=[bcm], writes=[bcm])
    mut = A.bf16(64); mutf = A.f32(64); bmut = P.buf("mut")
    P.op("pool", lambda e: e.memset(mutf[0:64, :], 1.0), writes=[bmut])
    P.op("pool", lambda e: e.affine_select(out=mutf[0:64, :], in_=mutf[0:64, :], pattern=[[1, 64]], compare_op=ALU.is_ge, fill=0.0, base=0, channel_multiplier=-1),
         reads=[bmut], writes=[bmut])
    P.op("dve", lambda e: e.tensor_copy(out=mut[0:64, :], in_=mutf[0:64, :]), reads=[bmut], writes=[bmut])
    qf = A.f32(T); bqf = P.buf("qf")
    sig = A.f32(T); bsig = P.buf("sig")
    lgf = A.f32(T); blgf = P.buf("lgf")
    bb = A.f32(T); bbb = P.buf("bb")
    Bs = A.f32(T); bBs = P.buf("Bs")
    eb = A.f32(T); beb = P.buf("eb")
    qt = A.bf16(T); bqt = P.buf("qt")
    kt = A.bf16(T); bkt = P.buf("kt")
    qh = A.bf16(T); bqh = P.buf("qh")
    qt2 = A.bf16(T); bqt2 = P.buf("qt2")
    ktok = A.bf16(NC_ * 128).rearrange("p (c k) -> p c k", k=128); bktok = P.buf("ktok")
    vtok = A.bf16(NC_ * 128).rearrange("p (c k) -> p c k", k=128); bvtok = P.buf("vtok")
    oloc = A.f32(T); boloc = P.buf("oloc")
    S = A.f32(128); bS = P.buf("S")
    Sb = A.bf16(128); bSb = P.buf("Sb")
    stmp = A.f32(128); bstmp = P.buf("stmp")
    scm = [A.bf16(64) for _ in range(2)]; bscm = [P.buf("scm0"), P.buf("scm1")]
    wq = ws
    for h in range(8):
        lbh, omlh, nomlh = lb[:, h:h + 1], oml[:, h:h + 1], noml[:, h:h + 1]

        def consume_q(m, n, tn, bank):
            P.op("act", lambda e: e.activation(out=qf[:, n * 512:n * 512 + tn], in_=cx.ps[bank][:, 0:tn], func=AF.Silu), reads=[cx.psb[bank]], writes=[bqf])

        def consume_f(m, n, tn, bank):
            P.op("act", lambda e: e.activation(out=sig[:, n * 512:n * 512 + tn], in_=cx.ps[bank][:, 0:tn], func=AF.Sigmoid), reads=[cx.psb[bank]], writes=[bsig])

        proj_fm(cx, hnT, bhnT, 128, T, w_in_v, 2048 + h * 128, 128, wq, consume_q)
        proj_fm(cx, hnT, bhnT, 128, T, w_in_v, 3072 + h * 128, 128, wq, consume_f)
        wt, bw = wq.load(w_in_v[:, :, 4096 + h * 128: 4096 + (h + 1) * 128], lambda s: s[:, 0:16 * 128].rearrange("p (c n) -> p c n", n=128))
        for c4 in range(0, NC_, 4):
            bank = (c4 // 4) % 2
            pv = cx.ps[bank][0:64, :].rearrange("p (c k) -> p c k", k=128)
            for cj in range(4):
                c = c4 + cj
                for kc in range(16):
                    P.op("pe", lambda e, pv=pv, cj=cj, c=c, kc=kc, wt=wt: e.matmul(pv[:, cj, :], lhsT=hnT[:, kc, 128 + c * 64: 128 + (c + 1) * 64], rhs=wt[:, kc, :],
                                                                                  start=(kc == 0), stop=(kc == 15)),
                         reads=[bw, bhnT], writes=[cx.psb[bank]], sig=(kc == 15 and cj == 3))
            P.op("act" if (c4 // 4) % 2 else "dve",
                 (lambda e, pv=pv, c4=c4: e.activation(out=vtok[0:64, c4:c4 + 4, :], in_=pv, func=AF.Copy)) if (c4 // 4) % 2 else
                 (lambda e, pv=pv, c4=c4: e.tensor_copy(out=vtok[0:64, c4:c4 + 4, :], in_=pv)),
                 reads=[cx.psb[bank]], writes=[bvtok])
        P.op("act", lambda e, omlh=omlh, lbh=lbh: e.activation(out=lgf, in_=sig, func=AF.Ln, scale=omlh, bias=lbh), reads=[bsig, bder], writes=[blgf])
        P.op("dve", lambda e, nomlh=nomlh, omlh=omlh: e.tensor_scalar(out=sig, in0=sig, scalar1=nomlh, scalar2=omlh, op0=ALU.mult, op1=ALU.add),
             reads=[bsig, bder], writes=[bsig])
        P.op("dve", lambda e: e.tensor_tensor_scan(out=bb, data0=mreset, data1=lgf, initial=0.0, op0=ALU.mult, op1=ALU.add), reads=[bcm, blgf], writes=[bbb])
        P.op("dve", lambda e: e.tensor_tensor_scan(out=Bs, data0=ones_row, data1=lgf, initial=0.0, op0=ALU.mult, op1=ALU.add), reads=[bcm, blgf], writes=[bBs])
        P.op("act", lambda e: e.activation(out=eb, in_=bb, func=AF.Exp), reads=[bbb], writes=[beb])
        P.op("dve", lambda e: e.tensor_scalar(out=lgf, in0=bb, scalar1=-1.0, scalar2=80.0, op0=ALU.mult, op1=ALU.min), reads=[bbb, blgf], writes=[blgf])
        P.op("act", lambda e: e.activation(out=lgf, in_=lgf, func=AF.Exp), reads=[blgf], writes=[blgf])
        P.op("dve", lambda e: e.tensor_tensor(out=kt, in0=sig, in1=lgf, op=ALU.mult), reads=[bsig, blgf], writes=[bkt])
        P.op("dve", lambda e: e.tensor_tensor(out=lgf, in0=qf, in1=eb, op=ALU.mult), reads=[bqf, beb, bkt], writes=[blgf])
        P.op("act", lambda e: e.activation(out=qt, in_=lgf, func=AF.Copy), reads=[blgf], writes=[bqt])
        P.op("pool", lambda e: e.memset(qt2[:, 0:64], 0.0), writes=[bqt2])
        P.op("dve", lambda e: e.tensor_tensor(out=qt2.rearrange("p (c t) -> p c t", t=64)[:, 1:NC_, :], in0=lgf.rearrange("p (c t) -> p c t", t=64)[:, 1:NC_, :],
                                              in1=eb.rearrange("p (c t) -> p c t", t=64)[:, 0:NC_ - 1, 63:64].to_broadcast([128, NC_ - 1, 64]), op=ALU.mult),
             reads=[blgf, beb], writes=[bqt2])
        P.op("act", lambda e: e.activation(out=Bs, in_=Bs, func=AF.Exp), reads=[bBs], writes=[bBs])
        P.op("dve", lambda e: e.tensor_tensor(out=qh, in0=qf, in1=Bs, op=ALU.mult), reads=[bqf, bBs], writes=[bqh])
        P.op("sp", lambda e, h=h: e.dma_start(out=outs["qh"][h], in_=qh), reads=[bqh], semkey="oqh")
        P.op("dve", lambda e, h=h: e.tensor_copy(out=car[:, 16 + h:17 + h], in_=Bs[:, T - 1:T]), reads=[bBs], writes=[bcar])
        for c8 in range(0, NC_, 8):
            bank = 2 + (c8 // 8) % 2
            pv = cx.psbf(bank)[0:64, :].rearrange("p (c k) -> p c k", k=128)
            for cj in range(8):
                c = c8 + cj
                P.op("pe", lambda e, pv=pv, cj=cj, c=c: e.transpose(pv[:, cj, :], kt[:, c * 64:(c + 1) * 64], cx.ident[:]),
                     reads=[bkt, cx.bconst], writes=[cx.psb[bank]], sig=(cj == 7))
            P.op("dve", lambda e, pv=pv, c8=c8: e.tensor_copy(out=ktok[0:64, c8:c8 + 8, :], in_=pv), reads=[cx.psb[bank]], writes=[bktok])
        P.op("pool", lambda e: e.memset(S, 0.0), writes=[bS])
        P.op("pool", lambda e: e.memset(Sb, 0.0), writes=[bSb])
        ebv = eb.rearrange("p (c t) -> p c t", t=64)
        for c in range(NC_):
            cs = slice(c * 64, (c + 1) * 64)
            k2 = c % 2
            sbank = 4 + k2
            P.op("pe", lambda e, sbank=sbank, cs=cs: e.matmul(cx.ps[sbank][0:64, 0:64], lhsT=kt[:, cs], rhs=qt[:, cs], start=True, stop=True),
                 reads=[bkt, bqt], writes=[cx.psb[sbank]])
            P.op("dve", lambda e, sbank=sbank, k2=k2: e.tensor_tensor(out=scm[k2][0:64, :], in0=cx.ps[sbank][0:64, 0:64], in1=mut[0:64, :], op=ALU.mult),
                 reads=[cx.psb[sbank], bmut], writes=[bscm[k2]])
            pbank = 2 + k2
            P.op("pe", lambda e, pbank=pbank, c=c: e.matmul(cx.ps[pbank][:, 0:128], lhsT=ktok[0:64, c, :], rhs=vtok[0:64, c, :], start=True, stop=True),
                 reads=[bktok, bvtok], writes=[cx.psb[pbank]])
            obank = 6 + (c // 8) % 2
            oc = slice((c % 8) * 64, (c % 8 + 1) * 64)
            P.op("pe", lambda e, obank=obank, oc=oc, cs=cs: e.matmul(cx.ps[obank][:, oc], lhsT=Sb, rhs=qt2[:, cs], start=True, stop=False),
                 reads=[bSb, bqt2], writes=[cx.psb[obank]], sig=False)
            P.op("pe", lambda e, obank=obank, oc=oc, c=c, k2=k2: e.matmul(cx.ps[obank][:, oc], lhsT=vtok[0:64, c, :], rhs=scm[k2][0:64, :], start=False, stop=True),
                 reads=[bvtok, bscm[k2]], writes=[cx.psb[obank]])
            if c % 8 == 7 or c == NC_ - 1:
                c0 = (c // 8) * 8
                w = (c - c0 + 1) * 64
                P.op("act", lambda e, obank=obank, c0=c0, w=w: e.activation(out=oloc[:, c0 * 64: c0 * 64 + w], in_=cx.ps[obank][:, 0:w], func=AF.Copy),
                     reads=[cx.psb[obank]], writes=[boloc])
            ep = ebv[:, max(c - 1, 0), 63:64]
            P.op("dve", lambda e, pbank=pbank, ep=ep: e.scalar_tensor_tensor(out=S, in0=S, scalar=ep, in1=cx.ps[pbank][:, 0:128], op0=ALU.mult, op1=ALU.add),
                 reads=[bS, beb, cx.psb[pbank]], writes=[bS])
            P.op("act", lambda e: e.activation(out=Sb, in_=S, func=AF.Copy), reads=[bS], writes=[bSb])
        P.op("dve", lambda e: e.tensor_scalar(out=S, in0=S, scalar1=ebv[:, NC_ - 1, 63:64], scalar2=None, op0=ALU.mult), reads=[bS, beb], writes=[bS])
        P.op("sp", lambda e, h=h: e.dma_start(out=outs["ol"][h], in_=oloc), reads=[boloc], semkey="ool")
        P.op("dve", lambda e, h=h: e.tensor_copy(out=car[:, 24 + h * 128: 24 + (h + 1) * 128], in_=S), reads=[bS], writes=[bcar])
    o = P.op("sp", lambda e: e.dma_start(out=outs["car"], in_=car), reads=[bcar], semkey="ocar")
    cx.out_ops.append(o)
    A.reset(m0)


def proj_tm_residual(cx, actT, bactT, w, res_in, out, T):
    P = cx.P
    A = cx.arena
    wv = w.rearrange("(kc p) n -> p kc n", p=128)
    wt = A.bf16(16 * D).rearrange("p (c n) -> p c n", n=D); bwt = [P.buf(f"wres{dq}") for dq in range(4)]
    for dq in range(4):
        P.op("poolq", lambda e, dq=dq: e.dma_start(out=wt[:, :, dq * 512:(dq + 1) * 512], in_=wv[:, :, dq * 512:(dq + 1) * 512]), writes=[bwt[dq]], semkey=f"wres{dq}")
    xt = [A.f32(D) for _ in range(2)]; bxt = [P.buf("rx0"), P.buf("rx1")]
    for i in range(T // 128):
        s = i % 2
        P.op(cx.hwq(), lambda e, s=s, i=i: e.dma_start(out=xt[s], in_=res_in[i * 128:(i + 1) * 128, :]), writes=[bxt[s]], semkey=f"rx{s}")
        for dq in range(4):
            bank = (i % 2) * 4 + dq
            for c in range(16):
                P.op("pe", lambda e, bank=bank, c=c, i=i, dq=dq: e.matmul(cx.ps[bank][:], lhsT=actT[:, c, i * 128:(i + 1) * 128], rhs=wt[:, c, dq * 512:(dq + 1) * 512],
                                                                          start=(c == 0), stop=(c == 15)), reads=[bactT, bwt[dq]], writes=[cx.psb[bank]], sig=(c == 15))
            dst = xt[s][:, dq * 512:(dq + 1) * 512]
            P.op("dve", lambda e, dst=dst, bank=bank: e.tensor_tensor(out=dst, in0=cx.ps[bank][:], in1=dst, op=ALU.add), reads=[cx.psb[bank], bxt[s]], writes=[bxt[s]])
        P.op(cx.hwq(), lambda e, s=s, i=i: e.dma_start(out=out[i * 128:(i + 1) * 128, :], in_=xt[s]), reads=[bxt[s]], semkey=f"ro{s}")


def stage_r2(cx, T, car_all, cmask, ins, x, vecs, w_out, h1_out, NR=8):
    P = cx.P
    A = cx.arena
    m0 = A.mark()
    nT5 = (T + 511) // 512
    vec = A.f32(NVEC); bvec = P.buf("vec")
    P.op("sp", lambda e: e.dma_start(out=vec, in_=vecs), writes=[bvec], semkey="vec")
    epsb = A.f32(1)
    P.op("pool", lambda e: e.memset(epsb, EPS), writes=[bvec])
    cm = A.f32(NR); bcm = P.buf("cm")
    P.op("sp", lambda e: e.dma_start(out=cm, in_=cmask), writes=[bcm], semkey="cm")
    mixT = A.bf16(16 * T).rearrange("p (c t) -> p c t", t=T); bmix = P.buf("mixT")
    hc = A.f32(8); bhc = P.buf("hc")
    Sc = A.f32(1024); bSc = P.buf("Sc")
    Scb = A.bf16(1024); bScb = P.buf("Scb")
    mk = A.mark()
    CW = 24 + 1024
    cars = A.f32(NR * CW).rearrange("p (r c) -> p r c", c=CW); bcars = P.buf("cars")
    P.op("sp", lambda e: e.dma_start(out=cars, in_=car_all.rearrange("r p c -> p r c")), writes=[bcars], semkey="cars")
    th = A.f32(8); bth = P.buf("th")
    tS = A.f32(1024); btS = P.buf("tS")
    P.op("pool", lambda e: e.memset(hc, 0.0), writes=[bhc])
    P.op("pool", lambda e: e.memset(Sc, 0.0), writes=[bSc])
    Sc3 = Sc.rearrange("p (h v) -> p h v", v=128)
    tS3 = tS.rearrange("p (h v) -> p h v", v=128)
    for r in range(NR):
        cr = cars[:, r, :]
        mr = cm[:, r:r + 1]
        P.op("dve", lambda e, cr=cr: e.tensor_tensor(out=th, in0=hc, in1=cr[:, 8:16], op=ALU.mult), reads=[bhc, bcars], writes=[bth])
        P.op("dve", lambda e, cr=cr: e.tensor_tensor(out=th, in0=th, in1=cr[:, 0:8], op=ALU.add), reads=[bth, bcars], writes=[bth])
        P.op("dve", lambda e: e.tensor_tensor(out=th, in0=th, in1=hc, op=ALU.subtract), reads=[bth, bhc], writes=[bth])
        P.op("dve", lambda e, mr=mr: e.scalar_tensor_tensor(out=hc, in0=th, scalar=mr, in1=hc, op0=ALU.mult, op1=ALU.add), reads=[bth, bhc, bcm], writes=[bhc])
        P.op("dve", lambda e, cr=cr: e.tensor_tensor(out=tS3, in0=Sc3, in1=cr[:, 16:24].unsqueeze(2).to_broadcast([128, 8, 128]), op=ALU.mult),
             reads=[bSc, bcars], writes=[btS])
        P.op("dve", lambda e, cr=cr: e.tensor_tensor(out=tS, in0=tS, in1=cr[:, 24:CW], op=ALU.add), reads=[btS, bcars], writes=[btS])
        P.op("dve", lambda e: e.tensor_tensor(out=tS, in0=tS, in1=Sc, op=ALU.subtract), reads=[btS, bSc], writes=[btS])
        P.op("dve", lambda e, mr=mr: e.scalar_tensor_tensor(out=Sc, in0=tS, scalar=mr, in1=Sc, op0=ALU.mult, op1=ALU.add), reads=[btS, bSc, bcm], writes=[bSc])
    P.op("act", lambda e: e.activation(out=Scb, in_=Sc, func=AF.Copy), reads=[bSc], writes=[bScb])
    A.reset(mk)
    ra = [A.f32(T) for _ in range(2)]; bra = [P.buf("ra0"), P.buf("ra1")]
    ru = [A.f32(T) for _ in range(2)]; bru = [P.buf("ru0"), P.buf("ru1")]
    rg = [A.f32(T) for _ in range(2)]; brg = [P.buf("rg0"), P.buf("rg1")]
    rq = [A.bf16(T) for _ in range(2)]; brq = [P.buf("rq0"), P.buf("rq1")]
    hs = A.f32(T); bhs = P.buf("hs")
    for j in range(8):
        s = j % 2
        P.op("sp", lambda e, s=s, j=j: e.dma_start(out=ra[s], in_=ins["a"][j]), writes=[bra[s]], semkey=f"ra{s}")
        P.op("actq", lambda e, s=s, j=j: e.dma_start(out=ru[s], in_=ins["u"][j]), writes=[bru[s]], semkey=f"ru{s}")
        P.op("sp", lambda e, s=s, j=j: e.dma_start(out=rg[s], in_=ins["gy"][j]), writes=[brg[s]], semkey=f"rg{s}")
        P.op("dve", lambda e, s=s, j=j: e.tensor_tensor_scan(out=hs, data0=ra[s], data1=ru[s], initial=hc[:, j:j + 1], op0=ALU.mult, op1=ALU.add),
             reads=[bra[s], bru[s], bhc], writes=[bhs])
        P.op("dve", lambda e, s=s, j=j: e.tensor_tensor(out=mixT[:, j, :], in0=hs, in1=rg[s], op=ALU.mult), reads=[bhs, brg[s]], writes=[bmix])
    osq = [A.f32(512) for _ in range(4)]; bosq = [P.buf(f"osq{i}") for i in range(4)]
    rsd = [A.f32(512) for _ in range(4)]; brsd = [P.buf(f"rsd{i}") for i in range(4)]
    assert nT5 <= 4
    for h in range(8):
        s = h % 2
        P.op("sp", lambda e, s=s, h=h: e.dma_start(out=ra[s], in_=ins["ol"][h]), writes=[bra[s]], semkey=f"ra{s}")
        P.op("actq", lambda e, s=s, h=h: e.dma_start(out=rg[s], in_=ins["sg"][h]), writes=[brg[s]], semkey=f"rg{s}")
        P.op("sp", lambda e, s=s, h=h: e.dma_start(out=rq[s], in_=ins["qh"][h]), writes=[brq[s]], semkey=f"rq{s}")
        tl = []
        for n in range(nT5):
            tn = min(512, T - n * 512)
            tl.append((n, tn, slice(n * 512, n * 512 + tn)))
        for n, tn, sl in tl:
            P.op("pe", lambda e, n=n, h=h, s=s, sl=sl, tn=tn: e.matmul(cx.ps[n][:, 0:tn], lhsT=Scb[:, h * 128:(h + 1) * 128], rhs=rq[s][:, sl], start=True, stop=True),
                 reads=[bScb, brq[s]], writes=[cx.psb[n]])
            o_ = ra[s][:, sl]
            P.op("dve", lambda e, n=n, o_=o_, tn=tn: e.tensor_tensor(out=o_, in0=cx.ps[n][:, 0:tn], in1=o_, op=ALU.add), reads=[cx.psb[n], bra[s]], writes=[bra[s]])
        for n, tn, sl in tl:
            o_ = ra[s][:, sl]; q_ = osq[n][:, 0:tn]
            P.op("act", lambda e, o_=o_, q_=q_: e.activation(out=q_, in_=o_, func=AF.Square), reads=[bra[s]], writes=[bosq[n]])
            P.op("pe", lambda e, n=n, q_=q_, tn=tn: e.matmul(cx.ps[4 + n][:, 0:tn], lhsT=cx.ones_f[:], rhs=q_, start=True, stop=True),
                 reads=[cx.bconst, bosq[n]], writes=[cx.psb[4 + n]])
        for n, tn, sl in tl:
            r_ = rsd[n][:, 0:tn]
            P.op("act", lambda e, n=n, r_=r_, tn=tn: e.activation(out=r_, in_=cx.ps[4 + n][:, 0:tn], func=AF.Sqrt, scale=1.0 / 128, bias=epsb[:, 0:1]),
                 reads=[cx.psb[4 + n], bvec], writes=[brsd[n]])
            P.op("dve", lambda e, r_=r_: e.reciprocal(out=r_, in_=r_), reads=[brsd[n]], writes=[brsd[n]])
        for n, tn, sl in tl:
            o_ = ra[s][:, sl]; r_ = rsd[n][:, 0:tn]
            P.op("pool", lambda e, r_=r_, o_=o_: e.tensor_tensor(out=r_, in0=o_, in1=r_, op=ALU.mult), reads=[brsd[n], bra[s]], writes=[brsd[n]])
            P.op("dve", lambda e, r_=r_, h=h, s=s, sl=sl: e.scalar_tensor_tensor(out=mixT[:, 8 + h, sl], in0=r_, scalar=vec[:, V_GN + h:V_GN + h + 1], in1=rg[s][:, sl],
                                                                               op0=ALU.mult, op1=ALU.mult), reads=[brsd[n], bvec, brg[s]], writes=[bmix])
    A.reset(mk)
    proj_tm_residual(cx, mixT, bmix, w_out, x, h1_out, T)
    A.reset(m0)


PI = 3.141592653589793


def stage_qkv(cx, T, h_in, gain, w_qkv, pos, rconst, outs, gather=None, after_q=None):
    P = cx.P
    A = cx.arena
    m0 = A.mark()
    nT5 = (T + 511) // 512
    gain_bc = A.f32(D); bgain = P.buf("gain")
    P.op("sp", lambda e: e.dma_start(out=gain_bc, in_=gain.partition_broadcast(128)), writes=[bgain], semkey="gain")
    rc = A.f32(36); brc = P.buf("rc")
    P.op("sp", lambda e: e.dma_start(out=rc[0:32, :], in_=rconst), writes=[brc], semkey="rc")
    cosT = A.f32(T); sinT = A.f32(T); btab = P.buf("tab")
    hnT = A.bf16(16 * T).rearrange("p (c t) -> p c t", t=T); bhnT = P.buf("hnT")
    mk1 = A.mark()
    posi = A.f32(T).bitcast(mybir.dt.int32); bpos = P.buf("pos")
    P.op("sp", lambda e: e.dma_start(out=posi[0:32, :], in_=pos.partition_broadcast(32)), writes=[bpos], semkey="pos")
    ang = A.f32(T); bang = P.buf("ang")
    tq = A.f32(T); btq = P.buf("tq")
    P.op("dve", lambda e: e.tensor_copy(out=ang[0:32, :], in_=posi[0:32, :]), reads=[bpos], writes=[bang])
    P.op("dve", lambda e: e.tensor_scalar(out=ang[0:32, :], in0=ang[0:32, :], scalar1=rc[0:32, 0:1], scalar2=None, op0=ALU.mult), reads=[bang, brc], writes=[bang])
    ti = A.f32(T).bitcast(mybir.dt.int32); bti = P.buf("ti")
    tf = A.f32(T); btf = P.buf("tf")

    def sin_table(dst, off):
        a32, q32, i32, f32_ = ang[0:32, :], tq[0:32, :], ti[0:32, :], tf[0:32, :]
        P.op("dve", lambda e: e.tensor_scalar(out=q32, in0=a32, scalar1=1.0 / (2 * PI), scalar2=off, op0=ALU.mult, op1=ALU.add), reads=[bang, btq], writes=[btq])
        P.op("dve", lambda e: e.tensor_copy(out=i32, in_=q32), reads=[btq], writes=[bti])
        P.op("dve", lambda e: e.tensor_copy(out=f32_, in_=i32), reads=[bti], writes=[btf])
        P.op("dve", lambda e: e.tensor_tensor(out=q32, in0=q32, in1=f32_, op=ALU.subtract), reads=[btq, btf], writes=[btq])
        P.op("dve", lambda e: e.tensor_scalar(out=f32_, in0=q32, scalar1=0.5, scalar2=None, op0=ALU.is_gt), reads=[btq, btf], writes=[btf])
        P.op("dve", lambda e: e.tensor_tensor(out=q32, in0=q32, in1=f32_, op=ALU.subtract), reads=[btq, btf], writes=[btq])
        P.op("dve", lambda e: e.tensor_scalar(out=f32_, in0=q32, scalar1=-0.5, scalar2=None, op0=ALU.is_lt), reads=[btq, btf], writes=[btf])
        P.op("dve", lambda e: e.tensor_tensor(out=q32, in0=q32, in1=f32_, op=ALU.add), reads=[btq, btf], writes=[btq])
        P.op("dve", lambda e: e.tensor_scalar(out=q32, in0=q32, scalar1=-0.49999, scalar2=0.49999, op0=ALU.max, op1=ALU.min), reads=[btq], writes=[btq])
        P.op("act", lambda e: e.activation(out=dst[0:32, :], in_=q32, func=AF.Sin, scale=2 * PI), reads=[btq], writes=[btab])

    sin_table(sinT, 0.0)
    P.op("dve", lambda e: e.tensor_scalar(out=sinT[0:32, :], in0=sinT[0:32, :], scalar1=rc[0:32, 1:2], scalar2=None, op0=ALU.mult), reads=[btab, brc], writes=[btab])
    sin_table(cosT, 0.25)
    xt = [A.f32(D) for _ in range(4)]; bxt = [P.buf(f"xt{i}") for i in range(4)]
    hn_tmp = [A.bf16(D) for _ in range(2)]; bhn_tmp = [P.buf("hntmp0"), P.buf("hntmp1")]
    stat = A.f32(16); bstat = P.buf("stat")
    def get_tile_qkv(i):
        s = i % 4
        P.op(cx.hwq(), lambda e: e.dma_start(out=xt[s], in_=h_in[i * 128:(i + 1) * 128, :]), writes=[bxt[s]], semkey=f"xt{s}")
        return xt[s], bxt[s]

    rmsnorm_pipe(cx, T // 128, get_tile_qkv, gain_bc, bgain, hnT, bhnT, hn_tmp, bhn_tmp, stat, bstat, col0=0)
    A.reset(mk1)
    wv = w_qkv.rearrange("(kc p) n -> p kc n", p=128)
    ws = WStream(cx, "wqkv", 2, 16 * 256)
    rows = [A.bf16(T) for _ in range(2)]; brows = [P.buf("orow0"), P.buf("orow1")]
    qf = [A.f32(512) for _ in range(2)]; bqf = [P.buf("qf0"), P.buf("qf1")]
    t1 = [A.f32(512) for _ in range(2)]; bt1 = [P.buf("t10"), P.buf("t11")]
    t2 = [A.f32(512) for _ in range(2)]; bt2 = [P.buf("t20"), P.buf("t21")]
    cnt = [0]
    names = ["q", "k", "v"]

    which_box = [0, None, None]
    deferred = []

    def run_deferred():
        while deferred:
            deferred.pop(0)()
    bkv = [P.buf(f"kvd{h}") for h in range(16)]

    def consume(m, n, tn, bank):
        which = which_box[0]
        hh = m % 16 if which_box[1] is None else which_box[1]
        ri = m % 2 if which_box[2] is None else which_box[2]
        row, brow = rows[ri], brows[ri]
        sl = slice(n * 512, n * 512 + tn)
        ps = cx.ps[bank][:, 0:tn]; bps = cx.psb[bank]
        if which == 2:
            run_deferred()
            P.op("act", lambda e: e.activation(out=row[:, sl], in_=ps, func=AF.Copy), reads=[bps], writes=[brow])
        else:
            k = cnt[0] % 2; cnt[0] += 1
            q_ = qf[k][:, 0:tn]
            sc = (128 ** -0.5) if which == 0 else 1.0
            run_deferred()
            P.op("act", lambda e: e.activation(out=q_, in_=ps, func=AF.Copy, scale=sc), reads=[bps], writes=[bqf[k]])
            deferred.append(lambda: rope_part(which, hh, ri, row, brow, sl, tn, n, k, q_))
            return
        finish_row(which, hh, ri, row, brow, n)

    def rope_part(which, hh, ri, row, brow, sl, tn, n, k, q_):
        if True:
            b2 = 2 + k
            P.op("pe", lambda e: e.matmul(cx.ps[b2][0:32, 0:tn], lhsT=rc[0:32, 4:36], rhs=q_[0:32, :], start=True, stop=True), reads=[brc, bqf[k]], writes=[cx.psb[b2]])
            a_ = t1[k][0:32, 0:tn]; b_ = t2[k][0:32, 0:tn]
            P.op("dve", lambda e: e.tensor_tensor(out=a_, in0=q_[0:32, :], in1=cosT[0:32, sl], op=ALU.mult), reads=[bqf[k], btab], writes=[bt1[k]])
            P.op("dve", lambda e: e.tensor_tensor(out=b_, in0=cx.ps[b2][0:32, 0:tn], in1=sinT[0:32, sl], op=ALU.mult), reads=[cx.psb[b2], btab], writes=[bt2[k]])
            P.op("act", lambda e: e.activation(out=row[:, sl], in_=q_, func=AF.Copy), reads=[bqf[k]], writes=[brow])
            P.op("dve", lambda e: e.tensor_tensor(out=row[0:32, sl], in0=a_, in1=b_, op=ALU.add), reads=[bt1[k], bt2[k]], writes=[brow])
        finish_row(which, hh, ri, row, brow, n)

    def finish_row(which, hh, ri, row, brow, n):
        if n == nT5 - 1:
            if which == 0:
                P.op("sp", lambda e: e.dma_start(out=outs["q"][hh], in_=row), reads=[brow], semkey=f"oqkv{ri}")
                if after_q is not None:
                    after_q(hh)
            else:
                r0 = 0 if which == 1 else 128
                P.op("sp", lambda e: e.dma_start(out=outs["kv"][hh][r0:r0 + 128, :], in_=row), reads=[brow], writes=[bkv[hh]], semkey=f"oqkv{ri}")
                if which == 2 and gather is not None:
                    pending.append(hh)

    pending = []

    def flush(keep=0):
        while len(pending) > keep:
            hh = pending.pop(0)
            gather(hh, bkv[hh])

    ws2 = WStream(cx, "wkv", 3, 16 * 512)
    cnt2 = 0
    for hp in range(8):
        s_ = ws2.i % ws2.n
        ws2.i += 1
        wt = ws2.slots[s_].rearrange("p (c n) -> p c n", n=512); bw = ws2.bufs[s_]
        P.op("poolq", lambda e, wt=wt, hp=hp: e.dma_start(out=wt[:, :, 0:256], in_=wv[:, :, 2048 + hp * 256: 2048 + (hp + 1) * 256]), writes=[bw], semkey=f"wkv{s_}")
        P.op("poolq", lambda e, wt=wt, hp=hp: e.dma_start(out=wt[:, :, 256:512], in_=wv[:, :, 4096 + hp * 256: 4096 + (hp + 1) * 256]), writes=[bw], semkey=f"wkv{s_}")
        flush(keep=2)
        for hl in range(2):
            hh = 2 * hp + hl
            for which, mm in ((1, 0), (2, 1)):
                which_box[0], which_box[1], which_box[2] = which, hh, mm
                c0 = mm * 256 + hl * 128
                for n in range(nT5):
                    tn = min(512, T - n * 512)
                    bank = cnt2 % 2
                    cnt2 += 1
                    for kc in range(16):
                        P.op("pe", lambda e, bank=bank, wt=wt, kc=kc, c0=c0, n=n, tn=tn: e.matmul(
                            cx.ps[bank][:, 0:tn], lhsT=wt[:, kc, c0:c0 + 128], rhs=hnT[:, kc, n * 512: n * 512 + tn],
                            start=(kc == 0), stop=(kc == 15)), reads=[bw, bhnT], writes=[cx.psb[bank]], sig=(kc == 15))
                    consume(hh, n, tn, bank)
    which_box[0], which_box[1], which_box[2] = 0, None, None
    proj_fm(cx, hnT, bhnT, 0, T, wv, 0, 2048, ws, consume, wtile=256, after_load=flush)
    run_deferred()
    flush()
    A.reset(m0)


DILS = (1, 4, 16)


def stage_attn(cx, T, HL, qT, kv, kvg, bkvg, flag, w_o, res_in, out):
    P = cx.P
    A = cx.arena
    m0 = A.mark()
    assert T % 2048 == 0 and HL == 2048
    TK = HL + T
    attnT = A.bf16(16 * T).rearrange("p (c t) -> p c t", t=T); battn = P.buf("attnT")
    mk = A.mark()
    mf = A.f32(256); bm = P.buf("mask")
    mk_n = A.bf16(256); mk_h = A.bf16(256)
    fl = A.f32(1)
    P.op("sp", lambda e: e.dma_start(out=fl, in_=flag), writes=[bm], semkey="flag")
    P.op("pool", lambda e: e.memset(mf, 1.0), reads=[bm], writes=[bm])
    P.op("pool", lambda e: e.affine_select(out=mf[:, 0:128], in_=mf[:, 0:128], pattern=[[-1, 128]], compare_op=ALU.is_ge, fill=0.0, base=0, channel_multiplier=1),
         reads=[bm], writes=[bm])
    P.op("pool", lambda e: e.affine_select(out=mf[:, 128:256], in_=mf[:, 128:256], pattern=[[1, 128]], compare_op=ALU.is_ge, fill=0.0, base=0, channel_multiplier=-1),
         reads=[bm], writes=[bm])
    P.op("dve", lambda e: e.tensor_scalar(out=mk_n, in0=mf, scalar1=30000.0, scalar2=-30000.0, op0=ALU.mult, op1=ALU.add), reads=[bm], writes=[bm])
    P.op("dve", lambda e: e.tensor_scalar(out=mf[:, 0:128], in0=mf[:, 0:128], scalar1=fl[:, 0:1], scalar2=None, op0=ALU.mult), reads=[bm], writes=[bm])
    P.op("dve", lambda e: e.tensor_scalar(out=mk_h, in0=mf, scalar1=30000.0, scalar2=-30000.0, op0=ALU.mult, op1=ALU.add), reads=[bm], writes=[bm])
    Q = [A.bf16(T) for _ in range(2)]; bQ = [P.buf("Q0"), P.buf("Q1")]
    K = [A.bf16(TK) for _ in range(2)]; bK = [P.buf("K0"), P.buf("K1")]
    V = [A.bf16(TK) for _ in range(2)]; bV = [P.buf("V0"), P.buf("V1")]
    sqb = A.bf16(TK); bsq = P.buf("sq")
    NVB = 17 + 20 + 32
    Vord = A.bf16(NVB * 128).rearrange("p (b d) -> p b d", d=128); bVo = P.buf("Vord")
    acc_o = A.f32(T); bao = P.buf("acc_o")
    acc_d = A.f32(T); bad = P.buf("acc_d")
    rows = A.f32(TK); brow = P.buf("nrow")
    biasC = A.f32(1); bbias = P.buf("biasC")
    kmx = A.f32(2); bkmx = P.buf("kmx")
    PT = [A.bf16(256) for _ in range(4)]; bPT = [P.buf(f"PT{i}") for i in range(4)]
    bsc = [P.buf(f"sc{i}") for i in range(4)]
    uid = [0]

    def load_head(h):
        s = h % 2
        P.op("sp", lambda e: e.dma_start(out=Q[s], in_=qT[h]), writes=[bQ[s]], semkey=f"Q{s}")
        P.op("actq", lambda e: e.dma_start(out=K[s][:, 0:HL], in_=kvg[h][0:128, :]), reads=[bkvg[h]], writes=[bK[s]], semkey=f"K{s}")
        P.op("sp", lambda e: e.dma_start(out=K[s][:, HL:TK], in_=kv[h][0:128, :]), writes=[bK[s]], semkey=f"K{s}")
        P.op("actq", lambda e: e.dma_start(out=V[s][:, 0:HL], in_=kvg[h][128:256, :]), reads=[bkvg[h]], writes=[bV[s]], semkey=f"V{s}")
        P.op("sp", lambda e: e.dma_start(out=V[s][:, HL:TK], in_=kv[h][128:256, :]), writes=[bV[s]], semkey=f"V{s}")

    def key_cols(d, r, blk):
        start = HL + r + d * 128 * blk
        return slice(start, start + d * 127 + 1, d)

    load_head(0)
    for h in range(16):
        s = h % 2
        if h + 1 < 16:
            load_head(h + 1)
        Qh, Kh, Vh = Q[s], K[s], V[s]
        P.op("act", lambda e, Kh=Kh: e.activation(out=sqb, in_=Kh, func=AF.Square), reads=[bK[s]], writes=[bsq])
        for n in range(TK // 512):
            bank = n % 2
            P.op("pe", lambda e, bank=bank, n=n: e.matmul(cx.ps[bank][:, :], lhsT=cx.ones_bf[:, :], rhs=sqb[:, n * 512:(n + 1) * 512], start=True, stop=True),
                 reads=[bsq, cx.bconst], writes=[cx.psb[bank]])
            P.op("dve", lambda e, bank=bank, n=n: e.tensor_copy(out=rows[:, n * 512:(n + 1) * 512], in_=cx.ps[bank][:, :]), reads=[cx.psb[bank]], writes=[brow])
        P.op("dve", lambda e: e.tensor_reduce(out=kmx[:, 0:1], in_=rows[:, 0:TK], axis=AX.X, op=ALU.max), reads=[brow], writes=[bkmx])
        P.op("act", lambda e, Qh=Qh: e.activation(out=sqb[:, 0:T], in_=Qh, func=AF.Square), reads=[bQ[s], bsq], writes=[bsq])
        for n in range(T // 512):
            bank = n % 2
            P.op("pe", lambda e, bank=bank, n=n: e.matmul(cx.ps[bank][:, :], lhsT=cx.ones_bf[:, :], rhs=sqb[:, n * 512:(n + 1) * 512], start=True, stop=True),
                 reads=[bsq, cx.bconst], writes=[cx.psb[bank]])
            P.op("dve", lambda e, bank=bank, n=n: e.tensor_copy(out=rows[:, n * 512:(n + 1) * 512], in_=cx.ps[bank][:, :]), reads=[cx.psb[bank], brow], writes=[brow])
        P.op("dve", lambda e: e.tensor_reduce(out=kmx[:, 1:2], in_=rows[:, 0:T], axis=AX.X, op=ALU.max), reads=[brow], writes=[bkmx])
        P.op("dve", lambda e: e.tensor_scalar(out=kmx[:, 1:2], in0=kmx[:, 1:2], scalar1=kmx[:, 0:1], scalar2=1.0404, op0=ALU.mult, op1=ALU.mult), reads=[bkmx], writes=[bkmx])
        P.op("act", lambda e: e.activation(out=kmx[:, 1:2], in_=kmx[:, 1:2], func=AF.Sqrt), reads=[bkmx], writes=[bkmx])
        P.op("dve", lambda e: e.tensor_scalar(out=kmx[:, 1:2], in0=kmx[:, 1:2], scalar1=-1.0, scalar2=None, op0=ALU.mult), reads=[bkmx], writes=[bkmx])
        P.op("dve", lambda e: e.tensor_copy(out=biasC[:, 0:1], in_=kmx[:, 1:2]), reads=[bkmx], writes=[bbias])
        vblocks = {}
        lst = []
        for d in DILS:
            for r in range(d):
                for blk in range(-1, T // (128 * d)):
                    vblocks[(d, r, blk)] = len(lst)
                    lst.append((d, r, blk))
        assert len(lst) == NVB
        for g0 in range(0, NVB, 8):
            bank = 2 + (g0 // 8) % 2
            pv = cx.psbf(bank).rearrange("p (c t) -> p c t", t=128)
            ng = min(8, NVB - g0)
            for gi in range(ng):
                d, r, blk = lst[g0 + gi]
                P.op("pe", lambda e, pv=pv, gi=gi, cols=key_cols(d, r, blk), Vh=Vh: e.transpose(pv[:, gi, :], Vh[:, cols], cx.ident[:]),
                     reads=[bV[s], cx.bconst], writes=[cx.psb[bank]], sig=(gi == ng - 1))
            if (g0 // 8) % 2:
                P.op("act", lambda e, pv=pv, g0=g0, ng=ng: e.activation(out=Vord[:, g0:g0 + ng, :], in_=pv[:, 0:ng, :], func=AF.Copy), reads=[cx.psb[bank]], writes=[bVo])
            else:
                P.op("dve", lambda e, pv=pv, g0=g0, ng=ng: e.tensor_copy(out=Vord[:, g0:g0 + ng, :], in_=pv[:, 0:ng, :]), reads=[cx.psb[bank]], writes=[bVo])
        ulist = []
        for di, d in enumerate(DILS):
            nb = T // (128 * d)
            groups = []
            if d == 1:
                for b0 in range(0, nb, 4):
                    groups.append(([(0, b0 + i) for i in range(4)], lambda acc, b0=b0: acc[:, b0 * 128:(b0 + 4) * 128]))
            elif d == 4:
                for r in range(4):
                    groups.append(([(r, b) for b in range(4)], lambda acc, r=r: acc[:, r:T:4]))
            else:
                for g in range(4):
                    groups.append(([(4 * g + i, 0) for i in range(4)],
                                   lambda acc, g=g: acc.rearrange("p (m r) -> p r m", r=16)[:, 4 * g:4 * g + 4, :]))
            for gi, (units, dview) in enumerate(groups):
                for ui, (r, blk) in enumerate(units):
                    ulist.append(dict(di=di, d=d, gi=gi, ui=ui, r=r, blk=blk, dview=dview, last=(ui == len(units) - 1)))
        gcount = [0]

        def emit_scores(U):
            u = uid[0]; uid[0] += 1
            d, r, blk = U["d"], U["r"], U["blk"]
            slot = u % 4
            sb = slot; so = 0
            bss = cx.psb[slot]
            pt = PT[u % 4]; bpt = bPT[u % 4]
            U["pt"], U["bpt"] = pt, bpt
            qstart = r + d * 128 * blk
            qcols = slice(qstart, qstart + d * 127 + 1, d)
            ps_s = cx.ps[sb][:, so:so + 256]
            msk = mk_h if blk == 0 else mk_n
            P.op("pe", lambda e, sb=sb, so=so, msk=msk: e.matmul(cx.ps[sb][:, so:so + 256], lhsT=cx.ident[:], rhs=msk, start=True, stop=False),
                 reads=[bm, cx.bconst], writes=[bss], sig=False)
            for kb, kblk in enumerate((blk - 1, blk)):
                kc = key_cols(d, r, kblk)
                P.op("pe", lambda e, sb=sb, so=so, kb=kb, kc=kc, qcols=qcols, Kh=Kh, Qh=Qh: e.matmul(cx.ps[sb][:, so + kb * 128:so + (kb + 1) * 128], lhsT=Kh[:, kc], rhs=Qh[:, qcols],
                                                                                                start=False, stop=(kb == 1)),
                     reads=[bK[s], bQ[s]], writes=[bss], sig=(kb == 1))
            P.op("act", lambda e, pt=pt, ps_s=ps_s: e.activation(out=pt, in_=ps_s, func=AF.Exp, bias=biasC[:, 0:1]), reads=[bss, bbias], writes=[bpt])

        def emit_pv(U):
            d, r, blk, ui, di = U["d"], U["r"], U["blk"], U["ui"], U["di"]
            pt, bpt = U["pt"], U["bpt"]
            g = gcount[0]
            ob = 4 + g % 2
            db = 6 + g % 2
            for kb, kblk in enumerate((blk - 1, blk)):
                vb = vblocks[(d, r, kblk)]
                P.op("pe", lambda e, ob=ob, ui=ui, vb=vb, pt=pt, kb=kb: e.matmul(cx.ps[ob][:, ui * 128:(ui + 1) * 128], lhsT=Vord[:, vb, :], rhs=pt[:, kb * 128:(kb + 1) * 128],
                                                                                start=(kb == 0), stop=(kb == 1)),
                     reads=[bVo, bpt], writes=[cx.psb[ob]], sig=False)
            for kb in range(2):
                P.op("pe", lambda e, db=db, ui=ui, pt=pt, kb=kb: e.matmul(cx.ps[db][:, ui * 128:(ui + 1) * 128], lhsT=cx.ones_bf[:], rhs=pt[:, kb * 128:(kb + 1) * 128],
                                                                         start=(kb == 0), stop=(kb == 1)),
                     reads=[cx.bconst, bpt], writes=[cx.psb[db]], sig=(kb == 1))
            if U["last"]:
                gcount[0] += 1
                dview = U["dview"]
                ov = dview(acc_o); dvw = dview(acc_d)
                pso = cx.ps[ob][:] if d != 16 else cx.ps[ob][:].rearrange("p (r m) -> p r m", m=128)
                psd = cx.ps[db][:] if d != 16 else cx.ps[db][:].rearrange("p (r m) -> p r m", m=128)
                if di == 0:
                    P.op("act", lambda e, ov=ov, pso=pso: e.activation(out=ov, in_=pso, func=AF.Copy), reads=[cx.psb[ob]], writes=[bao])
                    P.op("dve", lambda e, dvw=dvw, psd=psd: e.tensor_copy(out=dvw, in_=psd), reads=[cx.psb[db]], writes=[bad])
                else:
                    P.op("dve", lambda e, ov=ov, pso=pso: e.tensor_tensor(out=ov, in0=pso, in1=ov, op=ALU.add), reads=[cx.psb[ob], bao], writes=[bao])
                    P.op("dve", lambda e, dvw=dvw, psd=psd: e.tensor_tensor(out=dvw, in0=psd, in1=dvw, op=ALU.add), reads=[cx.psb[db], bad], writes=[bad])

        SK = 2
        for i in range(len(ulist) + SK):
            if i < len(ulist):
                emit_scores(ulist[i])
            if i >= SK:
                emit_pv(ulist[i - SK])
        P.op("dve", lambda e: e.reciprocal(out=acc_d, in_=acc_d), reads=[bad], writes=[bad])
        P.op("dve", lambda e, h=h: e.tensor_tensor(out=attnT[:, h, :], in0=acc_o, in1=acc_d, op=ALU.mult), reads=[bao, bad], writes=[battn])
    A.reset(mk)
    proj_tm_residual(cx, attnT, battn, w_o, res_in, out, T)
    A.reset(m0)


def mlp_block2(cx, h_in, h_out, gain_dram, w1, w2, T, final_gain=None):
    P = cx.P
    A = cx.arena
    m0 = A.mark()
    FF = 4 * D
    TT = 1024
    NTT = TT // 128
    NPART = 4
    FCP = 64 // NPART
    gain_bc = A.f32(D); bgain = P.buf("gain")
    P.op("sp", lambda e: e.dma_start(out=gain_bc, in_=gain_dram.partition_broadcast(128)), writes=[bgain], semkey="gain")
    if final_gain is not None:
        fg_bc = A.f32(D); bfg = P.buf("fgain")
        P.op("sp", lambda e: e.dma_start(out=fg_bc, in_=final_gain.partition_broadcast(128)), writes=[bfg], semkey="fgain")
    hres = [A.f32(D) for _ in range(NTT)]
    bres = [P.buf(f"hres{i}") for i in range(NTT)]
    hn_tmp = [A.bf16(D) for _ in range(2)]
    bhn_tmp = [P.buf("hntmp0"), P.buf("hntmp1")]
    stat = A.f32(4 * NTT); bstat = P.buf("stat")
    hnT = A.bf16(16 * TT).rearrange("p (c t) -> p c t", t=TT); bhnT = P.buf("hnT")
    aT = A.bf16(FCP * TT).rearrange("p (c t) -> p c t", t=TT)
    baT = [P.buf(f"aT{i}") for i in range(FCP)]
    sq = [A.f32(512) for _ in range(2)]; bsq = [P.buf("sq0"), P.buf("sq1")]
    W1C = 256
    w1s = WStream(cx, "w1s", 2, 16 * W1C)
    W2K = 8
    w2s = WStream(cx, "w2s", 2, W2K * 512)
    w1v = w1.rearrange("(kc p) n -> p kc n", p=128)
    w2v = w2.rearrange("(fc p) n -> p fc n", p=128)
    for blk in range(T // TT):
        t0 = blk * TT
        for i in range(NTT):
            P.op(cx.hwq(), lambda e, i=i, t0=t0: e.dma_start(out=hres[i], in_=h_in[t0 + i * 128: t0 + (i + 1) * 128, :]),
                 writes=[bres[i]], semkey=f"hres{i}")
        rmsnorm_pipe(cx, NTT, lambda i: (hres[i], bres[i]), gain_bc, bgain, hnT, bhnT, hn_tmp, bhn_tmp, stat, bstat)
        ei = 0
        for part in range(NPART):
            for g in range(FCP * 128 // W1C):
                c0 = part * FCP * 128 + g * W1C
                wt, bw = w1s.load(w1v[:, :, c0:c0 + W1C], lambda s: s.rearrange("p (c n) -> p c n", n=W1C))
                for mm in range(W1C // 128):
                    ml = g * (W1C // 128) + mm
                    for n in range(TT // 512):
                        bank = ei % 4
                        for kc in range(16):
                            P.op("pe", lambda e, bank=bank, wt=wt, kc=kc, mm=mm, n=n: e.matmul(cx.ps[bank][:], lhsT=wt[:, kc, mm * 128:(mm + 1) * 128],
                                                                                               rhs=hnT[:, kc, n * 512:(n + 1) * 512], start=(kc == 0), stop=(kc == 15)),
                                 reads=[bw, bhnT], writes=[cx.psb[bank]], sig=(kc == 15))
                        s = sq[ei % 2]; bs = bsq[ei % 2]
                        P.op("act", lambda e, s=s, bank=bank: e.activation(out=s, in_=cx.ps[bank][:], func=AF.Square), reads=[cx.psb[bank]], writes=[bs])
                        P.op("dve", lambda e, s=s, bank=bank, ml=ml, n=n: e.scalar_tensor_tensor(out=aT[:, ml, n * 512:(n + 1) * 512], in0=cx.ps[bank][:], scalar=0.0, in1=s,
                                                                                                 op0=ALU.is_gt, op1=ALU.mult),
                             reads=[cx.psb[bank], bs], writes=[baT[ml]])
                        ei += 1
            for dq in range(4):
                for fh in range(FCP // W2K):
                    f0 = part * FCP + fh * W2K
                    wt, bw = w2s.load(w2v[:, f0:f0 + W2K, dq * 512:(dq + 1) * 512], lambda s: s.rearrange("p (c n) -> p c n", n=512))
                    for tt in range(NTT):
                        for fl in range(W2K):
                            fc = fh * W2K + fl
                            P.op("pe", lambda e, tt=tt, fc=fc, fl=fl, wt=wt: e.matmul(cx.ps[tt][:], lhsT=aT[:, fc, tt * 128:(tt + 1) * 128], rhs=wt[:, fl, :],
                                                                                     start=(fc == 0), stop=(fc == FCP - 1)),
                                 reads=[bw, baT[fc]], writes=[cx.psb[tt]], sig=(fc == FCP - 1))
                for tt in range(NTT):
                    dst = hres[tt][:, dq * 512:(dq + 1) * 512]
                    P.op("dve", lambda e, dst=dst, tt=tt: e.tensor_tensor(out=dst, in0=cx.ps[tt][:], in1=dst, op=ALU.add),
                         reads=[cx.psb[tt], bres[tt]], writes=[bres[tt]])
        for i in range(NTT):
            src = hres[i]
            if final_gain is not None:
                st = stat[:, 0:4]
                junk = hn_tmp[0]
                P.op("act", lambda e, src=src, junk=junk, st=st: e.activation(out=junk, in_=src, func=AF.Square, accum_out=st[:, 0:1]),
                     reads=[bres[i]], writes=[bhn_tmp[0], bstat])
                P.op("dve", lambda e, st=st: e.tensor_scalar(out=st[:, 1:2], in0=st[:, 0:1], scalar1=1.0 / D, scalar2=EPS,
                                                            op0=ALU.mult, op1=ALU.add), reads=[bstat], writes=[bstat])
                P.op("act", lambda e, st=st: e.activation(out=st[:, 2:3], in_=st[:, 1:2], func=AF.Sqrt), reads=[bstat], writes=[bstat])
                P.op("dve", lambda e, st=st: e.reciprocal(out=st[:, 3:4], in_=st[:, 2:3]), reads=[bstat], writes=[bstat])
                P.op("dve", lambda e, src=src, st=st: e.scalar_tensor_tensor(out=src, in0=src, scalar=st[:, 3:4], in1=fg_bc,
                                                                            op0=ALU.mult, op1=ALU.mult),
                     reads=[bres[i], bstat, bfg], writes=[bres[i]])
            o = P.op(cx.hwq(), lambda e, i=i, t0=t0, src=src: e.dma_start(out=h_out[t0 + i * 128: t0 + (i + 1) * 128, :], in_=src),
                     reads=[bres[i]], semkey=f"hout{i}")
            cx.out_ops.append(o)
    A.reset(m0)


import ml_dtypes
from concourse.bass_utils import run_bass_kernel_spmd

NCORES = 8
TPC = 2048
HLK = 2048
NRK = 4
I32 = mybir.dt.int32
R1_OUTS = [("a", F32), ("u", F32), ("gy", F32), ("sg", F32), ("ol", F32), ("qh", BF16)]
CW = 24 + 1024
RG = [[0, 1, 2, 3], [4, 5, 6, 7]]


def _pm(v):
    return np.ascontiguousarray(np.asarray(v).reshape(8, 128).T)


def make_vecs(conv_w, conv_b, b_a, b_i, lam, lb_logits, g_norm):
    cols = [_pm(conv_w[j]) for j in range(4)] + [_pm(conv_b), _pm(b_a), _pm(b_i), _pm(lam)] + [_pm(lb_logits[k]) for k in range(3)] + [_pm(g_norm)]
    return np.ascontiguousarray(np.concatenate(cols, axis=1).astype(np.float32))


def make_rconst():
    half = 16
    inv = (1.0 / (500000.0 ** (np.arange(half, dtype=np.float32) * np.float32(2.0 / 32)))).astype(np.float32)
    rc = np.zeros((32, 36), np.float32)
    rc[:, 0] = np.concatenate([inv, inv]); rc[:16, 1] = -1; rc[16:, 1] = 1
    for m in range(32):
        rc[(m + 16) % 32, 4 + m] = 1
    return rc


def build_fused():
    nc = bass.Bass("TRN2", target_bir_lowering=False)
    T = TPC
    I = lambda n, s, dt=F32: nc.dram_tensor(n, list(s), dt, kind="ExternalInput").ap()
    N = lambda n, s, dt=F32: nc.dram_tensor(n, list(s), dt, kind="Internal").ap()
    x = I("x", [T, D]); xh = I("xh", [128, D]); gain0 = I("gain0", [D]); w_in = I("w_in", [D, 6144]); vecs = I("vecs", [128, NVEC])
    w_a = I("w_a", [4, 256, 256]); w_i = I("w_i", [4, 256, 256]); cmask = I("cmask", [128, NRK]); w_out = I("w_out", [D, D])
    gm0 = I("gmlp0", [D]); w1_0 = I("w1_0", [D, 4 * D]); w2_0 = I("w2_0", [4 * D, D])
    g1 = I("gain1", [D]); w_qkv = I("w_qkv", [D, 6144]); pos = I("pos", [T], I32); rc = I("rconst", [32, 36])
    flag = I("flag", [128, 1]); w_o = I("w_o", [D, D])
    gm1 = I("gmlp1", [D]); w1_1 = I("w1_1", [D, 4 * D]); w2_1 = I("w2_1", [4 * D, D]); fg = I("fgain", [D])
    out = nc.dram_tensor("out", [T, D], F32, kind="ExternalOutput").ap()
    r1 = {n: N("s_" + n, [8, 128, T], dt) for n, dt in R1_OUTS}
    r1["car"] = N("s_car", [128, CW])
    car_all = N("s_car_all", [NRK * 128, CW])
    h1 = N("s_h1", [T, D]); h2 = N("s_h2", [T, D]); h3 = N("s_h3", [T, D])
    q = N("s_q", [16, 128, T], BF16); kvg = N("s_kvg", [16, NRK * 256, T], BF16)
    kv = [N(f"s_kv{h}", [256, T], BF16) for h in range(16)]
    kvgs = [N(f"s_kvgs{h}", [NRK * 256, T], BF16) for h in range(16)]
    cx = Ctx(nc)
    P = cx.P
    stage_r1(cx, T, x, xh, gain0, w_in, vecs, w_a, w_i, r1)
    P.op("poolq", lambda e: e.collective_compute("AllGather", ALU.bypass, replica_groups=RG, ins=[r1["car"]], outs=[car_all]), semkey="cc_car", inc=1)
    P.barrier()
    stage_r2(cx, T, car_all.rearrange("(r p) c -> r p c", p=128), cmask, r1, x, vecs, w_out, h1, NR=NRK)
    mlp_block2(cx, h1, h2, gm0, w1_0, w2_0, T)
    bkvg = [P.buf(f"kvg{h}") for h in range(16)]

    def gather(hh, bkv_h):
        P.op("poolq", lambda e: e.collective_compute("AllGather", ALU.bypass, replica_groups=RG, ins=[kv[hh]], outs=[kvgs[hh]]),
             reads=[bkv_h], writes=[bkvg[hh]], semkey=f"cck{hh}", inc=1)

    def after_q(hh):
        P.op("sp", lambda e: e.dma_start(out=kvg[hh], in_=kvgs[hh]), reads=[bkvg[hh]], writes=[bkvg[hh]], semkey=f"kvgc{hh % 2}")

    stage_qkv(cx, T, h2, g1, w_qkv, pos, rc, {"q": q, "kv": kv}, gather=gather, after_q=after_q)
    halo = N("s_halo", [16, 256, T], BF16)
    bhalo = [P.buf(f"halo{h}") for h in range(16)]
    kvg4 = kvg.rearrange("h (r k) t -> h r k t", r=NRK)

    def halo_copy(e):
        prev = e.snap((e.partition_id() + (NRK - 1)) % NRK, min_val=0, max_val=NRK - 1)
        return e.dma_start(out=halo.rearrange("h (o k) t -> h o k t", o=1), in_=kvg4[:, bass.ds(prev, 1), :, :])

    P.op("sp", halo_copy, reads=bkvg, writes=bhalo, semkey="halo")
    stage_attn(cx, T, HLK, q, kv, halo, bhalo, flag, w_o, h2, h3)
    mlp_block2(cx, h3, out, gm1, w1_1, w2_1, T, final_gain=fg)
    P.barrier()
    P.emit()
    P.close()
    return nc


def kernel(x, positions, norm_mix, norm_mlp, final_norm, rec_w_in, rec_conv_w, rec_conv_b, lru_w_a, lru_b_a, lru_w_i, lru_b_i,
           lru_lambda, hgrn_lb_logits, hgrn_g_norm, rec_w_out, attn_w_qkv, attn_w_o, mlp_w1, mlp_w2):
    f32 = np.float32
    x = np.asarray(x, f32); positions = np.asarray(positions, np.int32)
    A = lambda a: np.ascontiguousarray(np.asarray(a, f32))
    B, S, _ = x.shape
    T = TPC
    PPB = S // T
    assert PPB == NRK and B * PPB == NCORES
    vecs = make_vecs(A(rec_conv_w)[0], A(rec_conv_b)[0], A(lru_b_a)[0], A(lru_b_i)[0], A(lru_lambda)[0], A(hgrn_lb_logits), A(hgrn_g_norm)[0])
    rc = make_rconst()
    cores = list(range(NCORES))
    shared = {"gain0": A(norm_mix[0]), "w_in": A(rec_w_in[0]), "vecs": vecs, "w_a": A(lru_w_a[0]), "w_i": A(lru_w_i[0]), "w_out": A(rec_w_out[0]),
              "gmlp0": A(norm_mlp[0]), "w1_0": A(mlp_w1[0]), "w2_0": A(mlp_w2[0]), "gain1": A(norm_mix[1]), "w_qkv": A(attn_w_qkv[0]), "rconst": rc,
              "w_o": A(attn_w_o[0]), "gmlp1": A(norm_mlp[1]), "w1_1": A(mlp_w1[1]), "w2_1": A(mlp_w2[1]), "fgain": A(final_norm)}
    maps = []
    for c in cores:
        b, p = divmod(c, PPB)
        m = dict(shared)
        m["x"] = np.ascontiguousarray(x[b, p * T:(p + 1) * T])
        m["xh"] = np.zeros((128, D), f32) if p == 0 else np.ascontiguousarray(x[b, p * T - 128:p * T])
        cm = np.zeros((128, NRK), f32); cm[:, :p] = 1.0
        m["cmask"] = cm
        m["pos"] = np.ascontiguousarray(positions[b, p * T:(p + 1) * T])
        m["flag"] = np.full((128, 1), 0.0 if p == 0 else 1.0, f32)
        maps.append(m)
    res = run_bass_kernel_spmd(build_fused(), maps, core_ids=cores).results
    out = np.zeros((B, S, D), f32)
    for c in cores:
        b, p = divmod(c, PPB)
        out[b, p * T:(p + 1) * T] = res[c]["out"]
    return out
```
